# Optimizing a Trainium2 kernel written in Bass

```python
import jax, jax.numpy as jnp
from jax import lax
import numpy as np

D_MODEL = 2048
BATCH = 4
SEQ = 4096
DEPTH = 1
DEC_BATCH = 16
DEC_SEQ = 32
PAST_LEN = 1024

CHUNK = 64
MEM_LEN = 256
RET_HEADS = 8
RET_HEAD_DIM = 128
RET_WIDTH = RET_HEADS * RET_HEAD_DIM
GMLP_GROUPS = 4
GMLP_GROUP_DIM = 128
GMLP_WIDTH = GMLP_GROUPS * GMLP_GROUP_DIM
GMLP_CHUNK = 128
XA_HEADS = 4
XA_HEAD_DIM = 128
XA_WIDTH = XA_HEADS * XA_HEAD_DIM
MIX_WIDTH = RET_WIDTH + GMLP_WIDTH + XA_WIDTH
IN_WIDTH = 4 * RET_WIDTH + 3 * GMLP_WIDTH + 2 * XA_WIDTH
ROPE_BASE = 10000.0
EPS = 1e-6

kernel_name = "hybrid_retention_gmlp_memxattn_stream_step"


def rmsnorm(x, g):
    x32 = x.astype(jnp.float32)
    y = x32 * lax.rsqrt(jnp.mean(x32 * x32, axis=-1, keepdims=True) + EPS)
    return (y * g.astype(jnp.float32)).astype(x.dtype)


def layernorm(x, g):
    x32 = x.astype(jnp.float32)
    mu = jnp.mean(x32, axis=-1, keepdims=True)
    var = jnp.mean(jnp.square(x32 - mu), axis=-1, keepdims=True)
    return ((x32 - mu) * lax.rsqrt(var + EPS) * g.astype(jnp.float32)).astype(x.dtype)


def rope(x, pos):
    dh = x.shape[-1]
    inv_freq = ROPE_BASE ** (-jnp.arange(0, dh, 2, dtype=jnp.float32) / dh)
    ang = pos.astype(jnp.float32)[:, None] * inv_freq[None, :]
    cos = jnp.cos(ang)[None, :, None, :]
    sin = jnp.sin(ang)[None, :, None, :]
    x32 = x.astype(jnp.float32)
    x1, x2 = x32[..., : dh // 2], x32[..., dh // 2:]
    return jnp.concatenate([x1 * cos - x2 * sin, x1 * sin + x2 * cos], axis=-1)


def retention_block(q, k, v, S0, log_gamma):
    L = q.shape[1]
    idx = jnp.arange(L, dtype=jnp.float32)
    diff = idx[:, None] - idx[None, :]
    causal = diff >= 0
    decay = jnp.where(causal[None], jnp.exp(log_gamma[:, None, None] * jnp.where(causal, diff, 0.0)[None]), 0.0)
    scores = jnp.einsum('bihd,bjhd->bhij', q, k) * decay[None]
    o_intra = jnp.einsum('bhij,bjhe->bihe', scores, v)
    q_decay = jnp.exp(log_gamma[None, :] * (idx[:, None] + 1.0))
    o_cross = jnp.einsum('bihd,bhde->bihe', q, S0) * q_decay[None, :, :, None]
    k_decay = jnp.exp(log_gamma[None, :] * (L - 1.0 - idx[:, None]))
    S_new = jnp.exp(log_gamma * L)[None, :, None, None] * S0 + jnp.einsum(
        'bjhd,bjhe->bhde', k * k_decay[None, :, :, None], v)
    return o_intra + o_cross, S_new


def retention_forward(q, k, v, S0, log_gamma):
    B, L, H, Dh = q.shape
    blk = CHUNK if L % CHUNK == 0 else L
    n = L // blk

    def to_blocks(t):
        return jnp.moveaxis(t.reshape(B, n, blk, H, t.shape[-1]), 1, 0)

    def step(S, qkv):
        qb, kb, vb = qkv
        o, S_next = retention_block(qb, kb, vb, S, log_gamma)
        return S_next, o

    S_final, o = lax.scan(step, S0.astype(jnp.float32), (to_blocks(q), to_blocks(k), to_blocks(v)))
    o = jnp.moveaxis(o, 0, 1).reshape(B, L, H, v.shape[-1])
    return o, S_final


def head_groupnorm(o, g):
    mu = jnp.mean(o, axis=-1, keepdims=True)
    var = jnp.mean(jnp.square(o - mu), axis=-1, keepdims=True)
    B, L, H, D = o.shape
    return ((o - mu) * lax.rsqrt(var + EPS)).reshape(B, L, H * D) * g.astype(jnp.float32)


def memory_kv(mem, g_mem, w_mem_kv):
    B, M, _ = mem.shape
    kv = rmsnorm(mem, g_mem) @ w_mem_kv
    mk, mv = jnp.split(kv, 2, axis=-1)
    return mk.reshape(B, M, XA_HEADS, XA_HEAD_DIM), mv.reshape(B, M, XA_HEADS, XA_HEAD_DIM)


def hybrid_layer(x, pos, S0, mem_k, mem_v, g_norm, w_in, g_ret, g_gmlp, w_s, b_s, w_out):
    B, L, _ = x.shape
    h = rmsnorm(x, g_norm)
    proj = h @ w_in
    cuts = [RET_WIDTH, 2 * RET_WIDTH, 3 * RET_WIDTH, 4 * RET_WIDTH,
            4 * RET_WIDTH + GMLP_WIDTH, 4 * RET_WIDTH + 2 * GMLP_WIDTH, 4 * RET_WIDTH + 3 * GMLP_WIDTH,
            4 * RET_WIDTH + 3 * GMLP_WIDTH + XA_WIDTH]
    rq, rk, rv, rg, gu, gv, gg, aq, ag = jnp.split(proj, cuts, axis=-1)

    rq = rope(rq.reshape(B, L, RET_HEADS, RET_HEAD_DIM), pos)
    rk = rope(rk.reshape(B, L, RET_HEADS, RET_HEAD_DIM), pos) * (RET_HEAD_DIM ** -0.5)
    rv = rv.reshape(B, L, RET_HEADS, RET_HEAD_DIM).astype(jnp.float32)
    log_gamma = jnp.log(1.0 - 2.0 ** (-5.0 - jnp.arange(RET_HEADS, dtype=jnp.float32)))
    ro, S_new = retention_forward(rq, rk, rv, S0, log_gamma)
    ro = head_groupnorm(ro, g_ret).astype(x.dtype) * jax.nn.silu(rg)

    gv_n = layernorm(gv, g_gmlp)
    blk = GMLP_CHUNK if L % GMLP_CHUNK == 0 else L
    n = L // blk
    vb = gv_n.reshape(B, n, blk, GMLP_GROUPS, GMLP_GROUP_DIM)
    w_mask = jnp.tril(w_s[:, :blk, :blk])
    s = jnp.einsum('gij,bnjgc->bnigc', w_mask, vb) + b_s[:, :blk].T[None, None, :, :, None]
    go = gu * s.reshape(B, L, GMLP_WIDTH) * jax.nn.silu(gg)

    aq = aq.reshape(B, L, XA_HEADS, XA_HEAD_DIM).astype(jnp.float32)
    sc = jnp.einsum('blhd,bmhd->bhlm', aq, mem_k.astype(jnp.float32)) * (XA_HEAD_DIM ** -0.5)
    p = jax.nn.softmax(sc, axis=-1)
    ao = jnp.einsum('bhlm,bmhd->blhd', p, mem_v.astype(jnp.float32)).reshape(B, L, XA_WIDTH)
    ao = ao.astype(x.dtype) * jax.nn.silu(ag)

    out = x + jnp.concatenate([ro, go, ao], axis=-1) @ w_out
    return out, S_new, gv_n


def setup_inputs(seed: int = 0) -> dict:
    key = jax.random.key(seed)
    ks = jax.random.split(key, 20)
    f32 = jnp.float32
    nrm = lambda k, shape, scale: jax.random.normal(k, shape, f32) * scale
    return {
        "x_prompt": nrm(ks[0], (BATCH, SEQ, D_MODEL), 1.0),
        "x_sample": nrm(ks[1], (DEC_BATCH, DEC_SEQ, D_MODEL), 1.0),
        "mem_prompt": nrm(ks[2], (BATCH, MEM_LEN, D_MODEL), 1.0),
        "state_ret": nrm(ks[3], (DEPTH, DEC_BATCH, RET_HEADS, RET_HEAD_DIM, RET_HEAD_DIM), 0.5),
        "cache_mem_k": nrm(ks[4], (DEPTH, DEC_BATCH, MEM_LEN, XA_HEADS, XA_HEAD_DIM), 1.0),
        "cache_mem_v": nrm(ks[5], (DEPTH, DEC_BATCH, MEM_LEN, XA_HEADS, XA_HEAD_DIM), 1.0),
        "g_norm": 1.0 + nrm(ks[6], (DEPTH, D_MODEL), 0.02),
        "w_in": nrm(ks[7], (DEPTH, D_MODEL, IN_WIDTH), D_MODEL ** -0.5),
        "g_ret": 1.0 + nrm(ks[8], (DEPTH, RET_WIDTH), 0.02),
        "g_gmlp": 1.0 + nrm(ks[9], (DEPTH, GMLP_WIDTH), 0.02),
        "w_s": nrm(ks[10], (DEPTH, GMLP_GROUPS, GMLP_CHUNK, GMLP_CHUNK), GMLP_CHUNK ** -0.5),
        "b_s": 1.0 + nrm(ks[11], (DEPTH, GMLP_GROUPS, GMLP_CHUNK), 0.02),
        "g_mem": 1.0 + nrm(ks[12], (DEPTH, D_MODEL), 0.02),
        "w_mem_kv": nrm(ks[13], (DEPTH, D_MODEL, 2 * XA_WIDTH), D_MODEL ** -0.5),
        "w_out": nrm(ks[14], (DEPTH, MIX_WIDTH, D_MODEL), MIX_WIDTH ** -0.5),
        "g_final": 1.0 + nrm(ks[15], (D_MODEL,), 0.02),
    }


def reference(x_prompt, x_sample, mem_prompt, state_ret, cache_mem_k, cache_mem_v,
              g_norm, w_in, g_ret, g_gmlp, w_s, b_s, g_mem, w_mem_kv, w_out, g_final):
    b_p, l_p, _ = x_prompt.shape
    l_s = x_sample.shape[1]
    pos_p = jnp.arange(l_p, dtype=jnp.int32)
    pos_s = PAST_LEN + jnp.arange(l_s, dtype=jnp.int32)
    hp, hs = x_prompt, x_sample
    sp_list, mk_list, mv_list, ss_list, vs_list = [], [], [], [], []
    for l in range(DEPTH):
        mk, mv = memory_kv(mem_prompt, g_mem[l], w_mem_kv[l])
        S0p = jnp.zeros((b_p, RET_HEADS, RET_HEAD_DIM, RET_HEAD_DIM), jnp.float32)
        hp, Sp, _ = hybrid_layer(hp, pos_p, S0p, mk, mv, g_norm[l], w_in[l], g_ret[l],
                                 g_gmlp[l], w_s[l], b_s[l], w_out[l])
        hs, Ss, vs = hybrid_layer(hs, pos_s, state_ret[l], cache_mem_k[l], cache_mem_v[l],
                                  g_norm[l], w_in[l], g_ret[l], g_gmlp[l], w_s[l], b_s[l], w_out[l])
        sp_list.append(Sp)
        mk_list.append(mk)
        mv_list.append(mv)
        ss_list.append(Ss)
        vs_list.append(vs)
    y_prompt = rmsnorm(hp, g_final)
    y_sample = rmsnorm(hs, g_final)
    return (y_prompt, y_sample, jnp.stack(sp_list), jnp.stack(mk_list), jnp.stack(mv_list),
            jnp.stack(ss_list), jnp.stack(vs_list))
```

```python
import numpy as np
import concourse.bass as bass
import concourse.mybir as mybir
from concourse.bass_utils import run_bass_kernel_spmd
from contextlib import ExitStack

F32 = mybir.dt.float32
BF16 = mybir.dt.bfloat16
ALU = mybir.AluOpType
AF = mybir.ActivationFunctionType
AX = mybir.AxisListType

import os
KSTOP = int(os.environ.get("KSTOP", "100000000"))
KCORES = int(os.environ.get("KCORES", "8"))
SEM_LIMIT = 30000
D = 2048
EPS = 1e-6
NU_IN = 15
NU_ALL = 19
U_RG0, U_RG1, U_GG, U_GU, U_GV, U_AG, U_AQ = 8, 9, 10, 11, 12, 13, 14


class Sem:
    def __init__(self, h):
        self.h = h
        self.n = 0


class Buf:
    __slots__ = ("name", "w", "r", "dsem", "excl")

    def __init__(self, name, excl=False):
        self.excl = excl
        self.name = name
        self.w = None
        self.r = []
        self.dsem = None


class Prog:
    ENGS = ("sync", "gpsimd", "scalar", "vector", "tensor")

    def __init__(self, nc, stack):
        self.nc = nc
        self.stack = stack
        self.q = {e: [] for e in self.ENGS}
        self.esem = {}
        self.known = {e: {} for e in self.ENGS}
        self.nsem = 0
        for e in self.ENGS:
            self.esem[e] = self.new_sem("e_" + e)

    def new_sem(self, name):
        self.nsem += 1
        h = self.stack.enter_context(self.nc.semaphore(f"{name}_{self.nsem}"))
        return Sem(h)

    def op(self, eng, fn, reads=(), writes=(), dma=False, sig=True):
        self.nops = getattr(self, "nops", 0) + 1
        if self.nops > KSTOP:
            return
        waits = {}
        kn = self.known[eng]
        pe_sem = self.esem["tensor"]

        def need(ev):
            if ev is None:
                return
            s, v = ev
            if eng == "tensor" and s is pe_sem:
                return
            if kn.get(s, 0) >= v:
                return
            if waits.get(s, 0) < v:
                waits[s] = v

        own = self.esem[eng]
        for b in reads:
            need(b.w)
            if b.excl:
                for ev in b.r:
                    if ev[0] is not own:
                        need(ev)
        for b in writes:
            need(b.w)
            for ev in b.r:
                need(ev)
        for s, v in waits.items():
            kn[s] = v
        if dma:
            tgt = writes[0] if writes else reads[0]
            if tgt.dsem is None or tgt.dsem.n + 16 > SEM_LIMIT:
                tgt.dsem = self.new_sem("d")
            dsem = tgt.dsem
            dsem.n += 16
            ev = (dsem, dsem.n)
            inc = (dsem, 16)
        else:
            s = self.esem[eng]
            if sig:
                s.n += 1
                ev = (s, s.n)
                inc = (s, 1)
                if s.n >= SEM_LIMIT:
                    self.esem[eng] = self.new_sem("e_" + eng)
            else:
                ev = (s, s.n + 1)
                inc = None
        for b in reads:
            b.r.append(ev)
        for b in writes:
            b.w = ev
            b.r = []
        self.q[eng].append((list(waits.items()), fn, inc))

    def wait_all(self, eng, bufs):
        waits = {}
        for b in bufs:
            for ev in ([b.w] if b.w else []) + list(b.r):
                s, v = ev
                if waits.get(s, 0) < v:
                    waits[s] = v
        self.q[eng].append((list(waits.items()), None, None))

    def replay(self):
        nc = self.nc
        with nc.Block() as block:
            def mk(ename):
                def body(e):
                    for waits, fn, inc in self.q[ename]:
                        for s, v in waits:
                            e.wait_ge(s.h, v)
                        if fn is None:
                            continue
                        ins = fn(e)
                        if inc is not None:
                            ins.then_inc(inc[0].h, inc[1])
                return body
            block.sync(mk("sync"))
            block.gpsimd(mk("gpsimd"))
            block.scalar(mk("scalar"))
            block.vector(mk("vector"))
            block.tensor(mk("tensor"))


def build_program(NB):
    nc = bass.Bass("TRN2", target_bir_lowering=False)
    NTOK = NB * 512
    NBLK = 2 * NB + 1

    def din(name, shape, dt=F32):
        return nc.dram_tensor(name, shape, dt, kind="ExternalInput").ap()

    def dout(name, shape):
        return nc.dram_tensor(name, shape, F32, kind="ExternalOutput").ap()

    x_main = din("x_main", [NTOK, D])
    x_pre = din("x_pre", [NTOK, D])
    x_s = din("x_s", [2, 32, D])
    mem = din("mem", [256, D])
    st_s = din("st_s", [2, 8, 128, 128])
    ck_s = din("ck_s", [2, 256, 512])
    cv_s = din("cv_s", [2, 256, 512])
    w_u = din("w_u", [NU_ALL, 128, 16, 512])
    wm_u = din("wm_u", [2, 128, 16, 512])
    gnorm_pk = din("gnorm_pk", [128, 16])
    gmem_pk = din("gmem_pk", [128, 16])
    gret_b = din("gret_b", [128, 1024])
    ggm_b = din("ggm_b", [128, 512])
    gfin_b = din("gfin_b", [128, D])
    wsT_in = din("wsT", [128, 4, 128])
    bs_in = din("bs_t", [128, 4])
    rope_in = din("rope", [NBLK, 128, 4, 256])
    cst_in = din("cst", [128, 320])
    dqT_in = din("dqT", [128, 8, 128])

    y_main = dout("y_main", [NTOK, D])
    y_s = dout("y_s", [2, 32, D])
    sp_out = dout("sp_out", [8, 128, 128])
    mk_out = dout("mk_out", [256, 512])
    mv_out = dout("mv_out", [256, 512])
    ss_out = dout("ss_out", [2, 8, 128, 128])
    gv_out = dout("gv_out", [2, 32, 512])

    scr = nc.dram_tensor("scr_w", [NU_ALL + 2, 128, 16, 512], BF16, kind="Internal").ap()

    with ExitStack() as st:
        P = Prog(nc, st)

        def sb(name, shape, dt):
            return st.enter_context(nc.sbuf_tensor(name, shape, dt))

        def ps(name, shape, dt):
            return st.enter_context(nc.psum_tensor(name, shape, dt))

        def B(name):
            return Buf(name)

        hT = sb("hT", [128, 16, 512], BF16); b_hT = [B(f"hT{t}") for t in range(4)]
        mixT = sb("mixT", [128, 16, 512], BF16); b_mixT = [B(f"mixT{t}") for t in range(4)]
        ub = [sb(f"ub{i}", [128, 16, 512], BF16) for i in range(2)]; b_ub = [B(f"ub{i}") for i in range(2)]
        x4 = sb("x4", [128, 4, D], F32); b_x4 = [B(f"x4_{t}") for t in range(4)]
        xb = sb("xb", [128, D], BF16); b_xb = B("xb")
        junk = sb("junk", [128, D], BF16); b_junk = B("junk")
        sgg = sb("sgg", [128, 4, 512], F32); b_sgg = [B(f"sgg{t}") for t in range(4)]
        gvs = [sb(f"gvs{i}", [128, 512], F32) for i in range(2)]; b_gvs = [B(f"gvs{i}") for i in range(2)]
        gvb = sb("gvb", [128, 512], BF16); b_gvb = B("gvb")
        sgr = sb("sgr", [128, 4, 1024], BF16); b_sgr = [B(f"sgr{t}") for t in range(4)]
        mtok = sb("mtok", [128, 4, 512], BF16); b_mtok = [B(f"mtok{t}") for t in range(4)]
        ropeB = sb("ropeB", [128, 4, 256], F32); b_rope = B("rope")
        gret = sb("gret", [128, 1024], F32)
        ggm = sb("ggm", [128, 512], F32)
        gfin = sb("gfin", [128, D], F32)
        gnp = sb("gnp", [128, 16], F32)
        gmp = sb("gmp", [128, 16], F32)
        b_cst = B("cst")
        cst = sb("cstt", [128, 320], F32)
        identb = sb("identb", [128, 128], BF16)
        wsT_f = sb("wsT_f", [128, 4, 128], F32)
        wsT = sb("wsTb", [128, 4, 128], BF16)
        bst = sb("bs_sb", [128, 4], F32)
        mkT = [sb(f"mkT{i}", [128, 4, 256], BF16) for i in range(3)]
        mvc = [sb(f"mvc{i}", [128, 2, 512], BF16) for i in range(3)]
        b_ctx = [B(f"ctx{i}") for i in range(3)]
        Sf = [sb(f"Sf{i}", [128, 8, 128], F32) for i in range(3)]
        Sb = [sb(f"Sb{i}", [128, 8, 128], BF16) for i in range(3)]
        b_Sf = [[B(f"Sf{i}_{h}") for h in range(8)] for i in range(3)]
        b_Sb = [[B(f"Sb{i}_{h}") for h in range(8)] for i in range(3)]
        r1 = sb("r1", [128, 2, 128], F32); b_r1 = B("r1")
        r2 = sb("r2", [128, 2, 128], F32); b_r2 = B("r2")
        qk = [sb(f"qk{i}", [128, 2, 128], BF16) for i in range(2)]; b_qk = [B(f"qk{i}") for i in range(2)]
        vd = [sb(f"vd{i}", [128, 128], BF16) for i in range(2)]; b_vd = [B(f"vd{i}") for i in range(2)]
        dqT = sb("dqT_sb", [128, 8, 128], F32)
        qkT = sb("qkT", [128, 2, 128], BF16); b_qkT = B("qkT")
        scm = sb("scm", [128, 128], BF16); b_scm = B("scm")
        sg4 = [sb(f"sg4_{i}", [128, 4, 128], F32) for i in range(2)]; b_sg4 = [[B(f"sg4_{i}_{t}") for t in range(4)] for i in range(2)]
        os4 = [sb(f"os4_{i}", [128, 4, 128], F32) for i in range(2)]; b_os4 = [[B(f"os4_{i}_{t}") for t in range(4)] for i in range(2)]
        bn6 = sb("bn6", [128, 6], F32); b_bn6 = B("bn6")
        mv4 = [sb(f"mv4_{i}", [128, 6, 4], F32) for i in range(2)]; b_mv4 = [B(f"mv4_{i}") for i in range(2)]
        mvg = sb("mvg", [128, 2], F32); b_mvg = B("mvg")
        rs4 = sb("rs4", [128, 4], F32); b_rs4 = B("rs4")
        sq4 = sb("sq4", [128, 4], F32); b_sq4 = B("sq4")
        ss4 = sb("ss4", [128, 4], F32); b_ss4 = B("ss4")
        eps_t = sb("eps_t", [128, 1], F32)
        aqT = [sb(f"aqT{i}", [128, 512], BF16) for i in range(2)]; b_aqT = [B(f"aqT{i}") for i in range(2)]
        pexp = sb("pexp", [128, 256], BF16); b_pexp = B("pexp")
        pT = sb("pT", [128, 2, 128], BF16); b_pT = B("pT")
        sm3 = sb("sm3", [128, 4], F32); b_sm3 = B("sm3")
        mkf = sb("mkf", [128, 2, 512], F32); b_mkf = B("mkf")
        mkb = sb("mkb", [128, 2, 512], BF16); b_mkb = B("mkb")
        PJ = [ps(f"PJ{i}", [128, 512], F32) for i in range(4)]; b_PJ = [Buf(f"PJ{i}", True) for i in range(4)]
        TR = [ps(f"TR{i}", [128, 1024], BF16) for i in range(2)]; b_TR = [Buf(f"TR{i}", True) for i in range(2)]
        SO = ps("SO", [128, 512], F32); b_SO = Buf("SO", True)
        SC = SO; b_SC = b_SO
        OP = SO; b_OP = b_SO
        STP = ps("STP", [128, 512], F32); b_STP = Buf("STP", True)

        ctr = {"pj": 0, "tr": 0, "ub": 0, "ev": 0}
        b_scr = [B(f"scr{u}") for u in range(NU_ALL + 2)]
        b_out = B("out")

        ident = identb
        mask = cst[:, 128:256]
        dq = cst[:, 256:264]
        dk = cst[:, 264:272]
        gam = {128: cst[:, 272:280], 32: cst[:, 280:288]}
        dkg = {128: cst[:, 288:296], 32: cst[:, 296:304]}
        ginv = {128: cst[:, 304:312], 32: cst[:, 312:320]}

        def dma(out, in_, reads, writes):
            P.op("sync", lambda e: e.dma_start(out=out, in_=in_), reads=reads, writes=writes, dma=True)

        def evac(out, in_, reads, writes, eng=None):
            if eng is None:
                ctr["ev"] += 1
                eng = "scalar" if ctr["ev"] % 2 else "vector"
            if eng == "scalar":
                P.op("scalar", lambda e: e.activation(out=out, in_=in_, func=AF.Copy), reads=reads, writes=writes)
            else:
                P.op(eng, lambda e: e.tensor_copy(out=out, in_=in_), reads=reads, writes=writes)

        dma(cst[:], cst_in, [], [b_cst])
        dma(dqT[:], dqT_in, [], [b_cst])
        dma(gret[:], gret_b, [], [b_cst])
        dma(ggm[:], ggm_b, [], [b_cst])
        dma(gfin[:], gfin_b, [], [b_cst])
        dma(gnp[:], gnorm_pk, [], [b_cst])
        dma(gmp[:], gmem_pk, [], [b_cst])
        dma(wsT_f[:], wsT_in, [], [b_cst])
        dma(bst[:], bs_in, [], [b_cst])
        b_c2 = B("c2")
        P.op("vector", lambda e: e.tensor_copy(out=identb[:], in_=cst[:, 0:128]), reads=[b_cst], writes=[b_c2])
        P.op("vector", lambda e: e.memset(eps_t[:], EPS), writes=[b_c2])
        for g in range(4):
            P.op("vector", lambda e, g=g: e.tensor_tensor(out=wsT[:, g, :], in0=wsT_f[:, g, :], in1=mask, op=ALU.mult),
                 reads=[b_cst], writes=[b_c2])
        CR = [b_cst, b_c2]

        cv_eng = ["vector", "scalar", "vector"]
        cvs = {"i": 0}

        def conv_quarter(u, qd, stage, b_stage, dst, b_dst, engs):
            src = w_u[u] if u < NU_ALL else wm_u[u - NU_ALL]
            gsc = gnp if u < NU_IN else (gmp if u >= NU_ALL else None)
            dma(stage, src[:, qd * 4:(qd + 1) * 4, :], [], [b_stage])
            for a in range(4):
                kc = qd * 4 + a
                eng = engs[cvs["i"] % len(engs)]; cvs["i"] += 1
                if gsc is None:
                    evac(dst[:, kc, :], stage[:, a, :], [b_stage], [b_dst], eng=eng)
                elif eng == "scalar":
                    P.op("scalar", lambda e, kc=kc, a=a: e.activation(
                        out=dst[:, kc, :], in_=stage[:, a, :], func=AF.Copy, scale=gsc[:, kc:kc + 1]),
                        reads=[b_stage] + CR, writes=[b_dst])
                else:
                    P.op(eng, lambda e, kc=kc, a=a: e.tensor_scalar(
                        out=dst[:, kc, :], in0=stage[:, a, :], scalar1=gsc[:, kc:kc + 1], scalar2=None, op0=ALU.mult),
                        reads=[b_stage] + CR, writes=[b_dst])

        b_wq = [B(f"wq{i}") for i in range(4)]
        w2_jobs = []
        for u in list(range(8)) + [NU_ALL, NU_ALL + 1] + list(range(8, NU_ALL)):
            for qd in range(4):
                w2_jobs.append((u, qd))

        jobs = list(w2_jobs)
        NJ = len(jobs)
        b_scrq = {}
        slot_sem = [P.new_sem(f"wst{i}") for i in range(4)]
        for j_, (u_, q_) in enumerate(jobs):
            b_scrq[(u_, q_)] = B(f"scr{u_}_{q_}")
            b_scrq[(u_, q_)].dsem = slot_sem[j_ % 4]

        def j_load(j):
            if not (0 <= j < NJ):
                return
            u, qd = jobs[j]
            slot = j % 4
            stage = x4[:, slot, :].rearrange("p (a b) -> p a b", b=512)
            src = w_u[u] if u < NU_ALL else wm_u[u - NU_ALL]
            dma(stage, src[:, qd * 4:(qd + 1) * 4, :], [], [b_x4[slot]])

        def j_conv(j):
            if not (0 <= j < NJ):
                return
            u, qd = jobs[j]
            slot = j % 4
            stage = x4[:, slot, :].rearrange("p (a b) -> p a b", b=512)
            dstq = mixT[:, slot * 4:(slot + 1) * 4, :]
            gsc = gnp if u < NU_IN else (gmp if u >= NU_ALL else None)
            for a_ in range(4):
                eng = cv_eng[cvs["i"] % 3]; cvs["i"] += 1
                if gsc is None:
                    evac(dstq[:, a_, :], stage[:, a_, :], [b_x4[slot]], [b_wq[slot]], eng=eng)
                elif eng == "scalar":
                    P.op("scalar", lambda e, a_=a_: e.activation(out=dstq[:, a_, :], in_=stage[:, a_, :], func=AF.Copy,
                                                                 scale=gsc[:, qd * 4 + a_:qd * 4 + a_ + 1]),
                         reads=[b_x4[slot]] + CR, writes=[b_wq[slot]])
                else:
                    P.op(eng, lambda e, a_=a_: e.tensor_scalar(out=dstq[:, a_, :], in0=stage[:, a_, :],
                                                                scalar1=gsc[:, qd * 4 + a_:qd * 4 + a_ + 1], scalar2=None, op0=ALU.mult),
                         reads=[b_x4[slot]] + CR, writes=[b_wq[slot]])

        def j_store(j):
            if not (0 <= j < NJ):
                return
            u, qd = jobs[j]
            slot = j % 4
            dstq = mixT[:, slot * 4:(slot + 1) * 4, :]
            dma(scr[u][:, qd * 4:(qd + 1) * 4, :], dstq, [b_wq[slot]], [b_scrq[(u, qd)]])

        wj = {"nl": 0, "nc": 0, "ns": 0}

        def w2_step(n=1):
            for _ in range(n):
                while wj["nl"] < NJ and wj["nl"] < wj["nc"] + 3:
                    j_load(wj["nl"]); wj["nl"] += 1
                if wj["ns"] < wj["nc"]:
                    j_store(wj["ns"]); wj["ns"] += 1
                if wj["nc"] < wj["nl"]:
                    j_conv(wj["nc"]); wj["nc"] += 1

        def w2_drain():
            while wj["nc"] < wj["nl"]:
                j_conv(wj["nc"]); wj["nc"] += 1
            while wj["ns"] < wj["nc"]:
                j_store(wj["ns"]); wj["ns"] += 1

        def load_unit(u, ncols=512):
            i = ctr["ub"] % 2; ctr["ub"] += 1
            rd = [b_scrq[(u, q_)] for q_ in range(4)]
            assert all(b_.w is not None for b_ in rd), ("unit loaded before converted", u)
            if ncols == 512:
                dma(ub[i][:], scr[u], rd, [b_ub[i]])
            elif ncols == 256:
                dma(ub[i][:, :, 128:384], scr[u][:, :, 128:384], rd, [b_ub[i]])
            else:
                dma(ub[i][:, :, 0:ncols], scr[u][:, :, 0:ncols], rd, [b_ub[i]])
            return i

        def next_pj():
            i = ctr["pj"] % 4; ctr["pj"] += 1
            return i

        def next_tr():
            i = ctr["tr"] % 2; ctr["tr"] += 1
            return i

        def rstd_batch(src, dst, n, T, reads_b, writes_b, scale=None):
            if scale is None:
                P.op("scalar", lambda e: e.activation(out=sq4[:T, :n], in_=src, func=AF.Sqrt, bias=eps_t[:T, 0:1], scale=1.0),
                     reads=reads_b + CR, writes=[b_sq4])
            else:
                P.op("scalar", lambda e: e.activation(out=sq4[:T, :n], in_=src, func=AF.Sqrt, bias=eps_t[:T, 0:1], scale=scale),
                     reads=reads_b + CR, writes=[b_sq4])
            P.op("vector", lambda e: e.reciprocal(out=dst, in_=sq4[:T, :n]), reads=[b_sq4], writes=writes_b)

        def front(tiles, only_x=False):
            nt = len(tiles)
            if only_x:
                for t, (src, T) in enumerate(tiles):
                    dma(x4[:T, t, :], src, [], [b_x4[t]])
                return
            for t, (src, T) in enumerate(tiles):
                dma(x4[:T, t, :], src, [], [b_x4[t]])
                P.op("scalar", lambda e, t=t, T=T: e.activation(out=junk[:T, :], in_=x4[:T, t, :], func=AF.Square,
                                                               accum_out=ss4[:T, t:t + 1]),
                     reads=[b_x4[t]], writes=[b_junk, b_ss4])
            T0 = tiles[0][1]
            rstd_batch(ss4[:T0, :nt], rs4[:T0, :nt], nt, T0, [b_ss4], [b_rs4], scale=1.0 / D)
            for t, (src, T) in enumerate(tiles):
                P.op("vector", lambda e, t=t, T=T: e.tensor_scalar(out=xb[:T, :], in0=x4[:T, t, :], scalar1=rs4[:T, t:t + 1],
                                                                   scalar2=None, op0=ALU.mult),
                     reads=[b_x4[t], b_rs4], writes=[b_xb])
                for grp in range(2):
                    ti = next_tr()
                    for j in range(8):
                        kc = grp * 8 + j
                        P.op("tensor", lambda e, ti=ti, j=j, kc=kc, T=T: e.transpose(
                            out=TR[ti][:, j * 128:j * 128 + T], in_=xb[:T, kc * 128:(kc + 1) * 128], identity=ident[:T, :T]),
                            reads=[b_xb] + CR, writes=[b_TR[ti]], sig=(j == 7))
                    evac(hT[:, grp * 8:(grp + 1) * 8, t * 128:t * 128 + T],
                         TR[ti][:].rearrange("p (a b) -> p a b", b=128)[:, :, :T], [b_TR[ti]], [b_hT[t]])

        ssE = sb("ssE", [128, 4], F32); b_ssE = B("ssE")
        rsE = sb("rsE", [128, 4], F32); b_rsE = B("rsE")
        sgg_flat = sgg[:].rearrange("p a b -> p (a b)")

        def efront_load(t, src, T):
            dma(sgg_flat[:T, :], src, [], b_sgg)
            P.op("scalar", lambda e: e.activation(out=junk[:T, :], in_=sgg_flat[:T, :], func=AF.Square, accum_out=ssE[:T, t:t + 1]),
                 reads=b_sgg, writes=[b_junk, b_ssE])
            P.op("scalar", lambda e: e.activation(out=sq4[:T, 0:1], in_=ssE[:T, t:t + 1], func=AF.Sqrt, bias=eps_t[:T, 0:1], scale=1.0 / D),
                 reads=[b_ssE] + CR, writes=[b_sq4])
            P.op("vector", lambda e: e.reciprocal(out=rsE[:T, t:t + 1], in_=sq4[:T, 0:1]), reads=[b_sq4], writes=[b_rsE])
            P.op("vector", lambda e: e.tensor_scalar(out=xb[:T, :], in0=sgg_flat[:T, :], scalar1=rsE[:T, t:t + 1], scalar2=None, op0=ALU.mult),
                 reads=b_sgg + [b_rsE], writes=[b_xb])

        def efront_tr(t, T):
            for grp in range(2):
                ti = next_tr()
                for j in range(8):
                    kc = grp * 8 + j
                    P.op("tensor", lambda e, ti=ti, j=j, kc=kc: e.transpose(
                        out=TR[ti][:, j * 128:j * 128 + T], in_=xb[:T, kc * 128:(kc + 1) * 128], identity=ident[:T, :T]),
                        reads=[b_xb] + CR, writes=[b_TR[ti]], sig=(j == 7))
                evac(hT[:, grp * 8:(grp + 1) * 8, t * 128:t * 128 + T],
                     TR[ti][:].rearrange("p (a b) -> p a b", b=128)[:, :, :T], [b_TR[ti]], [b_hT[t]])

        def proj(t, T, i, ncols, c0=0):
            pj = next_pj()
            for kc in range(16):
                P.op("tensor", lambda e, kc=kc, pj=pj, t=t, T=T, i=i: e.matmul(
                    PJ[pj][:T, :ncols], lhsT=hT[:, kc, t * 128:t * 128 + T], rhs=ub[i][:, kc, c0:c0 + ncols],
                    start=(kc == 0), stop=(kc == 15)),
                    reads=[b_hT[t], b_ub[i]], writes=[b_PJ[pj]], sig=(kc == 15))
            return pj

        def rope2(pj, ns, T, t, p):
            X = PJ[pj][:T, 0:ns * 128].rearrange("p (s d) -> p s d", d=128)
            A = ropeB[:T, t, 0:128].unsqueeze(1).broadcast_to([T, ns, 128])
            B1 = ropeB[:T, t, 128:192].unsqueeze(1).broadcast_to([T, ns, 64])
            B2 = ropeB[:T, t, 192:256].unsqueeze(1).broadcast_to([T, ns, 64])
            return [
                lambda: P.op("vector", lambda e: e.tensor_tensor(out=r1[:T, 0:ns, :], in0=X, in1=A, op=ALU.mult),
                             reads=[b_PJ[pj], b_rope], writes=[b_r1]),
                lambda: P.op("vector", lambda e: e.tensor_tensor(out=r2[:T, 0:ns, 0:64], in0=X[:, :, 64:128], in1=B1, op=ALU.mult),
                             reads=[b_PJ[pj], b_rope], writes=[b_r2]),
                lambda: P.op("vector", lambda e: e.tensor_tensor(out=r2[:T, 0:ns, 64:128], in0=X[:, :, 0:64], in1=B2, op=ALU.mult),
                             reads=[b_PJ[pj], b_rope], writes=[b_r2]),
                lambda: P.op("vector", lambda e: e.tensor_tensor(out=qk[p][:T, 2 - ns:2, :], in0=r1[:T, 0:ns, :], in1=r2[:T, 0:ns, :], op=ALU.add),
                             reads=[b_r1, b_r2], writes=[b_qk[p]]),
            ]

        ust = {"loaded": {}, "seq": [], "stepno": 0, "drip": None, "pre": False, "deferred": [], "wrate": 1}

        def get_unit(k):
            if k >= len(ust["seq"]):
                return None
            if k not in ust["loaded"]:
                u = ust["seq"][k]
                ust["loaded"][k] = load_unit(u, (256 if ust["pre"] else 384) if u < 8 else 512)
            return ust["loaded"][k]

        def drip(n):
            for d in (ust["drip"] or []):
                if "mm" not in d:
                    d["mm"] = d["prep"]()
                while n > 0 and d["mm"]:
                    d["mm"].pop(0)()
                    n -= 1
                if n == 0:
                    return

        def evdrip(n):
            q_ = ust.get("evq")
            while q_ and n > 0:
                q_.pop(0)()
                n -= 1

        def proj_mm(t, T, i, ncols, pj, c0=0):
            return [(lambda kc=kc: P.op("tensor", lambda e: e.matmul(
                PJ[pj][:T, :ncols], lhsT=hT[:, kc, t * 128:t * 128 + T], rhs=ub[i][:, kc, c0:c0 + ncols],
                start=(kc == 0), stop=(kc == 15)),
                reads=[b_hT[t], b_ub[i]], writes=[b_PJ[pj]], sig=(kc == 15))) for kc in range(16)]

        def ret_steps(k, h, tiles, pre):
            steps = []
            nt = len(tiles)
            up = h % 2
            for t, (T, sid) in enumerate(tiles):
                st_ = {}

                def prep(t=t, T=T, st_=st_):
                    i = get_unit(k)
                    if t == 0:
                        get_unit(k + 1)
                    st_["p"] = ust["stepno"] % 2; ust["stepno"] += 1
                    st_["pj"] = next_pj()
                    return proj_mm(t, T, i, 256 if pre else 384, st_["pj"], c0=(128 if pre else 0))

                def evl(t=t, T=T, st_=st_):
                    p, pj = st_["p"], st_["pj"]
                    vc = 128 if pre else 256
                    return [lambda: P.op("scalar", lambda e: e.activation(out=vd[p][:T, :], in_=PJ[pj][:T, vc:vc + 128], func=AF.Copy,
                                                                          scale=dkg[T][:T, h:h + 1]),
                                         reads=[b_PJ[pj]] + CR, writes=[b_vd[p]])] + rope2(pj, 1 if pre else 2, T, t, p)

                def ev(evl=evl):
                    for f_ in evl():
                        f_()

                def tail():
                    T0 = tiles[0][0]
                    M = mv4[up]
                    P.op("vector", lambda e: e.tensor_scalar(out=M[:T0, 2, :nt], in0=M[:T0, 0, :nt], scalar1=1.0 / 128, scalar2=None, op0=ALU.mult),
                         reads=[b_mv4[up]], writes=[b_mv4[up]])
                    P.op("vector", lambda e: e.tensor_tensor(out=M[:T0, 3, :nt], in0=M[:T0, 2, :nt], in1=M[:T0, 2, :nt], op=ALU.mult),
                         reads=[b_mv4[up]], writes=[b_mv4[up]])
                    P.op("vector", lambda e: e.scalar_tensor_tensor(out=M[:T0, 3, :nt], in0=M[:T0, 1, :nt], scalar=1.0 / 128, in1=M[:T0, 3, :nt],
                                                                    op0=ALU.mult, op1=ALU.subtract),
                         reads=[b_mv4[up]], writes=[b_mv4[up]])
                    rstd_batch(M[:T0, 3, :nt], M[:T0, 4, :nt], nt, T0, [b_mv4[up]], [b_mv4[up]])
                    P.op("vector", lambda e: e.scalar_tensor_tensor(out=M[:T0, 5, :nt], in0=M[:T0, 2, :nt], scalar=-1.0, in1=M[:T0, 4, :nt],
                                                                    op0=ALU.mult, op1=ALU.mult),
                         reads=[b_mv4[up]], writes=[b_mv4[up]])
                    for t2, (T2, sid2) in enumerate(tiles):
                        P.op("scalar", lambda e, t2=t2, T2=T2: e.activation(out=os4[up][:T2, t2, :], in_=os4[up][:T2, t2, :], func=AF.Identity,
                                                                            bias=M[:T2, 5, t2:t2 + 1], scale=M[:T2, 4, t2:t2 + 1]),
                             reads=[b_os4[up][t2], b_mv4[up]], writes=[b_os4[up][t2]])
                    ust["deferred"].append([2, tail2])

                def tail_gs():
                    for t2, (T2, sid2) in enumerate(tiles):
                        P.op("gpsimd", lambda e, t2=t2, T2=T2: e.tensor_tensor(out=sg4[up][:T2, t2, :], in0=sgr[:T2, t2, h * 128:(h + 1) * 128],
                                                                               in1=gret[:T2, h * 128:(h + 1) * 128], op=ALU.mult),
                             reads=[b_sgr[t2]] + CR, writes=[b_sg4[up][t2]])

                def tail2():
                    for t2, (T2, sid2) in enumerate(tiles):
                        P.op("vector", lambda e, t2=t2, T2=T2: e.tensor_tensor(out=mtok[:T2, t2, 0:128], in0=os4[up][:T2, t2, :], in1=sg4[up][:T2, t2, :],
                                                                               op=ALU.mult),
                             reads=[b_os4[up][t2], b_sg4[up][t2]], writes=[b_mtok[t2]])
                    for t2, (T2, sid2) in enumerate(tiles):
                        ti = 1
                        P.op("tensor", lambda e, ti=ti, t2=t2, T2=T2: e.transpose(out=TR[ti][:, t2 * 128:t2 * 128 + T2], in_=mtok[:T2, t2, 0:128], identity=ident[:T2, :T2]),
                             reads=[b_mtok[t2]] + CR, writes=[b_TR[ti]])
                        evac(mixT[:, h, t2 * 128:t2 * 128 + T2], TR[ti][:, t2 * 128:t2 * 128 + T2], [b_TR[ti]], [b_mixT[t2]])

                def Bf(t=t, T=T, sid=sid, st_=st_):
                    p = st_["p"]
                    if not pre and t == 0:
                        tail_gs()
                    if not pre:
                        ti = 0
                        for s_ in range(2):
                            P.op("tensor", lambda e, s_=s_: e.transpose(out=TR[ti][:, s_ * 128:s_ * 128 + T], in_=qk[p][:T, s_, :],
                                                                         identity=ident[:T, :T]),
                                 reads=[b_qk[p]] + CR, writes=[b_TR[ti]], sig=(s_ == 1))
                        evac(qkT[:, 1, :T], TR[ti][:, 128:128 + T], [b_TR[ti]], [b_qkT], eng="scalar")
                        P.op("vector", lambda e: e.tensor_tensor(out=qkT[:, 0, :T], in0=TR[ti][:, 0:T], in1=dqT[:, h, :T], op=ALU.mult),
                             reads=[b_TR[ti]] + CR, writes=[b_qkT])
                        evdrip(2)
                    P.op("tensor", lambda e: e.matmul(STP[:, 0:128], lhsT=qk[p][:T, 1, :], rhs=vd[p][:T, :], start=True, stop=True),
                         reads=[b_qk[p], b_vd[p]], writes=[b_STP])
                    if not pre:
                        drip(8)
                        P.op("tensor", lambda e: e.matmul(SC[:T, :T], lhsT=qkT[:, 1, :T], rhs=qkT[:, 0, :T], start=True, stop=True),
                             reads=[b_qkT], writes=[b_SC])
                        P.op("vector", lambda e: e.scalar_tensor_tensor(out=scm[:T, :T], in0=SC[:T, :T], scalar=ginv[T][:T, h:h + 1], in1=mask[:T, :T],
                                                                        op0=ALU.mult, op1=ALU.mult),
                             reads=[b_SC] + CR, writes=[b_scm])
                        evdrip(2)
                        drip(8)
                        P.op("tensor", lambda e: e.matmul(OP[:T, 256:384], lhsT=scm[:T, :T], rhs=vd[p][:T, :], start=True, stop=False),
                             reads=[b_scm, b_vd[p]], writes=[b_OP], sig=False)
                        P.op("tensor", lambda e: e.matmul(OP[:T, 256:384], lhsT=qkT[:, 0, :T], rhs=Sb[sid][:, h, :], start=False, stop=True),
                             reads=[b_qkT, b_Sb[sid][h]], writes=[b_OP])
                        P.op("scalar", lambda e: e.activation(out=os4[up][:T, t, :], in_=OP[:T, 256:384], func=AF.Copy,
                                                              accum_out=mv4[up][:T, 0, t:t + 1]),
                             reads=[b_OP], writes=[b_os4[up][t], b_mv4[up]])
                        P.op("scalar", lambda e: e.activation(out=junk[:T, 0:128], in_=OP[:T, 256:384], func=AF.Square,
                                                              accum_out=mv4[up][:T, 1, t:t + 1]),
                             reads=[b_OP], writes=[b_junk, b_mv4[up]])
                    g_ = gam[T][:, h:h + 1]
                    P.op("vector", lambda e: e.scalar_tensor_tensor(out=Sf[sid][:, h, :], in0=Sf[sid][:, h, :], scalar=g_, in1=STP[:, 0:128],
                                                                    op0=ALU.mult, op1=ALU.add),
                         reads=[b_STP, b_Sf[sid][h]] + CR, writes=[b_Sf[sid][h]])
                    if not pre or t == nt - 1:
                        P.op("gpsimd", lambda e: e.tensor_copy(out=Sb[sid][:, h, :], in_=Sf[sid][:, h, :]),
                             reads=[b_Sf[sid][h]], writes=[b_Sb[sid][h]])
                    if pre or t != nt - 1:
                        return
                    ust["deferred"].append([2, tail])

                steps.append(dict(prep=prep, evac=ev, evl=evl, B=Bf, flush=False))
            return steps

        def rg_steps(k0, tiles):
            steps = []
            for half in range(2):
                for t, (T, sid) in enumerate(tiles):
                    st_ = {}

                    def prep(half=half, t=t, T=T, st_=st_):
                        i = get_unit(k0 + half)
                        if t == 0:
                            get_unit(k0 + half + 1)
                        st_["pj"] = next_pj()
                        return proj_mm(t, T, i, 512, st_["pj"])

                    def ev(half=half, t=t, T=T, st_=st_):
                        pj = st_["pj"]
                        P.op("scalar", lambda e: e.activation(out=sgr[:T, t, half * 512:(half + 1) * 512], in_=PJ[pj][:T, :], func=AF.Silu),
                             reads=[b_PJ[pj]], writes=[b_sgr[t]])
                    steps.append(dict(prep=prep, evac=ev, B=None, flush=False))
            return steps

        def mix_transposes(t, T, kc0):
            ti = next_tr()
            for g in range(4):
                P.op("tensor", lambda e, ti=ti, g=g, t=t, T=T: e.transpose(out=TR[ti][:, g * 128:g * 128 + T],
                                                                           in_=mtok[:T, t, g * 128:(g + 1) * 128], identity=ident[:T, :T]),
                     reads=[b_mtok[t]] + CR, writes=[b_TR[ti]], sig=(g == 3))
            evac(mixT[:, kc0:kc0 + 4, t * 128:t * 128 + T],
                 TR[ti][:, 0:512].rearrange("p (a b) -> p a b", b=128)[:, :, :T], [b_TR[ti]], [b_mixT[t]])

        def gmlp_steps(k0, tiles, gv_dst):
            steps = []
            for t, (T, sid) in enumerate(tiles):
                st_ = {}

                def prep(t=t, T=T, st_=st_):
                    i = get_unit(k0)
                    if t == 0:
                        get_unit(k0 + 1)
                    st_["pj"] = next_pj()
                    return proj_mm(t, T, i, 512, st_["pj"])

                def ev(t=t, T=T, st_=st_):
                    pj = st_["pj"]
                    P.op("scalar", lambda e: e.activation(out=sgg[:T, t, :], in_=PJ[pj][:T, :], func=AF.Silu),
                         reads=[b_PJ[pj]], writes=[b_sgg[t]])
                steps.append(dict(prep=prep, evac=ev, B=None, flush=False))
            for t, (T, sid) in enumerate(tiles):
                st_ = {}

                def prep(t=t, T=T, st_=st_):
                    i = get_unit(k0 + 1)
                    if t == 0:
                        get_unit(k0 + 2)
                    st_["pj"] = next_pj()
                    return proj_mm(t, T, i, 512, st_["pj"])

                def ev(t=t, T=T, st_=st_):
                    pj = st_["pj"]
                    P.op("vector", lambda e: e.tensor_tensor(out=sgg[:T, t, :], in0=PJ[pj][:T, :], in1=sgg[:T, t, :], op=ALU.mult),
                         reads=[b_PJ[pj], b_sgg[t]], writes=[b_sgg[t]])
                steps.append(dict(prep=prep, evac=ev, B=None, flush=False))
            for t, (T, sid) in enumerate(tiles):
                st_ = {}

                def prep(t=t, T=T, st_=st_):
                    i = get_unit(k0 + 2)
                    if t == 0:
                        get_unit(k0 + 3)
                    st_["p"] = ust["stepno"] % 2; ust["stepno"] += 1
                    st_["pj"] = next_pj()
                    return proj_mm(t, T, i, 512, st_["pj"])

                def ev(t=t, T=T, st_=st_):
                    p, pj = st_["p"], st_["pj"]
                    evac(gvs[p][:T, :], PJ[pj][:T, :], [b_PJ[pj]], [b_gvs[p]], eng="scalar")

                def Bf(t=t, T=T, st_=st_):
                    p = st_["p"]
                    G = gvs[p]
                    P.op("vector", lambda e: e.bn_stats(out=bn6[:T, :], in_=G[:T, :]), reads=[b_gvs[p]], writes=[b_bn6])
                    P.op("vector", lambda e: e.bn_aggr(out=mvg[:T, :], in_=bn6[:T, :]), reads=[b_bn6], writes=[b_mvg])
                    rstd_batch(mvg[:T, 1:2], rs4[:T, 0:1], 1, T, [b_mvg], [b_rs4])
                    P.op("vector", lambda e: e.tensor_scalar(out=G[:T, :], in0=G[:T, :], scalar1=mvg[:T, 0:1], scalar2=rs4[:T, 0:1],
                                                             op0=ALU.subtract, op1=ALU.mult),
                         reads=[b_gvs[p], b_mvg, b_rs4], writes=[b_gvs[p]])
                    P.op("vector", lambda e: e.tensor_tensor(out=G[:T, :], in0=G[:T, :], in1=ggm[:T, :], op=ALU.mult),
                         reads=[b_gvs[p]] + CR, writes=[b_gvs[p]])
                    if gv_dst is not None:
                        dma(gv_dst[t], G[:T, :], [b_gvs[p]], [b_out])
                    evac(gvb[:T, :], G[:T, :], [b_gvs[p]], [b_gvb], eng="scalar")
                    drip(8)
                    for g in range(4):
                        P.op("tensor", lambda e, g=g: e.matmul(OP[:T, g * 128:(g + 1) * 128], lhsT=wsT[:T, g, :T],
                                                               rhs=gvb[:T, g * 128:(g + 1) * 128], start=True, stop=True),
                             reads=[b_gvb] + CR, writes=[b_OP], sig=(g == 3))
                    for g in range(4):
                        P.op("vector", lambda e, g=g: e.scalar_tensor_tensor(
                            out=mtok[:T, t, g * 128:(g + 1) * 128], in0=OP[:T, g * 128:(g + 1) * 128], scalar=bst[:T, g:g + 1],
                            in1=sgg[:T, t, g * 128:(g + 1) * 128], op0=ALU.add, op1=ALU.mult),
                            reads=[b_OP, b_sgg[t]] + CR, writes=[b_mtok[t]])
                    drip(8)
                    mix_transposes(t, T, 8)
                steps.append(dict(prep=prep, evac=ev, B=Bf, flush=False))
            return steps

        def xattn_steps(k0, tiles, ctxs):
            steps = []
            nt = len(tiles)
            ncol = nt * 128
            for t, (T, sid) in enumerate(tiles):
                st_ = {}

                def prep(t=t, T=T, st_=st_):
                    i = get_unit(k0)
                    if t == 0:
                        get_unit(k0 + 1)
                    st_["pj"] = next_pj()
                    return proj_mm(t, T, i, 512, st_["pj"])

                def ev(t=t, T=T, st_=st_):
                    pj = st_["pj"]
                    P.op("scalar", lambda e: e.activation(out=sgg[:T, t, :], in_=PJ[pj][:T, :], func=AF.Silu),
                         reads=[b_PJ[pj]], writes=[b_sgg[t]])
                steps.append(dict(prep=prep, evac=ev, B=None, flush=False))
            for hh in range(4):
                st_ = {}

                def prep(hh=hh, st_=st_):
                    i = get_unit(k0 + 1)
                    if hh == 0:
                        get_unit(k0 + 2)
                    st_["p"] = ust["stepno"] % 2; ust["stepno"] += 1
                    pj = st_["pj"] = next_pj()
                    return [(lambda kc=kc: P.op("tensor", lambda e: e.matmul(
                        PJ[pj][:, :ncol], lhsT=ub[i][:, kc, hh * 128:(hh + 1) * 128], rhs=hT[:, kc, 0:ncol],
                        start=(kc == 0), stop=(kc == 15)),
                        reads=b_hT[:nt] + [b_ub[i]], writes=[b_PJ[pj]], sig=(kc == 15))) for kc in range(16)]

                def ev(hh=hh, st_=st_):
                    p, pj = st_["p"], st_["pj"]
                    P.op("scalar", lambda e: e.activation(out=aqT[p][:, :ncol], in_=PJ[pj][:, :ncol], func=AF.Copy, scale=128.0 ** -0.5),
                         reads=[b_PJ[pj]], writes=[b_aqT[p]])

                def Bf(hh=hh, st_=st_):
                    p = st_["p"]
                    for t, (T, sid) in enumerate(tiles):
                        cx = ctxs[t]
                        P.op("tensor", lambda e, t=t, T=T, cx=cx: e.matmul(SC[:T, 0:256], lhsT=aqT[p][:, t * 128:t * 128 + T],
                                                                            rhs=mkT[cx][:, hh, :], start=True, stop=True),
                             reads=[b_aqT[p], b_ctx[cx]], writes=[b_SC])
                        P.op("vector", lambda e, T=T: e.reduce_max(out=sm3[:T, 0:1], in_=SC[:T, 0:256], axis=AX.X), reads=[b_SC], writes=[b_sm3])
                        P.op("vector", lambda e, T=T: e.tensor_scalar(out=sm3[:T, 1:2], in0=sm3[:T, 0:1], scalar1=-1.0, scalar2=None, op0=ALU.mult),
                             reads=[b_sm3], writes=[b_sm3])
                        P.op("scalar", lambda e, T=T: e.activation(out=pexp[:T, :], in_=SC[:T, 0:256], func=AF.Exp, bias=sm3[:T, 1:2], scale=1.0,
                                                                   accum_out=sm3[:T, 2:3]),
                             reads=[b_SC, b_sm3], writes=[b_pexp, b_sm3])
                        drip(2)
                        ti = next_tr()
                        for c in range(2):
                            P.op("tensor", lambda e, ti=ti, c=c, T=T: e.transpose(out=TR[ti][:, c * 128:c * 128 + T], in_=pexp[:T, c * 128:(c + 1) * 128],
                                                                                    identity=ident[:T, :T]),
                                 reads=[b_pexp] + CR, writes=[b_TR[ti]], sig=(c == 1))
                        evac(pT[:, :, :T], TR[ti][:, 0:256].rearrange("p (a b) -> p a b", b=128)[:, :, :T], [b_TR[ti]], [b_pT])
                        drip(2)
                        for c in range(2):
                            P.op("tensor", lambda e, c=c, T=T, cx=cx: e.matmul(OP[:T, 256:384], lhsT=pT[:, c, :T],
                                                                               rhs=mvc[cx][:, c, hh * 128:(hh + 1) * 128],
                                                                               start=(c == 0), stop=(c == 1)),
                                 reads=[b_pT, b_ctx[cx]], writes=[b_OP], sig=(c == 1))
                        P.op("vector", lambda e, T=T: e.reciprocal(out=sm3[:T, 3:4], in_=sm3[:T, 2:3]), reads=[b_sm3], writes=[b_sm3])
                        P.op("vector", lambda e, t=t, T=T: e.scalar_tensor_tensor(
                            out=mtok[:T, t, hh * 128:(hh + 1) * 128], in0=OP[:T, 256:384], scalar=sm3[:T, 3:4],
                            in1=sgg[:T, t, hh * 128:(hh + 1) * 128], op0=ALU.mult, op1=ALU.mult),
                            reads=[b_OP, b_sm3, b_sgg[t]], writes=[b_mtok[t]])
                    if hh == 3:
                        for t, (T, sid) in enumerate(tiles):
                            mix_transposes(t, T, 12)
                steps.append(dict(prep=prep, evac=ev, B=Bf, flush=False))
            return steps

        def out_steps(k0, tiles, dsts, nxt):
            steps = []
            nt = len(tiles)
            ef = []
            if nxt is not None:
                for t2, (src2, T2) in enumerate(nxt):
                    ef.append(lambda t2=t2, src2=src2, T2=T2: efront_load(t2, src2, T2))
                    ef.append(lambda t2=t2, T2=T2: efront_tr(t2, T2))
            for n in range(4):
                for t, (T, sid) in enumerate(tiles):
                    st_ = {}

                    def prep(n=n, t=t, T=T, st_=st_):
                        i = get_unit(k0 + n)
                        if t == 0:
                            get_unit(k0 + n + 1)
                        pj = st_["pj"] = next_pj()
                        return [(lambda kc=kc: P.op("tensor", lambda e: e.matmul(
                            PJ[pj][:T, :], lhsT=mixT[:, kc, t * 128:t * 128 + T], rhs=ub[i][:, kc, :],
                            start=(kc == 0), stop=(kc == 15)),
                            reads=[b_mixT[t], b_ub[i]], writes=[b_PJ[pj]], sig=(kc == 15))) for kc in range(16)]

                    def ev(n=n, t=t, T=T, st_=st_):
                        pj = st_["pj"]
                        P.op("vector", lambda e: e.tensor_tensor(out=x4[:T, t, n * 512:(n + 1) * 512], in0=PJ[pj][:T, :],
                                                                 in1=x4[:T, t, n * 512:(n + 1) * 512], op=ALU.add),
                             reads=[b_PJ[pj], b_x4[t]], writes=[b_x4[t]])
                        if ef:
                            ef.pop(0)()
                    steps.append(dict(prep=prep, evac=ev, B=None, flush=(n == 0 and t == 0)))

            def tail():
                while ef:
                    ef.pop(0)()
                if nxt is not None:
                    dma(ropeB[:], ust["next_rope"], [], [b_rope])
                for t, (T, sid) in enumerate(tiles):
                    P.op("scalar", lambda e, t=t, T=T: e.activation(out=junk[:T, :], in_=x4[:T, t, :], func=AF.Square, accum_out=ss4[:T, t:t + 1]),
                         reads=[b_x4[t]], writes=[b_junk, b_ss4])
                T0 = tiles[0][0]
                rstd_batch(ss4[:T0, :nt], rs4[:T0, :nt], nt, T0, [b_ss4], [b_rs4], scale=1.0 / D)
                for t, (T, sid) in enumerate(tiles):
                    P.op("vector", lambda e, t=t, T=T: e.scalar_tensor_tensor(out=x4[:T, t, :], in0=x4[:T, t, :], scalar=rs4[:T, t:t + 1],
                                                                              in1=gfin[:T, :], op0=ALU.mult, op1=ALU.mult),
                         reads=[b_x4[t], b_rs4] + CR, writes=[b_x4[t]])
                    dma(dsts[t], x4[:T, t, :], [b_x4[t]], [b_out])
            steps.append(dict(prep=lambda: [], evac=tail, B=None, flush=True))
            return steps

        def run_deferred(force):
            keep = []
            for item in ust["deferred"]:
                item[0] -= 1
                if force or item[0] <= 0:
                    item[1]()
                else:
                    keep.append(item)
            ust["deferred"] = keep

        def run_stream(steps):
            prevB = None
            n = len(steps)
            for idx, stp in enumerate(steps):
                if stp["flush"]:
                    ust["drip"] = None
                    if prevB is not None:
                        prevB()
                        prevB = None
                    run_deferred(True)
                if "mm" not in stp:
                    stp["mm"] = stp["prep"]()
                while stp["mm"]:
                    stp["mm"].pop(0)()
                if prevB is not None and "evl" in stp:
                    ust["evq"] = stp["evl"]()
                else:
                    ust["evq"] = None
                    stp["evac"]()
                if prevB is not None:
                    fut = []
                    for j in (idx + 1, idx + 2):
                        if j < n and not steps[j]["flush"]:
                            fut.append(steps[j])
                        else:
                            break
                    ust["drip"] = fut
                    prevB()
                    evdrip(100)
                    ust["evq"] = None
                    run_deferred(False)
                    ust["drip"] = None
                if ust["pre"]:
                    w2_step(ust["wrate"])
                prevB = stp["B"]
            if prevB is not None:
                prevB()
            run_deferred(True)

        def run_block(tiles, pre, ctxs, dsts, gv_dst, nxt=None):
            ust["loaded"] = {}
            ust["pre"] = pre
            steps = []
            if pre:
                ust["seq"] = list(range(8))
                for h in range(8):
                    steps += ret_steps(h, h, tiles, pre)
            else:
                ust["seq"] = [U_RG0, U_RG1] + list(range(8)) + [U_GG, U_GU, U_GV, U_AG, U_AQ] + [NU_IN + n for n in range(4)]
                steps += rg_steps(0, tiles)
                for h in range(8):
                    steps += ret_steps(2 + h, h, tiles, pre)
                steps += gmlp_steps(10, tiles, gv_dst)
                steps += xattn_steps(13, tiles, ctxs)
                steps += out_steps(15, tiles, dsts, nxt)
            run_stream(steps)

        def ctx_from_f32(cx, c, srck, srcv, rk, rv):
            evac(mkb[:, 0, :], srck, rk, [b_mkb], eng="vector")
            evac(mvc[cx][:, c, :], srcv, rv, [b_ctx[cx]], eng="gpsimd")
            ti = next_tr()
            for hh in range(4):
                P.op("tensor", lambda e, ti=ti, hh=hh: e.transpose(out=TR[ti][:, hh * 128:(hh + 1) * 128], in_=mkb[:, 0, hh * 128:(hh + 1) * 128],
                                                                  identity=ident[:, :]),
                     reads=[b_mkb] + CR, writes=[b_TR[ti]], sig=(hh == 3))
            evac(mkT[cx][:, :, c * 128:(c + 1) * 128], TR[ti][:, 0:512].rearrange("p (a b) -> p a b", b=128), [b_TR[ti]], [b_ctx[cx]])

        print('ops@states', P.nops)
        for h in range(8):
            P.op("vector", lambda e, h=h: e.memset(Sf[0][:, h, :], 0.0), writes=[b_Sf[0][h]])
            P.op("gpsimd", lambda e, h=h: e.memset(Sb[0][:, h, :], 0.0), writes=[b_Sb[0][h]])
        for s in range(2):
            dma(Sf[1 + s][:], st_s[s].rearrange("h d e -> d h e"), [], b_Sf[1 + s])
            for h in range(8):
                evac(Sb[1 + s][:, h, :], Sf[1 + s][:, h, :], [b_Sf[1 + s][h]], [b_Sb[1 + s][h]], eng="gpsimd")

        print('ops@pre', P.nops)
        blk = 0
        for pb in range(NB):
            dma(ropeB[:], rope_in[blk], [], [b_rope]); blk += 1
            w2_drain()
            front([(x_pre[pb * 512 + t * 128: pb * 512 + (t + 1) * 128, :], 128) for t in range(4)])
            if pb == 0:
                while wj["ns"] < 12:
                    w2_step(1)
            ust["wrate"] = 1
            run_block([(128, 0)] * 4, True, None, None, None)
        while wj["ns"] < NJ:
            w2_step(1)
            w2_drain()
        evs = []
        for q_ in range(4):
            evs += ([b_wq[q_].w] if b_wq[q_].w else []) + list(b_wq[q_].r)
        for t_ in range(4):
            b_mixT[t_].w = None
            b_mixT[t_].r = list(evs)
        print('ops@phaseM', P.nops)
        front([(mem[0:128, :], 128), (mem[128:256, :], 128)])
        print('ops@M-front-done', P.nops)
        ik = load_unit(NU_ALL)
        iv = load_unit(NU_ALL + 1)
        for c in range(2):
            pk = proj(c, 128, ik, 512)
            evac(mkf[:, 0, :], PJ[pk][:, :], [b_PJ[pk]], [b_mkf], eng="scalar")
            pv = proj(c, 128, iv, 512)
            evac(mkf[:, 1, :], PJ[pv][:, :], [b_PJ[pv]], [b_mkf], eng="vector")
            dma(mk_out[c * 128:(c + 1) * 128, :], mkf[:, 0, :], [b_mkf], [b_out])
            dma(mv_out[c * 128:(c + 1) * 128, :], mkf[:, 1, :], [b_mkf], [b_out])
            ctx_from_f32(0, c, mkf[:, 0, :], mkf[:, 1, :], [b_mkf], [b_mkf])
        print('ops@samplectx', P.nops)
        for s in range(2):
            for c in range(2):
                dma(mkf[:, 0, :], ck_s[s, c * 128:(c + 1) * 128, :], [], [b_mkf])
                dma(mkf[:, 1, :], cv_s[s, c * 128:(c + 1) * 128, :], [], [b_mkf])
                ctx_from_f32(1 + s, c, mkf[:, 0, :], mkf[:, 1, :], [b_mkf], [b_mkf])
        def main_tiles(mb):
            rows = [(mb * 512 + t * 128, mb * 512 + (t + 1) * 128) for t in range(4)]
            return rows, [(x_main[a_:b_, :], 128) for a_, b_ in rows]
        samp_tiles = [(x_s[0], 32), (x_s[1], 32)]
        for mb in range(NB):
            rows, xt_ = main_tiles(mb)
            if mb == 0:
                dma(ropeB[:], rope_in[blk], [], [b_rope])
            blk += 1
            front(xt_, only_x=(mb > 0))
            nxt = main_tiles(mb + 1)[1] if mb + 1 < NB else samp_tiles
            ust["next_rope"] = rope_in[blk]
            run_block([(128, 0)] * 4, False, [0] * 4, [y_main[a_:b_, :] for a_, b_ in rows], None, nxt=nxt)
        dma(sp_out.rearrange("h d e -> d h e"), Sf[0][:], b_Sf[0], [b_out])
        blk += 1
        front(samp_tiles, only_x=True)
        run_block([(32, 1), (32, 2)], False, [1, 2], [y_s[0], y_s[1]], [gv_out[0], gv_out[1]])
        for s_ in range(2):
            dma(ss_out[s_].rearrange("h d e -> d h e"), Sf[1 + s_][:], b_Sf[1 + s_], [b_out])
        P.wait_all("sync", [b_out])
        print("sbuf bytes remaining", nc.sbuf_bytes_remaining)
        print("planned ops", P.nops, {e: len(P.q[e]) for e in P.ENGS}, "sems", P.nsem)
        P.replay()
    return nc


def _consts(NB, half):
    h_idx = np.arange(8, dtype=np.float64)
    log_gamma = np.log(1.0 - 2.0 ** (-5.0 - h_idx))
    i = np.arange(128, dtype=np.float64)[:, None]
    dq = np.exp(log_gamma[None, :] * (i + 1.0))
    dk = np.exp(-log_gamma[None, :] * (i + 1.0)) * 128.0 ** -0.5
    gam128 = np.broadcast_to(np.exp(log_gamma * 128.0)[None, :], (128, 8))
    gam32 = np.broadcast_to(np.exp(log_gamma * 32.0)[None, :], (128, 8))
    ident = np.eye(128)
    jj = np.arange(128)[:, None]
    ii = np.arange(128)[None, :]
    mask = (ii >= jj).astype(np.float64)
    cst = np.concatenate([ident, mask, dq, dk, gam128, gam32, dk * gam128, dk * gam32, 1.0 / gam128, 1.0 / gam32], axis=1).astype(np.float32)
    dqT = np.ascontiguousarray(np.broadcast_to(dq.T[None, :, :], (128, 8, 128))).astype(np.float32)
    inv_freq = 10000.0 ** (-np.arange(0, 128, 2, dtype=np.float32) / 128.0)
    NTOK = NB * 512

    def tab(pos):
        ang = pos.astype(np.float32)[:, None] * inv_freq[None, :]
        c = np.cos(ang).astype(np.float32)
        s = np.sin(ang).astype(np.float32)
        return np.concatenate([c, c, -s, s], axis=1)

    blocks = []
    for pb in range(NB):
        pos = pb * 512 + np.arange(512)
        blocks.append(tab(pos).reshape(4, 128, 256).transpose(1, 0, 2))
    for mb in range(NB):
        pos = half * NTOK + mb * 512 + np.arange(512)
        blocks.append(tab(pos).reshape(4, 128, 256).transpose(1, 0, 2))
    ps = tab(1024 + np.arange(32))
    sblk = np.zeros((128, 4, 256), np.float32)
    sblk[:32, 0] = ps
    sblk[:32, 1] = ps
    blocks.append(sblk)
    rope = np.ascontiguousarray(np.stack(blocks, 0)).astype(np.float32)
    return cst, rope, dqT


def _unit_layout(w):
    return w.reshape(16, 128, w.shape[1]).transpose(1, 0, 2)


_CACHE = {}


def kernel(x_prompt, x_sample, mem_prompt, state_ret, cache_mem_k, cache_mem_v,
           g_norm, w_in, g_ret, g_gmlp, w_s, b_s, g_mem, w_mem_kv, w_out, g_final):
    f = np.float32
    x_prompt = np.asarray(x_prompt, f); x_sample = np.asarray(x_sample, f)
    Bp, L, _ = x_prompt.shape
    NTOK = L // 2
    NB = NTOK // 512
    if NB not in _CACHE:
        _CACHE[NB] = build_program(NB)
    nc = _CACHE[NB]
    w_in0 = np.asarray(w_in, f)[0]
    w_out0 = np.asarray(w_out, f)[0]
    wm0 = np.asarray(w_mem_kv, f)[0]
    units = []
    for h in range(8):
        cols = np.concatenate([np.arange(h * 128, (h + 1) * 128), np.arange(1024 + h * 128, 1024 + (h + 1) * 128),
                               np.arange(2048 + h * 128, 2048 + (h + 1) * 128), np.arange(h * 128, (h + 1) * 128)])
        units.append(_unit_layout(w_in0[:, cols]))
    for c0 in (3072, 3584, 5120, 4096, 4608, 6144, 5632):
        units.append(_unit_layout(w_in0[:, c0:c0 + 512]))
    for n in range(4):
        units.append(_unit_layout(w_out0[:, n * 512:(n + 1) * 512]))
    w_u = np.ascontiguousarray(np.stack(units, 0))
    wm_u = np.ascontiguousarray(np.stack([_unit_layout(wm0[:, 0:512]), _unit_layout(wm0[:, 512:1024])], 0))
    common = {
        "w_u": w_u, "wm_u": wm_u,
        "gnorm_pk": np.ascontiguousarray(np.asarray(g_norm, f)[0].reshape(16, 128).T),
        "gmem_pk": np.ascontiguousarray(np.asarray(g_mem, f)[0].reshape(16, 128).T),
        "gret_b": np.ascontiguousarray(np.broadcast_to(np.asarray(g_ret, f)[0][None, :], (128, 1024))),
        "ggm_b": np.ascontiguousarray(np.broadcast_to(np.asarray(g_gmlp, f)[0][None, :], (128, 512))),
        "gfin_b": np.ascontiguousarray(np.broadcast_to(np.asarray(g_final, f)[None, :], (128, D))),
        "wsT": np.ascontiguousarray(np.asarray(w_s, f)[0].transpose(2, 0, 1)),
        "bs_t": np.ascontiguousarray(np.asarray(b_s, f)[0].T),
    }
    sr = np.asarray(state_ret, f)[0]
    ck = np.asarray(cache_mem_k, f)[0].reshape(16, 256, 512)
    cv = np.asarray(cache_mem_v, f)[0].reshape(16, 256, 512)
    mp = np.asarray(mem_prompt, f)
    in_maps = []
    for c in range(8):
        b, half = c // 2, c % 2
        cst, rope, dqT_c = _consts(NB, half)
        m = dict(common)
        m["x_main"] = np.ascontiguousarray(x_prompt[b, half * NTOK:(half + 1) * NTOK])
        m["x_pre"] = np.ascontiguousarray(x_prompt[b, 0:NTOK]) if half == 1 else np.zeros((NTOK, D), f)
        m["x_s"] = np.ascontiguousarray(x_sample[2 * c:2 * c + 2])
        m["mem"] = np.ascontiguousarray(mp[b])
        m["st_s"] = np.ascontiguousarray(sr[2 * c:2 * c + 2])
        m["ck_s"] = np.ascontiguousarray(ck[2 * c:2 * c + 2])
        m["cv_s"] = np.ascontiguousarray(cv[2 * c:2 * c + 2])
        m["cst"] = cst
        m["dqT"] = dqT_c
        m["rope"] = rope
        in_maps.append(m)
    if KCORES < 8:
        res = run_bass_kernel_spmd(nc, in_maps[:KCORES], core_ids=list(range(KCORES)))
        R = list(res.results) * 8
    else:
        res = run_bass_kernel_spmd(nc, in_maps, core_ids=list(range(8)))
        R = res.results
    y_prompt = np.stack([np.concatenate([R[2 * b]["y_main"], R[2 * b + 1]["y_main"]], 0) for b in range(Bp)], 0)
    y_sample = np.concatenate([R[c]["y_s"] for c in range(8)], 0)
    sp = np.stack([R[2 * b + 1]["sp_out"] for b in range(Bp)], 0)[None]
    mk = np.stack([R[2 * b]["mk_out"].reshape(256, 4, 128) for b in range(Bp)], 0)[None]
    mv = np.stack([R[2 * b]["mv_out"].reshape(256, 4, 128) for b in range(Bp)], 0)[None]
    ss = np.concatenate([R[c]["ss_out"] for c in range(8)], 0)[None]
    gv = np.concatenate([R[c]["gv_out"] for c in range(8)], 0)[None]
    return (y_prompt.astype(f), y_sample.astype(f), sp.astype(f), mk.astype(f), mv.astype(f), ss.astype(f), gv.astype(f))
```

```python
import numpy as np
import concourse.bass as bass
import concourse.mybir as mybir
from concourse.bass_utils import run_bass_kernel_spmd
from contextlib import ExitStack

F32 = mybir.dt.float32
BF16 = mybir.dt.bfloat16
ALU = mybir.AluOpType
AF = mybir.ActivationFunctionType
AX = mybir.AxisListType

import os
KSTOP = int(os.environ.get("KSTOP", "100000000"))
KCORES = int(os.environ.get("KCORES", "8"))
SEM_LIMIT = 30000
D = 2048
EPS = 1e-6
NU_IN = 15
NU_ALL = 19
U_RG0, U_RG1, U_GG, U_GU, U_GV, U_AG, U_AQ = 8, 9, 10, 11, 12, 13, 14


class Sem:
    def __init__(self, h):
        self.h = h
        self.n = 0


class Buf:
    __slots__ = ("name", "w", "r", "dsem", "excl")

    def __init__(self, name, excl=False):
        self.excl = excl
        self.name = name
        self.w = None
        self.r = []
        self.dsem = None


class Prog:
    ENGS = ("sync", "gpsimd", "scalar", "vector", "tensor")

    def __init__(self, nc, stack):
        self.nc = nc
        self.stack = stack
        self.q = {e: [] for e in self.ENGS}
        self.esem = {}
        self.known = {e: {} for e in self.ENGS}
        self.nsem = 0
        for e in self.ENGS:
            self.esem[e] = self.new_sem("e_" + e)

    def new_sem(self, name):
        self.nsem += 1
        h = self.stack.enter_context(self.nc.semaphore(f"{name}_{self.nsem}"))
        return Sem(h)

    def op(self, eng, fn, reads=(), writes=(), dma=False, sig=True):
        self.nops = getattr(self, "nops", 0) + 1
        if self.nops > KSTOP:
            return
        waits = {}
        kn = self.known[eng]
        pe_sem = self.esem["tensor"]

        def need(ev):
            if ev is None:
                return
            s, v = ev
            if eng == "tensor" and s is pe_sem:
                return
            if kn.get(s, 0) >= v:
                return
            if waits.get(s, 0) < v:
                waits[s] = v

        own = self.esem[eng]
        for b in reads:
            need(b.w)
            if b.excl:
                for ev in b.r:
                    if ev[0] is not own:
                        need(ev)
        for b in writes:
            need(b.w)
            for ev in b.r:
                need(ev)
        for s, v in waits.items():
            kn[s] = v
        if dma:
            tgt = writes[0] if writes else reads[0]
            if tgt.dsem is None or tgt.dsem.n + 16 > SEM_LIMIT:
                tgt.dsem = self.new_sem("d")
            dsem = tgt.dsem
            dsem.n += 16
            ev = (dsem, dsem.n)
            inc = (dsem, 16)
        else:
            s = self.esem[eng]
            if sig:
                s.n += 1
                ev = (s, s.n)
                inc = (s, 1)
                if s.n >= SEM_LIMIT:
                    self.esem[eng] = self.new_sem("e_" + eng)
            else:
                ev = (s, s.n + 1)
                inc = None
        for b in reads:
            b.r.append(ev)
        for b in writes:
            b.w = ev
            b.r = []
        self.q[eng].append((list(waits.items()), fn, inc))

    def wait_all(self, eng, bufs):
        waits = {}
        for b in bufs:
            for ev in ([b.w] if b.w else []) + list(b.r):
                s, v = ev
                if waits.get(s, 0) < v:
                    waits[s] = v
        self.q[eng].append((list(waits.items()), None, None))

    def replay(self):
        nc = self.nc
        with nc.Block() as block:
            def mk(ename):
                def body(e):
                    for waits, fn, inc in self.q[ename]:
                        for s, v in waits:
                            e.wait_ge(s.h, v)
                        if fn is None:
                            continue
                        ins = fn(e)
                        if inc is not None:
                            ins.then_inc(inc[0].h, inc[1])
                return body
            block.sync(mk("sync"))
            block.gpsimd(mk("gpsimd"))
            block.scalar(mk("scalar"))
            block.vector(mk("vector"))
            block.tensor(mk("tensor"))


def build_program(NB):
    nc = bass.Bass("TRN2", target_bir_lowering=False)
    NTOK = NB * 512
    NBLK = 2 * NB + 1

    def din(name, shape, dt=F32):
        return nc.dram_tensor(name, shape, dt, kind="ExternalInput").ap()

    def dout(name, shape):
        return nc.dram_tensor(name, shape, F32, kind="ExternalOutput").ap()

    x_main = din("x_main", [NTOK, D])
    x_pre = din("x_pre", [NTOK, D])
    x_s = din("x_s", [2, 32, D])
    mem = din("mem", [256, D])
    st_s = din("st_s", [2, 8, 128, 128])
    ck_s = din("ck_s", [2, 256, 512])
    cv_s = din("cv_s", [2, 256, 512])
    w_u = din("w_u", [NU_ALL, 128, 16, 512])
    wm_u = din("wm_u", [2, 128, 16, 512])
    gnorm_pk = din("gnorm_pk", [128, 16])
    gmem_pk = din("gmem_pk", [128, 16])
    gret_b = din("gret_b", [128, 1024])
    ggm_b = din("ggm_b", [128, 512])
    gfin_b = din("gfin_b", [128, D])
    wsT_in = din("wsT", [128, 4, 128])
    bs_in = din("bs_t", [128, 4])
    rope_in = din("rope", [NBLK, 128, 4, 256])
    cst_in = din("cst", [128, 320])
    dqT_in = din("dqT", [128, 8, 128])

    y_main = dout("y_main", [NTOK, D])
    y_s = dout("y_s", [2, 32, D])
    sp_out = dout("sp_out", [8, 128, 128])
    mk_out = dout("mk_out", [256, 512])
    mv_out = dout("mv_out", [256, 512])
    ss_out = dout("ss_out", [2, 8, 128, 128])
    gv_out = dout("gv_out", [2, 32, 512])

    scr = nc.dram_tensor("scr_w", [NU_ALL + 2, 128, 16, 512], BF16, kind="Internal").ap()

    with ExitStack() as st:
        P = Prog(nc, st)

        def sb(name, shape, dt):
            return st.enter_context(nc.sbuf_tensor(name, shape, dt))

        def ps(name, shape, dt):
            return st.enter_context(nc.psum_tensor(name, shape, dt))

        def B(name):
            return Buf(name)

        hT = sb("hT", [128, 16, 512], BF16); b_hT = [B(f"hT{t}") for t in range(4)]
        mixT = sb("mixT", [128, 16, 512], BF16); b_mixT = [B(f"mixT{t}") for t in range(4)]
        ub = [sb(f"ub{i}", [128, 16, 512], BF16) for i in range(2)]; b_ub = [B(f"ub{i}") for i in range(2)]
        x4 = sb("x4", [128, 4, D], F32); b_x4 = [B(f"x4_{t}") for t in range(4)]
        xb = sb("xb", [128, D], BF16); b_xb = B("xb")
        junk = sb("junk", [128, D], BF16); b_junk = B("junk")
        sgg = sb("sgg", [128, 4, 512], F32); b_sgg = [B(f"sgg{t}") for t in range(4)]
        gvs = [sb(f"gvs{i}", [128, 512], F32) for i in range(2)]; b_gvs = [B(f"gvs{i}") for i in range(2)]
        gvb = sb("gvb", [128, 512], BF16); b_gvb = B("gvb")
        sgr = sb("sgr", [128, 4, 1024], BF16); b_sgr = [B(f"sgr{t}") for t in range(4)]
        mtok = sb("mtok", [128, 4, 512], BF16); b_mtok = [B(f"mtok{t}") for t in range(4)]
        ropeB = sb("ropeB", [128, 4, 256], F32); b_rope = B("rope")
        gret = sb("gret", [128, 1024], F32)
        ggm = sb("ggm", [128, 512], F32)
        gfin = sb("gfin", [128, D], F32)
        gnp = sb("gnp", [128, 16], F32)
        gmp = sb("gmp", [128, 16], F32)
        b_cst = B("cst")
        cst = sb("cstt", [128, 320], F32)
        identb = sb("identb", [128, 128], BF16)
        wsT_f = sb("wsT_f", [128, 4, 128], F32)
        wsT = sb("wsTb", [128, 4, 128], BF16)
        bst = sb("bs_sb", [128, 4], F32)
        mkT = [sb(f"mkT{i}", [128, 4, 256], BF16) for i in range(3)]
        mvc = [sb(f"mvc{i}", [128, 2, 512], BF16) for i in range(3)]
        b_ctx = [B(f"ctx{i}") for i in range(3)]
        Sf = [sb(f"Sf{i}", [128, 8, 128], F32) for i in range(3)]
        Sb = [sb(f"Sb{i}", [128, 8, 128], BF16) for i in range(3)]
        b_Sf = [[B(f"Sf{i}_{h}") for h in range(8)] for i in range(3)]
        b_Sb = [[B(f"Sb{i}_{h}") for h in range(8)] for i in range(3)]
        r1 = sb("r1", [128, 2, 128], F32); b_r1 = B("r1")
        r2 = sb("r2", [128, 2, 128], F32); b_r2 = B("r2")
        qk = [sb(f"qk{i}", [128, 2, 128], BF16) for i in range(2)]; b_qk = [B(f"qk{i}") for i in range(2)]
        vd = [sb(f"vd{i}", [128, 128], BF16) for i in range(2)]; b_vd = [B(f"vd{i}") for i in range(2)]
        dqT = sb("dqT_sb", [128, 8, 128], F32)
        qkT = sb("qkT", [128, 2, 128], BF16); b_qkT = B("qkT")
        scm = sb("scm", [128, 128], BF16); b_scm = B("scm")
        sg4 = [sb(f"sg4_{i}", [128, 4, 128], F32) for i in range(2)]; b_sg4 = [[B(f"sg4_{i}_{t}") for t in range(4)] for i in range(2)]
        os4 = [sb(f"os4_{i}", [128, 4, 128], F32) for i in range(2)]; b_os4 = [[B(f"os4_{i}_{t}") for t in range(4)] for i in range(2)]
        bn6 = sb("bn6", [128, 6], F32); b_bn6 = B("bn6")
        mv4 = [sb(f"mv4_{i}", [128, 6, 4], F32) for i in range(2)]; b_mv4 = [B(f"mv4_{i}") for i in range(2)]
        mvg = sb("mvg", [128, 2], F32); b_mvg = B("mvg")
        rs4 = sb("rs4", [128, 4], F32); b_rs4 = B("rs4")
        sq4 = sb("sq4", [128, 4], F32); b_sq4 = B("sq4")
        ss4 = sb("ss4", [128, 4], F32); b_ss4 = B("ss4")
        eps_t = sb("eps_t", [128, 1], F32)
        aqT = [sb(f"aqT{i}", [128, 512], BF16) for i in range(2)]; b_aqT = [B(f"aqT{i}") for i in range(2)]
        pexp = sb("pexp", [128, 256], BF16); b_pexp = B("pexp")
        pT = sb("pT", [128, 2, 128], BF16); b_pT = B("pT")
        sm3 = sb("sm3", [128, 4], F32); b_sm3 = B("sm3")
        mkf = sb("mkf", [128, 2, 512], F32); b_mkf = B("mkf")
        mkb = sb("mkb", [128, 2, 512], BF16); b_mkb = B("mkb")
        PJ = [ps(f"PJ{i}", [128, 512], F32) for i in range(4)]; b_PJ = [Buf(f"PJ{i}", True) for i in range(4)]
        TR = [ps(f"TR{i}", [128, 1024], BF16) for i in range(2)]; b_TR = [Buf(f"TR{i}", True) for i in range(2)]
        SO = ps("SO", [128, 512], F32); b_SO = Buf("SO", True)
        SC = SO; b_SC = b_SO
        OP = SO; b_OP = b_SO
        STP = ps("STP", [128, 512], F32); b_STP = Buf("STP", True)

        ctr = {"pj": 0, "tr": 0, "ub": 0, "ev": 0}
        b_scr = [B(f"scr{u}") for u in range(NU_ALL + 2)]
        b_out = B("out")

        ident = identb
        mask = cst[:, 128:256]
        dq = cst[:, 256:264]
        dk = cst[:, 264:272]
        gam = {128: cst[:, 272:280], 32: cst[:, 280:288]}
        dkg = {128: cst[:, 288:296], 32: cst[:, 296:304]}
        ginv = {128: cst[:, 304:312], 32: cst[:, 312:320]}

        def dma(out, in_, reads, writes):
            P.op("sync", lambda e: e.dma_start(out=out, in_=in_), reads=reads, writes=writes, dma=True)

        def evac(out, in_, reads, writes, eng=None):
            if eng is None:
                ctr["ev"] += 1
                eng = "scalar" if ctr["ev"] % 2 else "vector"
            if eng == "scalar":
                P.op("scalar", lambda e: e.activation(out=out, in_=in_, func=AF.Copy), reads=reads, writes=writes)
            else:
                P.op(eng, lambda e: e.tensor_copy(out=out, in_=in_), reads=reads, writes=writes)

        dma(cst[:], cst_in, [], [b_cst])
        dma(dqT[:], dqT_in, [], [b_cst])
        dma(gret[:], gret_b, [], [b_cst])
        dma(ggm[:], ggm_b, [], [b_cst])
        dma(gfin[:], gfin_b, [], [b_cst])
        dma(gnp[:], gnorm_pk, [], [b_cst])
        dma(gmp[:], gmem_pk, [], [b_cst])
        dma(wsT_f[:], wsT_in, [], [b_cst])
        dma(bst[:], bs_in, [], [b_cst])
        b_c2 = B("c2")
        P.op("vector", lambda e: e.tensor_copy(out=identb[:], in_=cst[:, 0:128]), reads=[b_cst], writes=[b_c2])
        P.op("vector", lambda e: e.memset(eps_t[:], EPS), writes=[b_c2])
        for g in range(4):
            P.op("vector", lambda e, g=g: e.tensor_tensor(out=wsT[:, g, :], in0=wsT_f[:, g, :], in1=mask, op=ALU.mult),
                 reads=[b_cst], writes=[b_c2])
        CR = [b_cst, b_c2]

        cv_eng = ["vector", "scalar", "vector"]
        cvs = {"i": 0}

        def conv_quarter(u, qd, stage, b_stage, dst, b_dst, engs):
            src = w_u[u] if u < NU_ALL else wm_u[u - NU_ALL]
            gsc = gnp if u < NU_IN else (gmp if u >= NU_ALL else None)
            dma(stage, src[:, qd * 4:(qd + 1) * 4, :], [], [b_stage])
            for a in range(4):
                kc = qd * 4 + a
                eng = engs[cvs["i"] % len(engs)]; cvs["i"] += 1
                if gsc is None:
                    evac(dst[:, kc, :], stage[:, a, :], [b_stage], [b_dst], eng=eng)
                elif eng == "scalar":
                    P.op("scalar", lambda e, kc=kc, a=a: e.activation(
                        out=dst[:, kc, :], in_=stage[:, a, :], func=AF.Copy, scale=gsc[:, kc:kc + 1]),
                        reads=[b_stage] + CR, writes=[b_dst])
                else:
                    P.op(eng, lambda e, kc=kc, a=a: e.tensor_scalar(
                        out=dst[:, kc, :], in0=stage[:, a, :], scalar1=gsc[:, kc:kc + 1], scalar2=None, op0=ALU.mult),
                        reads=[b_stage] + CR, writes=[b_dst])

        b_wq = [B(f"wq{i}") for i in range(4)]
        w2_jobs = []
        for u in list(range(8)) + [NU_ALL, NU_ALL + 1] + list(range(8, NU_ALL)):
            for qd in range(4):
                w2_jobs.append((u, qd))

        jobs = list(w2_jobs)
        NJ = len(jobs)
        b_scrq = {}
        slot_sem = [P.new_sem(f"wst{i}") for i in range(4)]
        for j_, (u_, q_) in enumerate(jobs):
            b_scrq[(u_, q_)] = B(f"scr{u_}_{q_}")
            b_scrq[(u_, q_)].dsem = slot_sem[j_ % 4]

        def j_load(j):
            if not (0 <= j < NJ):
                return
            u, qd = jobs[j]
            slot = j % 4
            stage = x4[:, slot, :].rearrange("p (a b) -> p a b", b=512)
            src = w_u[u] if u < NU_ALL else wm_u[u - NU_ALL]
            dma(stage, src[:, qd * 4:(qd + 1) * 4, :], [], [b_x4[slot]])

        def j_conv(j):
            if not (0 <= j < NJ):
                return
            u, qd = jobs[j]
            slot = j % 4
            stage = x4[:, slot, :].rearrange("p (a b) -> p a b", b=512)
            dstq = mixT[:, slot * 4:(slot + 1) * 4, :]
            gsc = gnp if u < NU_IN else (gmp if u >= NU_ALL else None)
            for a_ in range(4):
                eng = cv_eng[cvs["i"] % 3]; cvs["i"] += 1
                if gsc is None:
                    evac(dstq[:, a_, :], stage[:, a_, :], [b_x4[slot]], [b_wq[slot]], eng=eng)
                elif eng == "scalar":
                    P.op("scalar", lambda e, a_=a_: e.activation(out=dstq[:, a_, :], in_=stage[:, a_, :], func=AF.Copy,
                                                                 scale=gsc[:, qd * 4 + a_:qd * 4 + a_ + 1]),
                         reads=[b_x4[slot]] + CR, writes=[b_wq[slot]])
                else:
                    P.op(eng, lambda e, a_=a_: e.tensor_scalar(out=dstq[:, a_, :], in0=stage[:, a_, :],
                                                                scalar1=gsc[:, qd * 4 + a_:qd * 4 + a_ + 1], scalar2=None, op0=ALU.mult),
                         reads=[b_x4[slot]] + CR, writes=[b_wq[slot]])

        def j_store(j):
            if not (0 <= j < NJ):
                return
            u, qd = jobs[j]
            slot = j % 4
            dstq = mixT[:, slot * 4:(slot + 1) * 4, :]
            dma(scr[u][:, qd * 4:(qd + 1) * 4, :], dstq, [b_wq[slot]], [b_scrq[(u, qd)]])

        wj = {"nl": 0, "nc": 0, "ns": 0}

        def w2_step(n=1):
            for _ in range(n):
                while wj["nl"] < NJ and wj["nl"] < wj["nc"] + 3:
                    j_load(wj["nl"]); wj["nl"] += 1
                if wj["ns"] < wj["nc"]:
                    j_store(wj["ns"]); wj["ns"] += 1
                if wj["nc"] < wj["nl"]:
                    j_conv(wj["nc"]); wj["nc"] += 1

        def w2_drain():
            while wj["nc"] < wj["nl"]:
                j_conv(wj["nc"]); wj["nc"] += 1
            while wj["ns"] < wj["nc"]:
                j_store(wj["ns"]); wj["ns"] += 1

        def load_unit(u, ncols=512):
            i = ctr["ub"] % 2; ctr["ub"] += 1
            rd = [b_scrq[(u, q_)] for q_ in range(4)]
            assert all(b_.w is not None for b_ in rd), ("unit loaded before converted", u)
            if ncols == 512:
                dma(ub[i][:], scr[u], rd, [b_ub[i]])
            elif ncols == 256:
                dma(ub[i][:, :, 128:384], scr[u][:, :, 128:384], rd, [b_ub[i]])
            else:
                dma(ub[i][:, :, 0:ncols], scr[u][:, :, 0:ncols], rd, [b_ub[i]])
            return i

        def next_pj():
            i = ctr["pj"] % 4; ctr["pj"] += 1
            return i

        def next_tr():
            i = ctr["tr"] % 2; ctr["tr"] += 1
            return i

        def rstd_batch(src, dst, n, T, reads_b, writes_b, scale=None):
            if scale is None:
                P.op("scalar", lambda e: e.activation(out=sq4[:T, :n], in_=src, func=AF.Sqrt, bias=eps_t[:T, 0:1], scale=1.0),
                     reads=reads_b + CR, writes=[b_sq4])
            else:
                P.op("scalar", lambda e: e.activation(out=sq4[:T, :n], in_=src, func=AF.Sqrt, bias=eps_t[:T, 0:1], scale=scale),
                     reads=reads_b + CR, writes=[b_sq4])
            P.op("vector", lambda e: e.reciprocal(out=dst, in_=sq4[:T, :n]), reads=[b_sq4], writes=writes_b)

        def front(tiles, only_x=False):
            nt = len(tiles)
            if only_x:
                for t, (src, T) in enumerate(tiles):
                    dma(x4[:T, t, :], src, [], [b_x4[t]])
                return
            for t, (src, T) in enumerate(tiles):
                dma(x4[:T, t, :], src, [], [b_x4[t]])
                P.op("scalar", lambda e, t=t, T=T: e.activation(out=junk[:T, :], in_=x4[:T, t, :], func=AF.Square,
                                                               accum_out=ss4[:T, t:t + 1]),
                     reads=[b_x4[t]], writes=[b_junk, b_ss4])
            T0 = tiles[0][1]
            rstd_batch(ss4[:T0, :nt], rs4[:T0, :nt], nt, T0, [b_ss4], [b_rs4], scale=1.0 / D)
            for t, (src, T) in enumerate(tiles):
                P.op("vector", lambda e, t=t, T=T: e.tensor_scalar(out=xb[:T, :], in0=x4[:T, t, :], scalar1=rs4[:T, t:t + 1],
                                                                   scalar2=None, op0=ALU.mult),
                     reads=[b_x4[t], b_rs4], writes=[b_xb])
                for grp in range(2):
                    ti = next_tr()
                    for j in range(8):
                        kc = grp * 8 + j
                        P.op("tensor", lambda e, ti=ti, j=j, kc=kc, T=T: e.transpose(
                            out=TR[ti][:, j * 128:j * 128 + T], in_=xb[:T, kc * 128:(kc + 1) * 128], identity=ident[:T, :T]),
                            reads=[b_xb] + CR, writes=[b_TR[ti]], sig=(j == 7))
                    evac(hT[:, grp * 8:(grp + 1) * 8, t * 128:t * 128 + T],
                         TR[ti][:].rearrange("p (a b) -> p a b", b=128)[:, :, :T], [b_TR[ti]], [b_hT[t]])

        ssE = sb("ssE", [128, 4], F32); b_ssE = B("ssE")
        rsE = sb("rsE", [128, 4], F32); b_rsE = B("rsE")
        sgg_flat = sgg[:].rearrange("p a b -> p (a b)")

        def efront_load(t, src, T):
            dma(sgg_flat[:T, :], src, [], b_sgg)
            P.op("scalar", lambda e: e.activation(out=junk[:T, :], in_=sgg_flat[:T, :], func=AF.Square, accum_out=ssE[:T, t:t + 1]),
                 reads=b_sgg, writes=[b_junk, b_ssE])
            P.op("scalar", lambda e: e.activation(out=sq4[:T, 0:1], in_=ssE[:T, t:t + 1], func=AF.Sqrt, bias=eps_t[:T, 0:1], scale=1.0 / D),
                 reads=[b_ssE] + CR, writes=[b_sq4])
            P.op("vector", lambda e: e.reciprocal(out=rsE[:T, t:t + 1], in_=sq4[:T, 0:1]), reads=[b_sq4], writes=[b_rsE])
            P.op("vector", lambda e: e.tensor_scalar(out=xb[:T, :], in0=sgg_flat[:T, :], scalar1=rsE[:T, t:t + 1], scalar2=None, op0=ALU.mult),
                 reads=b_sgg + [b_rsE], writes=[b_xb])

        def efront_tr(t, T):
            for grp in range(2):
                ti = next_tr()
                for j in range(8):
                    kc = grp * 8 + j
                    P.op("tensor", lambda e, ti=ti, j=j, kc=kc: e.transpose(
                        out=TR[ti][:, j * 128:j * 128 + T], in_=xb[:T, kc * 128:(kc + 1) * 128], identity=ident[:T, :T]),
                        reads=[b_xb] + CR, writes=[b_TR[ti]], sig=(j == 7))
                evac(hT[:, grp * 8:(grp + 1) * 8, t * 128:t * 128 + T],
                     TR[ti][:].rearrange("p (a b) -> p a b", b=128)[:, :, :T], [b_TR[ti]], [b_hT[t]])

        def proj(t, T, i, ncols, c0=0):
            pj = next_pj()
            for kc in range(16):
                P.op("tensor", lambda e, kc=kc, pj=pj, t=t, T=T, i=i: e.matmul(
                    PJ[pj][:T, :ncols], lhsT=hT[:, kc, t * 128:t * 128 + T], rhs=ub[i][:, kc, c0:c0 + ncols],
                    start=(kc == 0), stop=(kc == 15)),
                    reads=[b_hT[t], b_ub[i]], writes=[b_PJ[pj]], sig=(kc == 15))
            return pj

        def rope2(pj, ns, T, t, p):
            X = PJ[pj][:T, 0:ns * 128].rearrange("p (s d) -> p s d", d=128)
            A = ropeB[:T, t, 0:128].unsqueeze(1).broadcast_to([T, ns, 128])
            B1 = ropeB[:T, t, 128:192].unsqueeze(1).broadcast_to([T, ns, 64])
            B2 = ropeB[:T, t, 192:256].unsqueeze(1).broadcast_to([T, ns, 64])
            return [
                lambda: P.op("vector", lambda e: e.tensor_tensor(out=r1[:T, 0:ns, :], in0=X, in1=A, op=ALU.mult),
                             reads=[b_PJ[pj], b_rope], writes=[b_r1]),
                lambda: P.op("vector", lambda e: e.tensor_tensor(out=r2[:T, 0:ns, 0:64], in0=X[:, :, 64:128], in1=B1, op=ALU.mult),
                             reads=[b_PJ[pj], b_rope], writes=[b_r2]),
                lambda: P.op("vector", lambda e: e.tensor_tensor(out=r2[:T, 0:ns, 64:128], in0=X[:, :, 0:64], in1=B2, op=ALU.mult),
                             reads=[b_PJ[pj], b_rope], writes=[b_r2]),
                lambda: P.op("vector", lambda e: e.tensor_tensor(out=qk[p][:T, 2 - ns:2, :], in0=r1[:T, 0:ns, :], in1=r2[:T, 0:ns, :], op=ALU.add),
                             reads=[b_r1, b_r2], writes=[b_qk[p]]),
            ]

        ust = {"loaded": {}, "seq": [], "stepno": 0, "drip": None, "pre": False, "deferred": [], "wrate": 1}

        def get_unit(k):
            if k >= len(ust["seq"]):
                return None
            if k not in ust["loaded"]:
                u = ust["seq"][k]
                ust["loaded"][k] = load_unit(u, (256 if ust["pre"] else 384) if u < 8 else 512)
            return ust["loaded"][k]

        def drip(n):
            for d in (ust["drip"] or []):
                if "mm" not in d:
                    d["mm"] = d["prep"]()
                while n > 0 and d["mm"]:
                    d["mm"].pop(0)()
                    n -= 1
                if n == 0:
                    return

        def evdrip(n):
            q_ = ust.get("evq")
            while q_ and n > 0:
                q_.pop(0)()
                n -= 1

        def proj_mm(t, T, i, ncols, pj, c0=0):
            return [(lambda kc=kc: P.op("tensor", lambda e: e.matmul(
                PJ[pj][:T, :ncols], lhsT=hT[:, kc, t * 128:t * 128 + T], rhs=ub[i][:, kc, c0:c0 + ncols],
                start=(kc == 0), stop=(kc == 15)),
                reads=[b_hT[t], b_ub[i]], writes=[b_PJ[pj]], sig=(kc == 15))) for kc in range(16)]

        def ret_steps(k, h, tiles, pre):
            steps = []
            nt = len(tiles)
            up = h % 2
            for t, (T, sid) in enumerate(tiles):
                st_ = {}

                def prep(t=t, T=T, st_=st_):
                    i = get_unit(k)
                    if t == 0:
                        get_unit(k + 1)
                    st_["p"] = ust["stepno"] % 2; ust["stepno"] += 1
                    st_["pj"] = next_pj()
                    return proj_mm(t, T, i, 256 if pre else 384, st_["pj"], c0=(128 if pre else 0))

                def evl(t=t, T=T, st_=st_):
                    p, pj = st_["p"], st_["pj"]
                    vc = 128 if pre else 256
                    return [lambda: P.op("scalar", lambda e: e.activation(out=vd[p][:T, :], in_=PJ[pj][:T, vc:vc + 128], func=AF.Copy,
                                                                          scale=dkg[T][:T, h:h + 1]),
                                         reads=[b_PJ[pj]] + CR, writes=[b_vd[p]])] + rope2(pj, 1 if pre else 2, T, t, p)

                def ev(evl=evl):
                    for f_ in evl():
                        f_()

                def tail():
                    T0 = tiles[0][0]
                    M = mv4[up]
                    P.op("vector", lambda e: e.tensor_scalar(out=M[:T0, 2, :nt], in0=M[:T0, 0, :nt], scalar1=1.0 / 128, scalar2=None, op0=ALU.mult),
                         reads=[b_mv4[up]], writes=[b_mv4[up]])
                    P.op("vector", lambda e: e.tensor_tensor(out=M[:T0, 3, :nt], in0=M[:T0, 2, :nt], in1=M[:T0, 2, :nt], op=ALU.mult),
                         reads=[b_mv4[up]], writes=[b_mv4[up]])
                    P.op("vector", lambda e: e.scalar_tensor_tensor(out=M[:T0, 3, :nt], in0=M[:T0, 1, :nt], scalar=1.0 / 128, in1=M[:T0, 3, :nt],
                                                                    op0=ALU.mult, op1=ALU.subtract),
                         reads=[b_mv4[up]], writes=[b_mv4[up]])
                    rstd_batch(M[:T0, 3, :nt], M[:T0, 4, :nt], nt, T0, [b_mv4[up]], [b_mv4[up]])
                    P.op("vector", lambda e: e.scalar_tensor_tensor(out=M[:T0, 5, :nt], in0=M[:T0, 2, :nt], scalar=-1.0, in1=M[:T0, 4, :nt],
                                                                    op0=ALU.mult, op1=ALU.mult),
                         reads=[b_mv4[up]], writes=[b_mv4[up]])
                    for t2, (T2, sid2) in enumerate(tiles):
                        P.op("scalar", lambda e, t2=t2, T2=T2: e.activation(out=os4[up][:T2, t2, :], in_=os4[up][:T2, t2, :], func=AF.Identity,
                                                                            bias=M[:T2, 5, t2:t2 + 1], scale=M[:T2, 4, t2:t2 + 1]),
                             reads=[b_os4[up][t2], b_mv4[up]], writes=[b_os4[up][t2]])
                        P.op("gpsimd", lambda e, t2=t2, T2=T2: e.tensor_tensor(out=sg4[up][:T2, t2, :], in0=sgr[:T2, t2, h * 128:(h + 1) * 128],
                                                                               in1=gret[:T2, h * 128:(h + 1) * 128], op=ALU.mult),
                             reads=[b_sgr[t2]] + CR, writes=[b_sg4[up][t2]])
                        P.op("vector", lambda e, t2=t2, T2=T2: e.tensor_tensor(out=mtok[:T2, t2, 0:128], in0=os4[up][:T2, t2, :], in1=sg4[up][:T2, t2, :],
                                                                               op=ALU.mult),
                             reads=[b_os4[up][t2], b_sg4[up][t2]], writes=[b_mtok[t2]])
                    for t2, (T2, sid2) in enumerate(tiles):
                        ti = 1
                        P.op("tensor", lambda e, ti=ti, t2=t2, T2=T2: e.transpose(out=TR[ti][:, t2 * 128:t2 * 128 + T2], in_=mtok[:T2, t2, 0:128], identity=ident[:T2, :T2]),
                             reads=[b_mtok[t2]] + CR, writes=[b_TR[ti]])
                        evac(mixT[:, h, t2 * 128:t2 * 128 + T2], TR[ti][:, t2 * 128:t2 * 128 + T2], [b_TR[ti]], [b_mixT[t2]])

                def Bf(t=t, T=T, sid=sid, st_=st_):
                    p = st_["p"]
                    if not pre:
                        ti = 0
                        for s_ in range(2):
                            P.op("tensor", lambda e, s_=s_: e.transpose(out=TR[ti][:, s_ * 128:s_ * 128 + T], in_=qk[p][:T, s_, :],
                                                                         identity=ident[:T, :T]),
                                 reads=[b_qk[p]] + CR, writes=[b_TR[ti]], sig=(s_ == 1))
                        evac(qkT[:, 1, :T], TR[ti][:, 128:128 + T], [b_TR[ti]], [b_qkT], eng="scalar")
                        P.op("vector", lambda e: e.tensor_tensor(out=qkT[:, 0, :T], in0=TR[ti][:, 0:T], in1=dqT[:, h, :T], op=ALU.mult),
                             reads=[b_TR[ti]] + CR, writes=[b_qkT])
                        evdrip(2)
                    P.op("tensor", lambda e: e.matmul(STP[:, 0:128], lhsT=qk[p][:T, 1, :], rhs=vd[p][:T, :], start=True, stop=True),
                         reads=[b_qk[p], b_vd[p]], writes=[b_STP])
                    if not pre:
                        drip(8)
                        P.op("tensor", lambda e: e.matmul(SC[:T, :T], lhsT=qkT[:, 1, :T], rhs=qkT[:, 0, :T], start=True, stop=True),
                             reads=[b_qkT], writes=[b_SC])
                        P.op("vector", lambda e: e.scalar_tensor_tensor(out=scm[:T, :T], in0=SC[:T, :T], scalar=ginv[T][:T, h:h + 1], in1=mask[:T, :T],
                                                                        op0=ALU.mult, op1=ALU.mult),
                             reads=[b_SC] + CR, writes=[b_scm])
                        evdrip(2)
                        drip(8)
                        P.op("tensor", lambda e: e.matmul(OP[:T, 256:384], lhsT=scm[:T, :T], rhs=vd[p][:T, :], start=True, stop=False),
                             reads=[b_scm, b_vd[p]], writes=[b_OP], sig=False)
                        P.op("tensor", lambda e: e.matmul(OP[:T, 256:384], lhsT=qkT[:, 0, :T], rhs=Sb[sid][:, h, :], start=False, stop=True),
                             reads=[b_qkT, b_Sb[sid][h]], writes=[b_OP])
                        P.op("scalar", lambda e: e.activation(out=os4[up][:T, t, :], in_=OP[:T, 256:384], func=AF.Copy,
                                                              accum_out=mv4[up][:T, 0, t:t + 1]),
                             reads=[b_OP], writes=[b_os4[up][t], b_mv4[up]])
                        P.op("scalar", lambda e: e.activation(out=junk[:T, 0:128], in_=OP[:T, 256:384], func=AF.Square,
                                                              accum_out=mv4[up][:T, 1, t:t + 1]),
                             reads=[b_OP], writes=[b_junk, b_mv4[up]])
                    g_ = gam[T][:, h:h + 1]
                    P.op("vector", lambda e: e.scalar_tensor_tensor(out=Sf[sid][:, h, :], in0=Sf[sid][:, h, :], scalar=g_, in1=STP[:, 0:128],
                                                                    op0=ALU.mult, op1=ALU.add),
                         reads=[b_STP, b_Sf[sid][h]] + CR, writes=[b_Sf[sid][h]])
                    if not pre or t == nt - 1:
                        P.op("gpsimd", lambda e: e.tensor_copy(out=Sb[sid][:, h, :], in_=Sf[sid][:, h, :]),
                             reads=[b_Sf[sid][h]], writes=[b_Sb[sid][h]])
                    if pre or t != nt - 1:
                        return
                    ust["deferred"].append([2, tail])

                steps.append(dict(prep=prep, evac=ev, evl=evl, B=Bf, flush=False))
            return steps

        def rg_steps(k0, tiles):
            steps = []
            for half in range(2):
                for t, (T, sid) in enumerate(tiles):
                    st_ = {}

                    def prep(half=half, t=t, T=T, st_=st_):
                        i = get_unit(k0 + half)
                        if t == 0:
                            get_unit(k0 + half + 1)
                        st_["pj"] = next_pj()
                        return proj_mm(t, T, i, 512, st_["pj"])

                    def ev(half=half, t=t, T=T, st_=st_):
                        pj = st_["pj"]
                        P.op("scalar", lambda e: e.activation(out=sgr[:T, t, half * 512:(half + 1) * 512], in_=PJ[pj][:T, :], func=AF.Silu),
                             reads=[b_PJ[pj]], writes=[b_sgr[t]])
                    steps.append(dict(prep=prep, evac=ev, B=None, flush=False))
            return steps

        def mix_transposes(t, T, kc0):
            ti = next_tr()
            for g in range(4):
                P.op("tensor", lambda e, ti=ti, g=g, t=t, T=T: e.transpose(out=TR[ti][:, g * 128:g * 128 + T],
                                                                           in_=mtok[:T, t, g * 128:(g + 1) * 128], identity=ident[:T, :T]),
                     reads=[b_mtok[t]] + CR, writes=[b_TR[ti]], sig=(g == 3))
            evac(mixT[:, kc0:kc0 + 4, t * 128:t * 128 + T],
                 TR[ti][:, 0:512].rearrange("p (a b) -> p a b", b=128)[:, :, :T], [b_TR[ti]], [b_mixT[t]])

        def gmlp_steps(k0, tiles, gv_dst):
            steps = []
            for t, (T, sid) in enumerate(tiles):
                st_ = {}

                def prep(t=t, T=T, st_=st_):
                    i = get_unit(k0)
                    if t == 0:
                        get_unit(k0 + 1)
                    st_["pj"] = next_pj()
                    return proj_mm(t, T, i, 512, st_["pj"])

                def ev(t=t, T=T, st_=st_):
                    pj = st_["pj"]
                    P.op("scalar", lambda e: e.activation(out=sgg[:T, t, :], in_=PJ[pj][:T, :], func=AF.Silu),
                         reads=[b_PJ[pj]], writes=[b_sgg[t]])
                steps.append(dict(prep=prep, evac=ev, B=None, flush=False))
            for t, (T, sid) in enumerate(tiles):
                st_ = {}

                def prep(t=t, T=T, st_=st_):
                    i = get_unit(k0 + 1)
                    if t == 0:
                        get_unit(k0 + 2)
                    st_["pj"] = next_pj()
                    return proj_mm(t, T, i, 512, st_["pj"])

                def ev(t=t, T=T, st_=st_):
                    pj = st_["pj"]
                    P.op("vector", lambda e: e.tensor_tensor(out=sgg[:T, t, :], in0=PJ[pj][:T, :], in1=sgg[:T, t, :], op=ALU.mult),
                         reads=[b_PJ[pj], b_sgg[t]], writes=[b_sgg[t]])
                steps.append(dict(prep=prep, evac=ev, B=None, flush=False))
            for t, (T, sid) in enumerate(tiles):
                st_ = {}

                def prep(t=t, T=T, st_=st_):
                    i = get_unit(k0 + 2)
                    if t == 0:
                        get_unit(k0 + 3)
                    st_["p"] = ust["stepno"] % 2; ust["stepno"] += 1
                    st_["pj"] = next_pj()
                    return proj_mm(t, T, i, 512, st_["pj"])

                def ev(t=t, T=T, st_=st_):
                    p, pj = st_["p"], st_["pj"]
                    evac(gvs[p][:T, :], PJ[pj][:T, :], [b_PJ[pj]], [b_gvs[p]], eng="scalar")

                def Bf(t=t, T=T, st_=st_):
                    p = st_["p"]
                    G = gvs[p]
                    P.op("vector", lambda e: e.bn_stats(out=bn6[:T, :], in_=G[:T, :]), reads=[b_gvs[p]], writes=[b_bn6])
                    P.op("vector", lambda e: e.bn_aggr(out=mvg[:T, :], in_=bn6[:T, :]), reads=[b_bn6], writes=[b_mvg])
                    rstd_batch(mvg[:T, 1:2], rs4[:T, 0:1], 1, T, [b_mvg], [b_rs4])
                    P.op("vector", lambda e: e.tensor_scalar(out=G[:T, :], in0=G[:T, :], scalar1=mvg[:T, 0:1], scalar2=rs4[:T, 0:1],
                                                             op0=ALU.subtract, op1=ALU.mult),
                         reads=[b_gvs[p], b_mvg, b_rs4], writes=[b_gvs[p]])
                    P.op("vector", lambda e: e.tensor_tensor(out=G[:T, :], in0=G[:T, :], in1=ggm[:T, :], op=ALU.mult),
                         reads=[b_gvs[p]] + CR, writes=[b_gvs[p]])
                    if gv_dst is not None:
                        dma(gv_dst[t], G[:T, :], [b_gvs[p]], [b_out])
                    evac(gvb[:T, :], G[:T, :], [b_gvs[p]], [b_gvb], eng="scalar")
                    drip(8)
                    for g in range(4):
                        P.op("tensor", lambda e, g=g: e.matmul(OP[:T, g * 128:(g + 1) * 128], lhsT=wsT[:T, g, :T],
                                                               rhs=gvb[:T, g * 128:(g + 1) * 128], start=True, stop=True),
                             reads=[b_gvb] + CR, writes=[b_OP], sig=(g == 3))
                    for g in range(4):
                        P.op("vector", lambda e, g=g: e.scalar_tensor_tensor(
                            out=mtok[:T, t, g * 128:(g + 1) * 128], in0=OP[:T, g * 128:(g + 1) * 128], scalar=bst[:T, g:g + 1],
                            in1=sgg[:T, t, g * 128:(g + 1) * 128], op0=ALU.add, op1=ALU.mult),
                            reads=[b_OP, b_sgg[t]] + CR, writes=[b_mtok[t]])
                    drip(8)
                    mix_transposes(t, T, 8)
                steps.append(dict(prep=prep, evac=ev, B=Bf, flush=False))
            return steps

        def xattn_steps(k0, tiles, ctxs):
            steps = []
            nt = len(tiles)
            ncol = nt * 128
            for t, (T, sid) in enumerate(tiles):
                st_ = {}

                def prep(t=t, T=T, st_=st_):
                    i = get_unit(k0)
                    if t == 0:
                        get_unit(k0 + 1)
                    st_["pj"] = next_pj()
                    return proj_mm(t, T, i, 512, st_["pj"])

                def ev(t=t, T=T, st_=st_):
                    pj = st_["pj"]
                    P.op("scalar", lambda e: e.activation(out=sgg[:T, t, :], in_=PJ[pj][:T, :], func=AF.Silu),
                         reads=[b_PJ[pj]], writes=[b_sgg[t]])
                steps.append(dict(prep=prep, evac=ev, B=None, flush=False))
            for hh in range(4):
                st_ = {}

                def prep(hh=hh, st_=st_):
                    i = get_unit(k0 + 1)
                    if hh == 0:
                        get_unit(k0 + 2)
                    st_["p"] = ust["stepno"] % 2; ust["stepno"] += 1
                    pj = st_["pj"] = next_pj()
                    return [(lambda kc=kc: P.op("tensor", lambda e: e.matmul(
                        PJ[pj][:, :ncol], lhsT=ub[i][:, kc, hh * 128:(hh + 1) * 128], rhs=hT[:, kc, 0:ncol],
                        start=(kc == 0), stop=(kc == 15)),
                        reads=b_hT[:nt] + [b_ub[i]], writes=[b_PJ[pj]], sig=(kc == 15))) for kc in range(16)]

                def ev(hh=hh, st_=st_):
                    p, pj = st_["p"], st_["pj"]
                    P.op("scalar", lambda e: e.activation(out=aqT[p][:, :ncol], in_=PJ[pj][:, :ncol], func=AF.Copy, scale=128.0 ** -0.5),
                         reads=[b_PJ[pj]], writes=[b_aqT[p]])

                def Bf(hh=hh, st_=st_):
                    p = st_["p"]
                    for t, (T, sid) in enumerate(tiles):
                        cx = ctxs[t]
                        P.op("tensor", lambda e, t=t, T=T, cx=cx: e.matmul(SC[:T, 0:256], lhsT=aqT[p][:, t * 128:t * 128 + T],
                                                                            rhs=mkT[cx][:, hh, :], start=True, stop=True),
                             reads=[b_aqT[p], b_ctx[cx]], writes=[b_SC])
                        P.op("vector", lambda e, T=T: e.reduce_max(out=sm3[:T, 0:1], in_=SC[:T, 0:256], axis=AX.X), reads=[b_SC], writes=[b_sm3])
                        P.op("vector", lambda e, T=T: e.tensor_scalar(out=sm3[:T, 1:2], in0=sm3[:T, 0:1], scalar1=-1.0, scalar2=None, op0=ALU.mult),
                             reads=[b_sm3], writes=[b_sm3])
                        P.op("scalar", lambda e, T=T: e.activation(out=pexp[:T, :], in_=SC[:T, 0:256], func=AF.Exp, bias=sm3[:T, 1:2], scale=1.0,
                                                                   accum_out=sm3[:T, 2:3]),
                             reads=[b_SC, b_sm3], writes=[b_pexp, b_sm3])
                        drip(2)
                        ti = next_tr()
                        for c in range(2):
                            P.op("tensor", lambda e, ti=ti, c=c, T=T: e.transpose(out=TR[ti][:, c * 128:c * 128 + T], in_=pexp[:T, c * 128:(c + 1) * 128],
                                                                                    identity=ident[:T, :T]),
                                 reads=[b_pexp] + CR, writes=[b_TR[ti]], sig=(c == 1))
                        evac(pT[:, :, :T], TR[ti][:, 0:256].rearrange("p (a b) -> p a b", b=128)[:, :, :T], [b_TR[ti]], [b_pT])
                        drip(2)
                        for c in range(2):
                            P.op("tensor", lambda e, c=c, T=T, cx=cx: e.matmul(OP[:T, 256:384], lhsT=pT[:, c, :T],
                                                                               rhs=mvc[cx][:, c, hh * 128:(hh + 1) * 128],
                                                                               start=(c == 0), stop=(c == 1)),
                                 reads=[b_pT, b_ctx[cx]], writes=[b_OP], sig=(c == 1))
                        P.op("vector", lambda e, T=T: e.reciprocal(out=sm3[:T, 3:4], in_=sm3[:T, 2:3]), reads=[b_sm3], writes=[b_sm3])
                        P.op("vector", lambda e, t=t, T=T: e.scalar_tensor_tensor(
                            out=mtok[:T, t, hh * 128:(hh + 1) * 128], in0=OP[:T, 256:384], scalar=sm3[:T, 3:4],
                            in1=sgg[:T, t, hh * 128:(hh + 1) * 128], op0=ALU.mult, op1=ALU.mult),
                            reads=[b_OP, b_sm3, b_sgg[t]], writes=[b_mtok[t]])
                    if hh == 3:
                        for t, (T, sid) in enumerate(tiles):
                            mix_transposes(t, T, 12)
                steps.append(dict(prep=prep, evac=ev, B=Bf, flush=False))
            return steps

        def out_steps(k0, tiles, dsts, nxt):
            steps = []
            nt = len(tiles)
            ef = []
            if nxt is not None:
                for t2, (src2, T2) in enumerate(nxt):
                    ef.append(lambda t2=t2, src2=src2, T2=T2: efront_load(t2, src2, T2))
                    ef.append(lambda t2=t2, T2=T2: efront_tr(t2, T2))
            for n in range(4):
                for t, (T, sid) in enumerate(tiles):
                    st_ = {}

                    def prep(n=n, t=t, T=T, st_=st_):
                        i = get_unit(k0 + n)
                        if t == 0:
                            get_unit(k0 + n + 1)
                        pj = st_["pj"] = next_pj()
                        return [(lambda kc=kc: P.op("tensor", lambda e: e.matmul(
                            PJ[pj][:T, :], lhsT=mixT[:, kc, t * 128:t * 128 + T], rhs=ub[i][:, kc, :],
                            start=(kc == 0), stop=(kc == 15)),
                            reads=[b_mixT[t], b_ub[i]], writes=[b_PJ[pj]], sig=(kc == 15))) for kc in range(16)]

                    def ev(n=n, t=t, T=T, st_=st_):
                        pj = st_["pj"]
                        P.op("vector", lambda e: e.tensor_tensor(out=x4[:T, t, n * 512:(n + 1) * 512], in0=PJ[pj][:T, :],
                                                                 in1=x4[:T, t, n * 512:(n + 1) * 512], op=ALU.add),
                             reads=[b_PJ[pj], b_x4[t]], writes=[b_x4[t]])
                        if ef:
                            ef.pop(0)()
                    steps.append(dict(prep=prep, evac=ev, B=None, flush=(n == 0 and t == 0)))

            def tail():
                while ef:
                    ef.pop(0)()
                if nxt is not None:
                    ust["preloaded"] = {0: load_unit(U_RG0), 1: load_unit(U_RG1)}
                    dma(ropeB[:], ust["next_rope"], [], [b_rope])
                for t, (T, sid) in enumerate(tiles):
                    P.op("scalar", lambda e, t=t, T=T: e.activation(out=junk[:T, :], in_=x4[:T, t, :], func=AF.Square, accum_out=ss4[:T, t:t + 1]),
                         reads=[b_x4[t]], writes=[b_junk, b_ss4])
                T0 = tiles[0][0]
                rstd_batch(ss4[:T0, :nt], rs4[:T0, :nt], nt, T0, [b_ss4], [b_rs4], scale=1.0 / D)
                for t, (T, sid) in enumerate(tiles):
                    P.op("vector", lambda e, t=t, T=T: e.scalar_tensor_tensor(out=x4[:T, t, :], in0=x4[:T, t, :], scalar=rs4[:T, t:t + 1],
                                                                              in1=gfin[:T, :], op0=ALU.mult, op1=ALU.mult),
                         reads=[b_x4[t], b_rs4] + CR, writes=[b_x4[t]])
                    dma(dsts[t], x4[:T, t, :], [b_x4[t]], [b_out])
            steps.append(dict(prep=lambda: [], evac=tail, B=None, flush=True))
            return steps

        def run_deferred(force):
            keep = []
            for item in ust["deferred"]:
                item[0] -= 1
                if force or item[0] <= 0:
                    item[1]()
                else:
                    keep.append(item)
            ust["deferred"] = keep

        def run_stream(steps):
            prevB = None
            n = len(steps)
            for idx, stp in enumerate(steps):
                if stp["flush"]:
                    ust["drip"] = None
                    if prevB is not None:
                        prevB()
                        prevB = None
                    run_deferred(True)
                if "mm" not in stp:
                    stp["mm"] = stp["prep"]()
                while stp["mm"]:
                    stp["mm"].pop(0)()
                if prevB is not None and "evl" in stp:
                    ust["evq"] = stp["evl"]()
                else:
                    ust["evq"] = None
                    stp["evac"]()
                if prevB is not None:
                    fut = []
                    for j in (idx + 1, idx + 2):
                        if j < n and not steps[j]["flush"]:
                            fut.append(steps[j])
                        else:
                            break
                    ust["drip"] = fut
                    prevB()
                    evdrip(100)
                    ust["evq"] = None
                    run_deferred(False)
                    ust["drip"] = None
                if ust["pre"]:
                    w2_step(ust["wrate"])
                prevB = stp["B"]
            if prevB is not None:
                prevB()
            run_deferred(True)

        def run_block(tiles, pre, ctxs, dsts, gv_dst, nxt=None):
            ust["loaded"] = dict(ust.pop("preloaded", {})) if not pre else {}
            ust["pre"] = pre
            steps = []
            if pre:
                ust["seq"] = list(range(8))
                for h in range(8):
                    steps += ret_steps(h, h, tiles, pre)
            else:
                ust["seq"] = [U_RG0, U_RG1] + list(range(8)) + [U_GG, U_GU, U_GV, U_AG, U_AQ] + [NU_IN + n for n in range(4)]
                steps += rg_steps(0, tiles)
                for h in range(8):
                    steps += ret_steps(2 + h, h, tiles, pre)
                steps += gmlp_steps(10, tiles, gv_dst)
                steps += xattn_steps(13, tiles, ctxs)
                steps += out_steps(15, tiles, dsts, nxt)
            run_stream(steps)

        def ctx_from_f32(cx, c, srck, srcv, rk, rv):
            evac(mkb[:, 0, :], srck, rk, [b_mkb], eng="vector")
            evac(mvc[cx][:, c, :], srcv, rv, [b_ctx[cx]], eng="gpsimd")
            ti = next_tr()
            for hh in range(4):
                P.op("tensor", lambda e, ti=ti, hh=hh: e.transpose(out=TR[ti][:, hh * 128:(hh + 1) * 128], in_=mkb[:, 0, hh * 128:(hh + 1) * 128],
                                                                  identity=ident[:, :]),
                     reads=[b_mkb] + CR, writes=[b_TR[ti]], sig=(hh == 3))
            evac(mkT[cx][:, :, c * 128:(c + 1) * 128], TR[ti][:, 0:512].rearrange("p (a b) -> p a b", b=128), [b_TR[ti]], [b_ctx[cx]])

        print('ops@states', P.nops)
        for h in range(8):
            P.op("vector", lambda e, h=h: e.memset(Sf[0][:, h, :], 0.0), writes=[b_Sf[0][h]])
            P.op("gpsimd", lambda e, h=h: e.memset(Sb[0][:, h, :], 0.0), writes=[b_Sb[0][h]])
        for s in range(2):
            dma(Sf[1 + s][:], st_s[s].rearrange("h d e -> d h e"), [], b_Sf[1 + s])
            for h in range(8):
                evac(Sb[1 + s][:, h, :], Sf[1 + s][:, h, :], [b_Sf[1 + s][h]], [b_Sb[1 + s][h]], eng="gpsimd")

        print('ops@pre', P.nops)
        blk = 0
        for pb in range(NB):
            dma(ropeB[:], rope_in[blk], [], [b_rope]); blk += 1
            w2_drain()
            front([(x_pre[pb * 512 + t * 128: pb * 512 + (t + 1) * 128, :], 128) for t in range(4)])
            if pb == 0:
                while wj["ns"] < 12:
                    w2_step(1)
            ust["wrate"] = 1
            run_block([(128, 0)] * 4, True, None, None, None)
        while wj["ns"] < NJ:
            w2_step(1)
            w2_drain()
        evs = []
        for q_ in range(4):
            evs += ([b_wq[q_].w] if b_wq[q_].w else []) + list(b_wq[q_].r)
        for t_ in range(4):
            b_mixT[t_].w = None
            b_mixT[t_].r = list(evs)
        print('ops@phaseM', P.nops)
        front([(mem[0:128, :], 128), (mem[128:256, :], 128)])
        print('ops@M-front-done', P.nops)
        ik = load_unit(NU_ALL)
        iv = load_unit(NU_ALL + 1)
        for c in range(2):
            pk = proj(c, 128, ik, 512)
            evac(mkf[:, 0, :], PJ[pk][:, :], [b_PJ[pk]], [b_mkf], eng="scalar")
            pv = proj(c, 128, iv, 512)
            evac(mkf[:, 1, :], PJ[pv][:, :], [b_PJ[pv]], [b_mkf], eng="vector")
            dma(mk_out[c * 128:(c + 1) * 128, :], mkf[:, 0, :], [b_mkf], [b_out])
            dma(mv_out[c * 128:(c + 1) * 128, :], mkf[:, 1, :], [b_mkf], [b_out])
            ctx_from_f32(0, c, mkf[:, 0, :], mkf[:, 1, :], [b_mkf], [b_mkf])
        print('ops@samplectx', P.nops)
        for s in range(2):
            for c in range(2):
                dma(mkf[:, 0, :], ck_s[s, c * 128:(c + 1) * 128, :], [], [b_mkf])
                dma(mkf[:, 1, :], cv_s[s, c * 128:(c + 1) * 128, :], [], [b_mkf])
                ctx_from_f32(1 + s, c, mkf[:, 0, :], mkf[:, 1, :], [b_mkf], [b_mkf])
        def main_tiles(mb):
            rows = [(mb * 512 + t * 128, mb * 512 + (t + 1) * 128) for t in range(4)]
            return rows, [(x_main[a_:b_, :], 128) for a_, b_ in rows]
        samp_tiles = [(x_s[0], 32), (x_s[1], 32)]
        for mb in range(NB):
            rows, xt_ = main_tiles(mb)
            if mb == 0:
                dma(ropeB[:], rope_in[blk], [], [b_rope])
            blk += 1
            front(xt_, only_x=(mb > 0))
            nxt = main_tiles(mb + 1)[1] if mb + 1 < NB else samp_tiles
            ust["next_rope"] = rope_in[blk]
            run_block([(128, 0)] * 4, False, [0] * 4, [y_main[a_:b_, :] for a_, b_ in rows], None, nxt=nxt)
        dma(sp_out.rearrange("h d e -> d h e"), Sf[0][:], b_Sf[0], [b_out])
        blk += 1
        front(samp_tiles, only_x=True)
        run_block([(32, 1), (32, 2)], False, [1, 2], [y_s[0], y_s[1]], [gv_out[0], gv_out[1]])
        for s_ in range(2):
            dma(ss_out[s_].rearrange("h d e -> d h e"), Sf[1 + s_][:], b_Sf[1 + s_], [b_out])
        P.wait_all("sync", [b_out])
        print("sbuf bytes remaining", nc.sbuf_bytes_remaining)
        print("planned ops", P.nops, {e: len(P.q[e]) for e in P.ENGS}, "sems", P.nsem)
        P.replay()
    return nc


def _consts(NB, half):
    h_idx = np.arange(8, dtype=np.float64)
    log_gamma = np.log(1.0 - 2.0 ** (-5.0 - h_idx))
    i = np.arange(128, dtype=np.float64)[:, None]
    dq = np.exp(log_gamma[None, :] * (i + 1.0))
    dk = np.exp(-log_gamma[None, :] * (i + 1.0)) * 128.0 ** -0.5
    gam128 = np.broadcast_to(np.exp(log_gamma * 128.0)[None, :], (128, 8))
    gam32 = np.broadcast_to(np.exp(log_gamma * 32.0)[None, :], (128, 8))
    ident = np.eye(128)
    jj = np.arange(128)[:, None]
    ii = np.arange(128)[None, :]
    mask = (ii >= jj).astype(np.float64)
    cst = np.concatenate([ident, mask, dq, dk, gam128, gam32, dk * gam128, dk * gam32, 1.0 / gam128, 1.0 / gam32], axis=1).astype(np.float32)
    dqT = np.ascontiguousarray(np.broadcast_to(dq.T[None, :, :], (128, 8, 128))).astype(np.float32)
    inv_freq = 10000.0 ** (-np.arange(0, 128, 2, dtype=np.float32) / 128.0)
    NTOK = NB * 512

    def tab(pos):
        ang = pos.astype(np.float32)[:, None] * inv_freq[None, :]
        c = np.cos(ang).astype(np.float32)
        s = np.sin(ang).astype(np.float32)
        return np.concatenate([c, c, -s, s], axis=1)

    blocks = []
    for pb in range(NB):
        pos = pb * 512 + np.arange(512)
        blocks.append(tab(pos).reshape(4, 128, 256).transpose(1, 0, 2))
    for mb in range(NB):
        pos = half * NTOK + mb * 512 + np.arange(512)
        blocks.append(tab(pos).reshape(4, 128, 256).transpose(1, 0, 2))
    ps = tab(1024 + np.arange(32))
    sblk = np.zeros((128, 4, 256), np.float32)
    sblk[:32, 0] = ps
    sblk[:32, 1] = ps
    blocks.append(sblk)
    rope = np.ascontiguousarray(np.stack(blocks, 0)).astype(np.float32)
    return cst, rope, dqT


def _unit_layout(w):
    return w.reshape(16, 128, w.shape[1]).transpose(1, 0, 2)


_CACHE = {}


def kernel(x_prompt, x_sample, mem_prompt, state_ret, cache_mem_k, cache_mem_v,
           g_norm, w_in, g_ret, g_gmlp, w_s, b_s, g_mem, w_mem_kv, w_out, g_final):
    f = np.float32
    x_prompt = np.asarray(x_prompt, f); x_sample = np.asarray(x_sample, f)
    Bp, L, _ = x_prompt.shape
    NTOK = L // 2
    NB = NTOK // 512
    if NB not in _CACHE:
        _CACHE[NB] = build_program(NB)
    nc = _CACHE[NB]
    w_in0 = np.asarray(w_in, f)[0]
    w_out0 = np.asarray(w_out, f)[0]
    wm0 = np.asarray(w_mem_kv, f)[0]
    units = []
    for h in range(8):
        cols = np.concatenate([np.arange(h * 128, (h + 1) * 128), np.arange(1024 + h * 128, 1024 + (h + 1) * 128),
                               np.arange(2048 + h * 128, 2048 + (h + 1) * 128), np.arange(h * 128, (h + 1) * 128)])
        units.append(_unit_layout(w_in0[:, cols]))
    for c0 in (3072, 3584, 5120, 4096, 4608, 6144, 5632):
        units.append(_unit_layout(w_in0[:, c0:c0 + 512]))
    for n in range(4):
        units.append(_unit_layout(w_out0[:, n * 512:(n + 1) * 512]))
    w_u = np.ascontiguousarray(np.stack(units, 0))
    wm_u = np.ascontiguousarray(np.stack([_unit_layout(wm0[:, 0:512]), _unit_layout(wm0[:, 512:1024])], 0))
    common = {
        "w_u": w_u, "wm_u": wm_u,
        "gnorm_pk": np.ascontiguousarray(np.asarray(g_norm, f)[0].reshape(16, 128).T),
        "gmem_pk": np.ascontiguousarray(np.asarray(g_mem, f)[0].reshape(16, 128).T),
        "gret_b": np.ascontiguousarray(np.broadcast_to(np.asarray(g_ret, f)[0][None, :], (128, 1024))),
        "ggm_b": np.ascontiguousarray(np.broadcast_to(np.asarray(g_gmlp, f)[0][None, :], (128, 512))),
        "gfin_b": np.ascontiguousarray(np.broadcast_to(np.asarray(g_final, f)[None, :], (128, D))),
        "wsT": np.ascontiguousarray(np.asarray(w_s, f)[0].transpose(2, 0, 1)),
        "bs_t": np.ascontiguousarray(np.asarray(b_s, f)[0].T),
    }
    sr = np.asarray(state_ret, f)[0]
    ck = np.asarray(cache_mem_k, f)[0].reshape(16, 256, 512)
    cv = np.asarray(cache_mem_v, f)[0].reshape(16, 256, 512)
    mp = np.asarray(mem_prompt, f)
    in_maps = []
    for c in range(8):
        b, half = c // 2, c % 2
        cst, rope, dqT_c = _consts(NB, half)
        m = dict(common)
        m["x_main"] = np.ascontiguousarray(x_prompt[b, half * NTOK:(half + 1) * NTOK])
        m["x_pre"] = np.ascontiguousarray(x_prompt[b, 0:NTOK]) if half == 1 else np.zeros((NTOK, D), f)
        m["x_s"] = np.ascontiguousarray(x_sample[2 * c:2 * c + 2])
        m["mem"] = np.ascontiguousarray(mp[b])
        m["st_s"] = np.ascontiguousarray(sr[2 * c:2 * c + 2])
        m["ck_s"] = np.ascontiguousarray(ck[2 * c:2 * c + 2])
        m["cv_s"] = np.ascontiguousarray(cv[2 * c:2 * c + 2])
        m["cst"] = cst
        m["dqT"] = dqT_c
        m["rope"] = rope
        in_maps.append(m)
    if KCORES < 8:
        res = run_bass_kernel_spmd(nc, in_maps[:KCORES], core_ids=list(range(KCORES)))
        R = list(res.results) * 8
    else:
        res = run_bass_kernel_spmd(nc, in_maps, core_ids=list(range(8)))
        R = res.results
    y_prompt = np.stack([np.concatenate([R[2 * b]["y_main"], R[2 * b + 1]["y_main"]], 0) for b in range(Bp)], 0)
    y_sample = np.concatenate([R[c]["y_s"] for c in range(8)], 0)
    sp = np.stack([R[2 * b + 1]["sp_out"] for b in range(Bp)], 0)[None]
    mk = np.stack([R[2 * b]["mk_out"].reshape(256, 4, 128) for b in range(Bp)], 0)[None]
    mv = np.stack([R[2 * b]["mv_out"].reshape(256, 4, 128) for b in range(Bp)], 0)[None]
    ss = np.concatenate([R[c]["ss_out"] for c in range(8)], 0)[None]
    gv = np.concatenate([R[c]["gv_out"] for c in range(8)], 0)[None]
    return (y_prompt.astype(f), y_sample.astype(f), sp.astype(f), mk.astype(f), mv.astype(f), ss.astype(f), gv.astype(f))
```

```python
import numpy as np
import concourse.bass as bass
import concourse.mybir as mybir
from concourse.bass_utils import run_bass_kernel_spmd
from contextlib import ExitStack

F32 = mybir.dt.float32
BF16 = mybir.dt.bfloat16
ALU = mybir.AluOpType
AF = mybir.ActivationFunctionType
AX = mybir.AxisListType

import os
KSTOP = int(os.environ.get("KSTOP", "100000000"))
KCORES = int(os.environ.get("KCORES", "8"))
SEM_LIMIT = 30000
D = 2048
EPS = 1e-6
NU_IN = 15
NU_ALL = 19
U_RG0, U_RG1, U_GG, U_GU, U_GV, U_AG, U_AQ = 8, 9, 10, 11, 12, 13, 14


class Sem:
    def __init__(self, h):
        self.h = h
        self.n = 0


class Buf:
    __slots__ = ("name", "w", "r", "dsem", "excl")

    def __init__(self, name, excl=False):
        self.excl = excl
        self.name = name
        self.w = None
        self.r = []
        self.dsem = None


class Prog:
    ENGS = ("sync", "gpsimd", "scalar", "vector", "tensor")

    def __init__(self, nc, stack):
        self.nc = nc
        self.stack = stack
        self.q = {e: [] for e in self.ENGS}
        self.esem = {}
        self.known = {e: {} for e in self.ENGS}
        self.nsem = 0
        for e in self.ENGS:
            self.esem[e] = self.new_sem("e_" + e)

    def new_sem(self, name):
        self.nsem += 1
        h = self.stack.enter_context(self.nc.semaphore(f"{name}_{self.nsem}"))
        return Sem(h)

    def op(self, eng, fn, reads=(), writes=(), dma=False, sig=True):
        self.nops = getattr(self, "nops", 0) + 1
        if self.nops > KSTOP:
            return
        waits = {}
        kn = self.known[eng]
        pe_sem = self.esem["tensor"]

        def need(ev):
            if ev is None:
                return
            s, v = ev
            if eng == "tensor" and s is pe_sem:
                return
            if kn.get(s, 0) >= v:
                return
            if waits.get(s, 0) < v:
                waits[s] = v

        own = self.esem[eng]
        for b in reads:
            need(b.w)
            if b.excl:
                for ev in b.r:
                    if ev[0] is not own:
                        need(ev)
        for b in writes:
            need(b.w)
            for ev in b.r:
                need(ev)
        for s, v in waits.items():
            kn[s] = v
        if dma:
            tgt = writes[0] if writes else reads[0]
            if tgt.dsem is None or tgt.dsem.n + 16 > SEM_LIMIT:
                tgt.dsem = self.new_sem("d")
            dsem = tgt.dsem
            dsem.n += 16
            ev = (dsem, dsem.n)
            inc = (dsem, 16)
        else:
            s = self.esem[eng]
            if sig:
                s.n += 1
                ev = (s, s.n)
                inc = (s, 1)
                if s.n >= SEM_LIMIT:
                    self.esem[eng] = self.new_sem("e_" + eng)
            else:
                ev = (s, s.n + 1)
                inc = None
        for b in reads:
            b.r.append(ev)
        for b in writes:
            b.w = ev
            b.r = []
        self.q[eng].append((list(waits.items()), fn, inc))

    def wait_all(self, eng, bufs):
        waits = {}
        for b in bufs:
            for ev in ([b.w] if b.w else []) + list(b.r):
                s, v = ev
                if waits.get(s, 0) < v:
                    waits[s] = v
        self.q[eng].append((list(waits.items()), None, None))

    def replay(self):
        nc = self.nc
        with nc.Block() as block:
            def mk(ename):
                def body(e):
                    for waits, fn, inc in self.q[ename]:
                        for s, v in waits:
                            e.wait_ge(s.h, v)
                        if fn is None:
                            continue
                        ins = fn(e)
                        if inc is not None:
                            ins.then_inc(inc[0].h, inc[1])
                return body
            block.sync(mk("sync"))
            block.gpsimd(mk("gpsimd"))
            block.scalar(mk("scalar"))
            block.vector(mk("vector"))
            block.tensor(mk("tensor"))


def build_program(NB):
    nc = bass.Bass("TRN2", target_bir_lowering=False)
    NTOK = NB * 512
    NBLK = 2 * NB + 1

    def din(name, shape, dt=F32):
        return nc.dram_tensor(name, shape, dt, kind="ExternalInput").ap()

    def dout(name, shape):
        return nc.dram_tensor(name, shape, F32, kind="ExternalOutput").ap()

    x_main = din("x_main", [NTOK, D])
    x_pre = din("x_pre", [NTOK, D])
    x_s = din("x_s", [2, 32, D])
    mem = din("mem", [256, D])
    st_s = din("st_s", [2, 8, 128, 128])
    ck_s = din("ck_s", [2, 256, 512])
    cv_s = din("cv_s", [2, 256, 512])
    w_u = din("w_u", [NU_ALL, 128, 16, 512])
    wm_u = din("wm_u", [2, 128, 16, 512])
    gnorm_pk = din("gnorm_pk", [128, 16])
    gmem_pk = din("gmem_pk", [128, 16])
    gret_b = din("gret_b", [128, 1024])
    ggm_b = din("ggm_b", [128, 512])
    gfin_b = din("gfin_b", [128, D])
    wsT_in = din("wsT", [128, 4, 128])
    bs_in = din("bs_t", [128, 4])
    rope_in = din("rope", [NBLK, 128, 4, 256])
    cst_in = din("cst", [128, 320])
    dqT_in = din("dqT", [128, 8, 128])

    y_main = dout("y_main", [NTOK, D])
    y_s = dout("y_s", [2, 32, D])
    sp_out = dout("sp_out", [8, 128, 128])
    mk_out = dout("mk_out", [256, 512])
    mv_out = dout("mv_out", [256, 512])
    ss_out = dout("ss_out", [2, 8, 128, 128])
    gv_out = dout("gv_out", [2, 32, 512])

    scr = nc.dram_tensor("scr_w", [NU_ALL + 2, 128, 16, 512], BF16, kind="Internal").ap()

    with ExitStack() as st:
        P = Prog(nc, st)

        def sb(name, shape, dt):
            return st.enter_context(nc.sbuf_tensor(name, shape, dt))

        def ps(name, shape, dt):
            return st.enter_context(nc.psum_tensor(name, shape, dt))

        def B(name):
            return Buf(name)

        hT = sb("hT", [128, 16, 512], BF16); b_hT = [B(f"hT{t}") for t in range(4)]
        mixT = sb("mixT", [128, 16, 512], BF16); b_mixT = [B(f"mixT{t}") for t in range(4)]
        ub = [sb(f"ub{i}", [128, 16, 512], BF16) for i in range(2)]; b_ub = [B(f"ub{i}") for i in range(2)]
        x4 = sb("x4", [128, 4, D], F32); b_x4 = [B(f"x4_{t}") for t in range(4)]
        xb = sb("xb", [128, D], BF16); b_xb = B("xb")
        junk = sb("junk", [128, D], BF16); b_junk = B("junk")
        sgg = sb("sgg", [128, 4, 512], F32); b_sgg = [B(f"sgg{t}") for t in range(4)]
        gvs = [sb(f"gvs{i}", [128, 512], F32) for i in range(2)]; b_gvs = [B(f"gvs{i}") for i in range(2)]
        gvb = sb("gvb", [128, 512], BF16); b_gvb = B("gvb")
        sgr = sb("sgr", [128, 4, 1024], BF16); b_sgr = [B(f"sgr{t}") for t in range(4)]
        mtok = sb("mtok", [128, 4, 512], BF16); b_mtok = [B(f"mtok{t}") for t in range(4)]
        ropeB = sb("ropeB", [128, 4, 256], F32); b_rope = B("rope")
        gret = sb("gret", [128, 1024], F32)
        ggm = sb("ggm", [128, 512], F32)
        gfin = sb("gfin", [128, D], F32)
        gnp = sb("gnp", [128, 16], F32)
        gmp = sb("gmp", [128, 16], F32)
        b_cst = B("cst")
        cst = sb("cstt", [128, 320], F32)
        identb = sb("identb", [128, 128], BF16)
        wsT_f = sb("wsT_f", [128, 4, 128], F32)
        wsT = sb("wsTb", [128, 4, 128], BF16)
        bst = sb("bs_sb", [128, 4], F32)
        mkT = [sb(f"mkT{i}", [128, 4, 256], BF16) for i in range(3)]
        mvc = [sb(f"mvc{i}", [128, 2, 512], BF16) for i in range(3)]
        b_ctx = [B(f"ctx{i}") for i in range(3)]
        Sf = [sb(f"Sf{i}", [128, 8, 128], F32) for i in range(3)]
        Sb = [sb(f"Sb{i}", [128, 8, 128], BF16) for i in range(3)]
        b_Sf = [[B(f"Sf{i}_{h}") for h in range(8)] for i in range(3)]
        b_Sb = [[B(f"Sb{i}_{h}") for h in range(8)] for i in range(3)]
        r1 = sb("r1", [128, 2, 128], F32); b_r1 = B("r1")
        r2 = sb("r2", [128, 2, 128], F32); b_r2 = B("r2")
        qk = [sb(f"qk{i}", [128, 2, 128], BF16) for i in range(2)]; b_qk = [B(f"qk{i}") for i in range(2)]
        vd = [sb(f"vd{i}", [128, 128], BF16) for i in range(2)]; b_vd = [B(f"vd{i}") for i in range(2)]
        dqT = sb("dqT_sb", [128, 8, 128], F32)
        qkT = sb("qkT", [128, 2, 128], BF16); b_qkT = B("qkT")
        scm = sb("scm", [128, 128], BF16); b_scm = B("scm")
        sg4 = [sb(f"sg4_{i}", [128, 4, 128], F32) for i in range(2)]; b_sg4 = [[B(f"sg4_{i}_{t}") for t in range(4)] for i in range(2)]
        os4 = [sb(f"os4_{i}", [128, 4, 128], F32) for i in range(2)]; b_os4 = [[B(f"os4_{i}_{t}") for t in range(4)] for i in range(2)]
        bn6 = sb("bn6", [128, 6], F32); b_bn6 = B("bn6")
        mv4 = [sb(f"mv4_{i}", [128, 6, 4], F32) for i in range(2)]; b_mv4 = [B(f"mv4_{i}") for i in range(2)]
        mvg = sb("mvg", [128, 2], F32); b_mvg = B("mvg")
        rs4 = sb("rs4", [128, 4], F32); b_rs4 = B("rs4")
        sq4 = sb("sq4", [128, 4], F32); b_sq4 = B("sq4")
        ss4 = sb("ss4", [128, 4], F32); b_ss4 = B("ss4")
        eps_t = sb("eps_t", [128, 1], F32)
        aqT = [sb(f"aqT{i}", [128, 512], BF16) for i in range(2)]; b_aqT = [B(f"aqT{i}") for i in range(2)]
        pexp = sb("pexp", [128, 256], BF16); b_pexp = B("pexp")
        pT = sb("pT", [128, 2, 128], BF16); b_pT = B("pT")
        sm3 = sb("sm3", [128, 4], F32); b_sm3 = B("sm3")
        sm3b = sb("sm3b", [128, 4], F32); b_sm3b = B("sm3b")
        mkf = sb("mkf", [128, 2, 512], F32); b_mkf = B("mkf")
        mkb = sb("mkb", [128, 2, 512], BF16); b_mkb = B("mkb")
        PJ = [ps(f"PJ{i}", [128, 512], F32) for i in range(4)]; b_PJ = [Buf(f"PJ{i}", True) for i in range(4)]
        TR = [ps(f"TR{i}", [128, 1024], BF16) for i in range(2)]; b_TR = [Buf(f"TR{i}", True) for i in range(2)]
        SO = ps("SO", [128, 512], F32); b_SO = Buf("SO", True)
        SC = SO; b_SC = b_SO
        OP = SO; b_OP = b_SO
        STP = ps("STP", [128, 512], F32); b_STP = Buf("STP", True)

        ctr = {"pj": 0, "tr": 0, "ub": 0, "ev": 0}
        b_scr = [B(f"scr{u}") for u in range(NU_ALL + 2)]
        b_out = B("out")

        ident = identb
        mask = cst[:, 128:256]
        dq = cst[:, 256:264]
        dk = cst[:, 264:272]
        gam = {128: cst[:, 272:280], 32: cst[:, 280:288]}
        dkg = {128: cst[:, 288:296], 32: cst[:, 296:304]}
        ginv = {128: cst[:, 304:312], 32: cst[:, 312:320]}

        def dma(out, in_, reads, writes):
            P.op("sync", lambda e: e.dma_start(out=out, in_=in_), reads=reads, writes=writes, dma=True)

        def evac(out, in_, reads, writes, eng=None):
            if eng is None:
                ctr["ev"] += 1
                eng = "scalar" if ctr["ev"] % 2 else "vector"
            if eng == "scalar":
                P.op("scalar", lambda e: e.activation(out=out, in_=in_, func=AF.Copy), reads=reads, writes=writes)
            else:
                P.op(eng, lambda e: e.tensor_copy(out=out, in_=in_), reads=reads, writes=writes)

        dma(cst[:], cst_in, [], [b_cst])
        dma(dqT[:], dqT_in, [], [b_cst])
        dma(gret[:], gret_b, [], [b_cst])
        dma(ggm[:], ggm_b, [], [b_cst])
        dma(gfin[:], gfin_b, [], [b_cst])
        dma(gnp[:], gnorm_pk, [], [b_cst])
        dma(gmp[:], gmem_pk, [], [b_cst])
        dma(wsT_f[:], wsT_in, [], [b_cst])
        dma(bst[:], bs_in, [], [b_cst])
        b_c2 = B("c2")
        P.op("vector", lambda e: e.tensor_copy(out=identb[:], in_=cst[:, 0:128]), reads=[b_cst], writes=[b_c2])
        P.op("vector", lambda e: e.memset(eps_t[:], EPS), writes=[b_c2])
        for g in range(4):
            P.op("vector", lambda e, g=g: e.tensor_tensor(out=wsT[:, g, :], in0=wsT_f[:, g, :], in1=mask, op=ALU.mult),
                 reads=[b_cst], writes=[b_c2])
        CR = [b_cst, b_c2]

        cv_eng = ["vector", "scalar", "vector"]
        cvs = {"i": 0}

        def conv_quarter(u, qd, stage, b_stage, dst, b_dst, engs):
            src = w_u[u] if u < NU_ALL else wm_u[u - NU_ALL]
            gsc = gnp if u < NU_IN else (gmp if u >= NU_ALL else None)
            dma(stage, src[:, qd * 4:(qd + 1) * 4, :], [], [b_stage])
            for a in range(4):
                kc = qd * 4 + a
                eng = engs[cvs["i"] % len(engs)]; cvs["i"] += 1
                if gsc is None:
                    evac(dst[:, kc, :], stage[:, a, :], [b_stage], [b_dst], eng=eng)
                elif eng == "scalar":
                    P.op("scalar", lambda e, kc=kc, a=a: e.activation(
                        out=dst[:, kc, :], in_=stage[:, a, :], func=AF.Copy, scale=gsc[:, kc:kc + 1]),
                        reads=[b_stage] + CR, writes=[b_dst])
                else:
                    P.op(eng, lambda e, kc=kc, a=a: e.tensor_scalar(
                        out=dst[:, kc, :], in0=stage[:, a, :], scalar1=gsc[:, kc:kc + 1], scalar2=None, op0=ALU.mult),
                        reads=[b_stage] + CR, writes=[b_dst])

        b_wq = [B(f"wq{i}") for i in range(4)]
        w2_jobs = []
        for u in list(range(8)) + [NU_ALL, NU_ALL + 1] + list(range(8, NU_ALL)):
            for qd in range(4):
                w2_jobs.append((u, qd))

        jobs = list(w2_jobs)
        NJ = len(jobs)
        b_scrq = {}
        slot_sem = [P.new_sem(f"wst{i}") for i in range(4)]
        for j_, (u_, q_) in enumerate(jobs):
            b_scrq[(u_, q_)] = B(f"scr{u_}_{q_}")
            b_scrq[(u_, q_)].dsem = slot_sem[j_ % 4]

        def j_load(j):
            if not (0 <= j < NJ):
                return
            u, qd = jobs[j]
            slot = j % 4
            stage = x4[:, slot, :].rearrange("p (a b) -> p a b", b=512)
            src = w_u[u] if u < NU_ALL else wm_u[u - NU_ALL]
            dma(stage, src[:, qd * 4:(qd + 1) * 4, :], [], [b_x4[slot]])

        def j_conv(j):
            if not (0 <= j < NJ):
                return
            u, qd = jobs[j]
            slot = j % 4
            stage = x4[:, slot, :].rearrange("p (a b) -> p a b", b=512)
            dstq = mixT[:, slot * 4:(slot + 1) * 4, :]
            gsc = gnp if u < NU_IN else (gmp if u >= NU_ALL else None)
            for a_ in range(4):
                eng = cv_eng[cvs["i"] % 3]; cvs["i"] += 1
                if gsc is None:
                    evac(dstq[:, a_, :], stage[:, a_, :], [b_x4[slot]], [b_wq[slot]], eng=eng)
                elif eng == "scalar":
                    P.op("scalar", lambda e, a_=a_: e.activation(out=dstq[:, a_, :], in_=stage[:, a_, :], func=AF.Copy,
                                                                 scale=gsc[:, qd * 4 + a_:qd * 4 + a_ + 1]),
                         reads=[b_x4[slot]] + CR, writes=[b_wq[slot]])
                else:
                    P.op(eng, lambda e, a_=a_: e.tensor_scalar(out=dstq[:, a_, :], in0=stage[:, a_, :],
                                                                scalar1=gsc[:, qd * 4 + a_:qd * 4 + a_ + 1], scalar2=None, op0=ALU.mult),
                         reads=[b_x4[slot]] + CR, writes=[b_wq[slot]])

        def j_store(j):
            if not (0 <= j < NJ):
                return
            u, qd = jobs[j]
            slot = j % 4
            dstq = mixT[:, slot * 4:(slot + 1) * 4, :]
            dma(scr[u][:, qd * 4:(qd + 1) * 4, :], dstq, [b_wq[slot]], [b_scrq[(u, qd)]])

        wj = {"nl": 0, "nc": 0, "ns": 0}

        def w2_step(n=1):
            for _ in range(n):
                while wj["nl"] < NJ and wj["nl"] < wj["nc"] + 3:
                    j_load(wj["nl"]); wj["nl"] += 1
                if wj["ns"] < wj["nc"]:
                    j_store(wj["ns"]); wj["ns"] += 1
                if wj["nc"] < wj["nl"]:
                    j_conv(wj["nc"]); wj["nc"] += 1

        def w2_drain():
            while wj["nc"] < wj["nl"]:
                j_conv(wj["nc"]); wj["nc"] += 1
            while wj["ns"] < wj["nc"]:
                j_store(wj["ns"]); wj["ns"] += 1

        def load_unit(u, ncols=512):
            i = ctr["ub"] % 2; ctr["ub"] += 1
            rd = [b_scrq[(u, q_)] for q_ in range(4)]
            assert all(b_.w is not None for b_ in rd), ("unit loaded before converted", u)
            if ncols == 512:
                dma(ub[i][:], scr[u], rd, [b_ub[i]])
            elif ncols == 256:
                dma(ub[i][:, :, 128:384], scr[u][:, :, 128:384], rd, [b_ub[i]])
            else:
                dma(ub[i][:, :, 0:ncols], scr[u][:, :, 0:ncols], rd, [b_ub[i]])
            return i

        def next_pj():
            i = ctr["pj"] % 4; ctr["pj"] += 1
            return i

        def next_tr():
            i = ctr["tr"] % 2; ctr["tr"] += 1
            return i

        def rstd_batch(src, dst, n, T, reads_b, writes_b, scale=None):
            if scale is None:
                P.op("scalar", lambda e: e.activation(out=sq4[:T, :n], in_=src, func=AF.Sqrt, bias=eps_t[:T, 0:1], scale=1.0),
                     reads=reads_b + CR, writes=[b_sq4])
            else:
                P.op("scalar", lambda e: e.activation(out=sq4[:T, :n], in_=src, func=AF.Sqrt, bias=eps_t[:T, 0:1], scale=scale),
                     reads=reads_b + CR, writes=[b_sq4])
            P.op("vector", lambda e: e.reciprocal(out=dst, in_=sq4[:T, :n]), reads=[b_sq4], writes=writes_b)

        def front(tiles, only_x=False):
            nt = len(tiles)
            if only_x:
                for t, (src, T) in enumerate(tiles):
                    dma(x4[:T, t, :], src, [], [b_x4[t]])
                return
            for t, (src, T) in enumerate(tiles):
                dma(x4[:T, t, :], src, [], [b_x4[t]])
                P.op("scalar", lambda e, t=t, T=T: e.activation(out=junk[:T, :], in_=x4[:T, t, :], func=AF.Square,
                                                               accum_out=ss4[:T, t:t + 1]),
                     reads=[b_x4[t]], writes=[b_junk, b_ss4])
            T0 = tiles[0][1]
            rstd_batch(ss4[:T0, :nt], rs4[:T0, :nt], nt, T0, [b_ss4], [b_rs4], scale=1.0 / D)
            for t, (src, T) in enumerate(tiles):
                P.op("vector", lambda e, t=t, T=T: e.tensor_scalar(out=xb[:T, :], in0=x4[:T, t, :], scalar1=rs4[:T, t:t + 1],
                                                                   scalar2=None, op0=ALU.mult),
                     reads=[b_x4[t], b_rs4], writes=[b_xb])
                for grp in range(2):
                    ti = next_tr()
                    for j in range(8):
                        kc = grp * 8 + j
                        P.op("tensor", lambda e, ti=ti, j=j, kc=kc, T=T: e.transpose(
                            out=TR[ti][:, j * 128:j * 128 + T], in_=xb[:T, kc * 128:(kc + 1) * 128], identity=ident[:T, :T]),
                            reads=[b_xb] + CR, writes=[b_TR[ti]], sig=(j == 7))
                    evac(hT[:, grp * 8:(grp + 1) * 8, t * 128:t * 128 + T],
                         TR[ti][:].rearrange("p (a b) -> p a b", b=128)[:, :, :T], [b_TR[ti]], [b_hT[t]])

        ssE = sb("ssE", [128, 4], F32); b_ssE = B("ssE")
        rsE = sb("rsE", [128, 4], F32); b_rsE = B("rsE")
        sgg_flat = sgg[:].rearrange("p a b -> p (a b)")

        def efront_load(t, src, T):
            dma(sgg_flat[:T, :], src, [], b_sgg)
            P.op("scalar", lambda e: e.activation(out=junk[:T, :], in_=sgg_flat[:T, :], func=AF.Square, accum_out=ssE[:T, t:t + 1]),
                 reads=b_sgg, writes=[b_junk, b_ssE])
            P.op("scalar", lambda e: e.activation(out=sq4[:T, 0:1], in_=ssE[:T, t:t + 1], func=AF.Sqrt, bias=eps_t[:T, 0:1], scale=1.0 / D),
                 reads=[b_ssE] + CR, writes=[b_sq4])
            P.op("vector", lambda e: e.reciprocal(out=rsE[:T, t:t + 1], in_=sq4[:T, 0:1]), reads=[b_sq4], writes=[b_rsE])
            P.op("vector", lambda e: e.tensor_scalar(out=xb[:T, :], in0=sgg_flat[:T, :], scalar1=rsE[:T, t:t + 1], scalar2=None, op0=ALU.mult),
                 reads=b_sgg + [b_rsE], writes=[b_xb])

        def efront_tr(t, T):
            for grp in range(2):
                ti = next_tr()
                for j in range(8):
                    kc = grp * 8 + j
                    P.op("tensor", lambda e, ti=ti, j=j, kc=kc: e.transpose(
                        out=TR[ti][:, j * 128:j * 128 + T], in_=xb[:T, kc * 128:(kc + 1) * 128], identity=ident[:T, :T]),
                        reads=[b_xb] + CR, writes=[b_TR[ti]], sig=(j == 7))
                evac(hT[:, grp * 8:(grp + 1) * 8, t * 128:t * 128 + T],
                     TR[ti][:].rearrange("p (a b) -> p a b", b=128)[:, :, :T], [b_TR[ti]], [b_hT[t]])

        def proj(t, T, i, ncols, c0=0):
            pj = next_pj()
            for kc in range(16):
                P.op("tensor", lambda e, kc=kc, pj=pj, t=t, T=T, i=i: e.matmul(
                    PJ[pj][:T, :ncols], lhsT=hT[:, kc, t * 128:t * 128 + T], rhs=ub[i][:, kc, c0:c0 + ncols],
                    start=(kc == 0), stop=(kc == 15)),
                    reads=[b_hT[t], b_ub[i]], writes=[b_PJ[pj]], sig=(kc == 15))
            return pj

        def rope2(pj, ns, T, t, p):
            X = PJ[pj][:T, 0:ns * 128].rearrange("p (s d) -> p s d", d=128)
            A = ropeB[:T, t, 0:128].unsqueeze(1).broadcast_to([T, ns, 128])
            B1 = ropeB[:T, t, 128:192].unsqueeze(1).broadcast_to([T, ns, 64])
            B2 = ropeB[:T, t, 192:256].unsqueeze(1).broadcast_to([T, ns, 64])
            return [
                lambda: P.op("vector", lambda e: e.tensor_tensor(out=r1[:T, 0:ns, :], in0=X, in1=A, op=ALU.mult),
                             reads=[b_PJ[pj], b_rope], writes=[b_r1]),
                lambda: P.op("vector", lambda e: e.tensor_tensor(out=r2[:T, 0:ns, 0:64], in0=X[:, :, 64:128], in1=B1, op=ALU.mult),
                             reads=[b_PJ[pj], b_rope], writes=[b_r2]),
                lambda: P.op("vector", lambda e: e.tensor_tensor(out=r2[:T, 0:ns, 64:128], in0=X[:, :, 0:64], in1=B2, op=ALU.mult),
                             reads=[b_PJ[pj], b_rope], writes=[b_r2]),
                lambda: P.op("vector", lambda e: e.tensor_tensor(out=qk[p][:T, 2 - ns:2, :], in0=r1[:T, 0:ns, :], in1=r2[:T, 0:ns, :], op=ALU.add),
                             reads=[b_r1, b_r2], writes=[b_qk[p]]),
            ]

        ust = {"loaded": {}, "seq": [], "stepno": 0, "drip": None, "pre": False, "deferred": [], "wrate": 1}

        def get_unit(k):
            if k >= len(ust["seq"]):
                return None
            if k not in ust["loaded"]:
                u = ust["seq"][k]
                ust["loaded"][k] = load_unit(u, (256 if ust["pre"] else 384) if u < 8 else 512)
            return ust["loaded"][k]

        def drip(n):
            for d in (ust["drip"] or []):
                if "mm" not in d:
                    d["mm"] = d["prep"]()
                while n > 0 and d["mm"]:
                    d["mm"].pop(0)()
                    n -= 1
                if n == 0:
                    return

        def evdrip(n):
            q_ = ust.get("evq")
            while q_ and n > 0:
                q_.pop(0)()
                n -= 1

        def proj_mm(t, T, i, ncols, pj, c0=0):
            return [(lambda kc=kc: P.op("tensor", lambda e: e.matmul(
                PJ[pj][:T, :ncols], lhsT=hT[:, kc, t * 128:t * 128 + T], rhs=ub[i][:, kc, c0:c0 + ncols],
                start=(kc == 0), stop=(kc == 15)),
                reads=[b_hT[t], b_ub[i]], writes=[b_PJ[pj]], sig=(kc == 15))) for kc in range(16)]

        def ret_steps(k, h, tiles, pre):
            steps = []
            nt = len(tiles)
            up = h % 2
            for t, (T, sid) in enumerate(tiles):
                st_ = {}

                def prep(t=t, T=T, st_=st_):
                    i = get_unit(k)
                    if t == 0:
                        get_unit(k + 1)
                    st_["p"] = ust["stepno"] % 2; ust["stepno"] += 1
                    st_["pj"] = next_pj()
                    return proj_mm(t, T, i, 256 if pre else 384, st_["pj"], c0=(128 if pre else 0))

                def evl(t=t, T=T, st_=st_):
                    p, pj = st_["p"], st_["pj"]
                    vc = 128 if pre else 256
                    return [lambda: P.op("scalar", lambda e: e.activation(out=vd[p][:T, :], in_=PJ[pj][:T, vc:vc + 128], func=AF.Copy,
                                                                          scale=dkg[T][:T, h:h + 1]),
                                         reads=[b_PJ[pj]] + CR, writes=[b_vd[p]])] + rope2(pj, 1 if pre else 2, T, t, p)

                def ev(evl=evl):
                    for f_ in evl():
                        f_()

                def tail():
                    T0 = tiles[0][0]
                    M = mv4[up]
                    P.op("vector", lambda e: e.tensor_scalar(out=M[:T0, 2, :nt], in0=M[:T0, 0, :nt], scalar1=1.0 / 128, scalar2=None, op0=ALU.mult),
                         reads=[b_mv4[up]], writes=[b_mv4[up]])
                    P.op("vector", lambda e: e.tensor_tensor(out=M[:T0, 3, :nt], in0=M[:T0, 2, :nt], in1=M[:T0, 2, :nt], op=ALU.mult),
                         reads=[b_mv4[up]], writes=[b_mv4[up]])
                    P.op("vector", lambda e: e.scalar_tensor_tensor(out=M[:T0, 3, :nt], in0=M[:T0, 1, :nt], scalar=1.0 / 128, in1=M[:T0, 3, :nt],
                                                                    op0=ALU.mult, op1=ALU.subtract),
                         reads=[b_mv4[up]], writes=[b_mv4[up]])
                    rstd_batch(M[:T0, 3, :nt], M[:T0, 4, :nt], nt, T0, [b_mv4[up]], [b_mv4[up]])
                    P.op("vector", lambda e: e.scalar_tensor_tensor(out=M[:T0, 5, :nt], in0=M[:T0, 2, :nt], scalar=-1.0, in1=M[:T0, 4, :nt],
                                                                    op0=ALU.mult, op1=ALU.mult),
                         reads=[b_mv4[up]], writes=[b_mv4[up]])
                    for t2, (T2, sid2) in enumerate(tiles):
                        P.op("scalar", lambda e, t2=t2, T2=T2: e.activation(out=os4[up][:T2, t2, :], in_=os4[up][:T2, t2, :], func=AF.Identity,
                                                                            bias=M[:T2, 5, t2:t2 + 1], scale=M[:T2, 4, t2:t2 + 1]),
                             reads=[b_os4[up][t2], b_mv4[up]], writes=[b_os4[up][t2]])
                        P.op("gpsimd", lambda e, t2=t2, T2=T2: e.tensor_tensor(out=sg4[up][:T2, t2, :], in0=sgr[:T2, t2, h * 128:(h + 1) * 128],
                                                                               in1=gret[:T2, h * 128:(h + 1) * 128], op=ALU.mult),
                             reads=[b_sgr[t2]] + CR, writes=[b_sg4[up][t2]])
                        P.op("vector", lambda e, t2=t2, T2=T2: e.tensor_tensor(out=mtok[:T2, t2, 0:128], in0=os4[up][:T2, t2, :], in1=sg4[up][:T2, t2, :],
                                                                               op=ALU.mult),
                             reads=[b_os4[up][t2], b_sg4[up][t2]], writes=[b_mtok[t2]])
                    for t2, (T2, sid2) in enumerate(tiles):
                        ti = 1
                        P.op("tensor", lambda e, ti=ti, t2=t2, T2=T2: e.transpose(out=TR[ti][:, t2 * 128:t2 * 128 + T2], in_=mtok[:T2, t2, 0:128], identity=ident[:T2, :T2]),
                             reads=[b_mtok[t2]] + CR, writes=[b_TR[ti]])
                        evac(mixT[:, h, t2 * 128:t2 * 128 + T2], TR[ti][:, t2 * 128:t2 * 128 + T2], [b_TR[ti]], [b_mixT[t2]])

                def Bf(t=t, T=T, sid=sid, st_=st_):
                    p = st_["p"]
                    if not pre:
                        ti = 0
                        for s_ in range(2):
                            P.op("tensor", lambda e, s_=s_: e.transpose(out=TR[ti][:, s_ * 128:s_ * 128 + T], in_=qk[p][:T, s_, :],
                                                                         identity=ident[:T, :T]),
                                 reads=[b_qk[p]] + CR, writes=[b_TR[ti]], sig=(s_ == 1))
                        evac(qkT[:, 1, :T], TR[ti][:, 128:128 + T], [b_TR[ti]], [b_qkT], eng="scalar")
                        P.op("vector", lambda e: e.tensor_tensor(out=qkT[:, 0, :T], in0=TR[ti][:, 0:T], in1=dqT[:, h, :T], op=ALU.mult),
                             reads=[b_TR[ti]] + CR, writes=[b_qkT])
                        evdrip(2)
                    P.op("tensor", lambda e: e.matmul(STP[:, 0:128], lhsT=qk[p][:T, 1, :], rhs=vd[p][:T, :], start=True, stop=True),
                         reads=[b_qk[p], b_vd[p]], writes=[b_STP])
                    if not pre:
                        drip(8)
                        P.op("tensor", lambda e: e.matmul(SC[:T, :T], lhsT=qkT[:, 1, :T], rhs=qkT[:, 0, :T], start=True, stop=True),
                             reads=[b_qkT], writes=[b_SC])
                        P.op("vector", lambda e: e.scalar_tensor_tensor(out=scm[:T, :T], in0=SC[:T, :T], scalar=ginv[T][:T, h:h + 1], in1=mask[:T, :T],
                                                                        op0=ALU.mult, op1=ALU.mult),
                             reads=[b_SC] + CR, writes=[b_scm])
                        evdrip(2)
                        drip(8)
                        P.op("tensor", lambda e: e.matmul(OP[:T, 256:384], lhsT=scm[:T, :T], rhs=vd[p][:T, :], start=True, stop=False),
                             reads=[b_scm, b_vd[p]], writes=[b_OP], sig=False)
                        P.op("tensor", lambda e: e.matmul(OP[:T, 256:384], lhsT=qkT[:, 0, :T], rhs=Sb[sid][:, h, :], start=False, stop=True),
                             reads=[b_qkT, b_Sb[sid][h]], writes=[b_OP])
                        P.op("scalar", lambda e: e.activation(out=os4[up][:T, t, :], in_=OP[:T, 256:384], func=AF.Copy,
                                                              accum_out=mv4[up][:T, 0, t:t + 1]),
                             reads=[b_OP], writes=[b_os4[up][t], b_mv4[up]])
                        P.op("scalar", lambda e: e.activation(out=junk[:T, 0:128], in_=OP[:T, 256:384], func=AF.Square,
                                                              accum_out=mv4[up][:T, 1, t:t + 1]),
                             reads=[b_OP], writes=[b_junk, b_mv4[up]])
                    g_ = gam[T][:, h:h + 1]
                    P.op("vector", lambda e: e.scalar_tensor_tensor(out=Sf[sid][:, h, :], in0=Sf[sid][:, h, :], scalar=g_, in1=STP[:, 0:128],
                                                                    op0=ALU.mult, op1=ALU.add),
                         reads=[b_STP, b_Sf[sid][h]] + CR, writes=[b_Sf[sid][h]])
                    if not pre or t == nt - 1:
                        P.op("gpsimd", lambda e: e.tensor_copy(out=Sb[sid][:, h, :], in_=Sf[sid][:, h, :]),
                             reads=[b_Sf[sid][h]], writes=[b_Sb[sid][h]])
                    if pre or t != nt - 1:
                        return
                    ust["deferred"].append([2, tail])

                steps.append(dict(prep=prep, evac=ev, evl=evl, B=Bf, flush=False))
            return steps

        def rg_steps(k0, tiles):
            steps = []
            for half in range(2):
                for t, (T, sid) in enumerate(tiles):
                    st_ = {}

                    def prep(half=half, t=t, T=T, st_=st_):
                        i = get_unit(k0 + half)
                        if t == 0:
                            get_unit(k0 + half + 1)
                        st_["pj"] = next_pj()
                        return proj_mm(t, T, i, 512, st_["pj"])

                    def ev(half=half, t=t, T=T, st_=st_):
                        pj = st_["pj"]
                        P.op("scalar", lambda e: e.activation(out=sgr[:T, t, half * 512:(half + 1) * 512], in_=PJ[pj][:T, :], func=AF.Silu),
                             reads=[b_PJ[pj]], writes=[b_sgr[t]])
                    steps.append(dict(prep=prep, evac=ev, B=None, flush=False))
            return steps

        def mix_transposes(t, T, kc0):
            ti = next_tr()
            for g in range(4):
                P.op("tensor", lambda e, ti=ti, g=g, t=t, T=T: e.transpose(out=TR[ti][:, g * 128:g * 128 + T],
                                                                           in_=mtok[:T, t, g * 128:(g + 1) * 128], identity=ident[:T, :T]),
                     reads=[b_mtok[t]] + CR, writes=[b_TR[ti]], sig=(g == 3))
            evac(mixT[:, kc0:kc0 + 4, t * 128:t * 128 + T],
                 TR[ti][:, 0:512].rearrange("p (a b) -> p a b", b=128)[:, :, :T], [b_TR[ti]], [b_mixT[t]])

        def gmlp_steps(k0, tiles, gv_dst):
            steps = []
            for t, (T, sid) in enumerate(tiles):
                st_ = {}

                def prep(t=t, T=T, st_=st_):
                    i = get_unit(k0)
                    if t == 0:
                        get_unit(k0 + 1)
                    st_["pj"] = next_pj()
                    return proj_mm(t, T, i, 512, st_["pj"])

                def ev(t=t, T=T, st_=st_):
                    pj = st_["pj"]
                    P.op("scalar", lambda e: e.activation(out=sgg[:T, t, :], in_=PJ[pj][:T, :], func=AF.Silu),
                         reads=[b_PJ[pj]], writes=[b_sgg[t]])
                steps.append(dict(prep=prep, evac=ev, B=None, flush=False))
            for t, (T, sid) in enumerate(tiles):
                st_ = {}

                def prep(t=t, T=T, st_=st_):
                    i = get_unit(k0 + 1)
                    if t == 0:
                        get_unit(k0 + 2)
                    st_["pj"] = next_pj()
                    return proj_mm(t, T, i, 512, st_["pj"])

                def ev(t=t, T=T, st_=st_):
                    pj = st_["pj"]
                    P.op("vector", lambda e: e.tensor_tensor(out=sgg[:T, t, :], in0=PJ[pj][:T, :], in1=sgg[:T, t, :], op=ALU.mult),
                         reads=[b_PJ[pj], b_sgg[t]], writes=[b_sgg[t]])
                steps.append(dict(prep=prep, evac=ev, B=None, flush=False))
            for t, (T, sid) in enumerate(tiles):
                st_ = {}

                def prep(t=t, T=T, st_=st_):
                    i = get_unit(k0 + 2)
                    if t == 0:
                        get_unit(k0 + 3)
                    st_["p"] = ust["stepno"] % 2; ust["stepno"] += 1
                    st_["pj"] = next_pj()
                    return proj_mm(t, T, i, 512, st_["pj"])

                def ev(t=t, T=T, st_=st_):
                    p, pj = st_["p"], st_["pj"]
                    evac(gvs[p][:T, :], PJ[pj][:T, :], [b_PJ[pj]], [b_gvs[p]], eng="scalar")

                def Bf(t=t, T=T, st_=st_):
                    p = st_["p"]
                    G = gvs[p]
                    P.op("vector", lambda e: e.bn_stats(out=bn6[:T, :], in_=G[:T, :]), reads=[b_gvs[p]], writes=[b_bn6])
                    P.op("vector", lambda e: e.bn_aggr(out=mvg[:T, :], in_=bn6[:T, :]), reads=[b_bn6], writes=[b_mvg])
                    rstd_batch(mvg[:T, 1:2], rs4[:T, 0:1], 1, T, [b_mvg], [b_rs4])
                    P.op("vector", lambda e: e.tensor_scalar(out=G[:T, :], in0=G[:T, :], scalar1=mvg[:T, 0:1], scalar2=rs4[:T, 0:1],
                                                             op0=ALU.subtract, op1=ALU.mult),
                         reads=[b_gvs[p], b_mvg, b_rs4], writes=[b_gvs[p]])
                    P.op("vector", lambda e: e.tensor_tensor(out=G[:T, :], in0=G[:T, :], in1=ggm[:T, :], op=ALU.mult),
                         reads=[b_gvs[p]] + CR, writes=[b_gvs[p]])
                    if gv_dst is not None:
                        dma(gv_dst[t], G[:T, :], [b_gvs[p]], [b_out])
                    evac(gvb[:T, :], G[:T, :], [b_gvs[p]], [b_gvb], eng="scalar")
                    drip(8)
                    for g in range(4):
                        P.op("tensor", lambda e, g=g: e.matmul(OP[:T, g * 128:(g + 1) * 128], lhsT=wsT[:T, g, :T],
                                                               rhs=gvb[:T, g * 128:(g + 1) * 128], start=True, stop=True),
                             reads=[b_gvb] + CR, writes=[b_OP], sig=(g == 3))
                    for g in range(4):
                        P.op("vector", lambda e, g=g: e.scalar_tensor_tensor(
                            out=mtok[:T, t, g * 128:(g + 1) * 128], in0=OP[:T, g * 128:(g + 1) * 128], scalar=bst[:T, g:g + 1],
                            in1=sgg[:T, t, g * 128:(g + 1) * 128], op0=ALU.add, op1=ALU.mult),
                            reads=[b_OP, b_sgg[t]] + CR, writes=[b_mtok[t]])
                    drip(8)
                    mix_transposes(t, T, 8)
                steps.append(dict(prep=prep, evac=ev, B=Bf, flush=False))
            return steps

        def xattn_steps(k0, tiles, ctxs):
            steps = []
            nt = len(tiles)
            ncol = nt * 128
            for t, (T, sid) in enumerate(tiles):
                st_ = {}

                def prep(t=t, T=T, st_=st_):
                    i = get_unit(k0)
                    if t == 0:
                        get_unit(k0 + 1)
                    st_["pj"] = next_pj()
                    return proj_mm(t, T, i, 512, st_["pj"])

                def ev(t=t, T=T, st_=st_):
                    pj = st_["pj"]
                    P.op("scalar", lambda e: e.activation(out=sgg[:T, t, :], in_=PJ[pj][:T, :], func=AF.Silu),
                         reads=[b_PJ[pj]], writes=[b_sgg[t]])
                steps.append(dict(prep=prep, evac=ev, B=None, flush=False))
            for hh in range(4):
                st_ = {}

                def prep(hh=hh, st_=st_):
                    i = get_unit(k0 + 1)
                    if hh == 0:
                        get_unit(k0 + 2)
                    st_["p"] = ust["stepno"] % 2; ust["stepno"] += 1
                    pj = st_["pj"] = next_pj()
                    return [(lambda kc=kc: P.op("tensor", lambda e: e.matmul(
                        PJ[pj][:, :ncol], lhsT=ub[i][:, kc, hh * 128:(hh + 1) * 128], rhs=hT[:, kc, 0:ncol],
                        start=(kc == 0), stop=(kc == 15)),
                        reads=b_hT[:nt] + [b_ub[i]], writes=[b_PJ[pj]], sig=(kc == 15))) for kc in range(16)]

                def ev(hh=hh, st_=st_):
                    p, pj = st_["p"], st_["pj"]
                    P.op("scalar", lambda e: e.activation(out=aqT[p][:, :ncol], in_=PJ[pj][:, :ncol], func=AF.Copy, scale=128.0 ** -0.5),
                         reads=[b_PJ[pj]], writes=[b_aqT[p]])

                def Bf(hh=hh, st_=st_):
                    p = st_["p"]
                    RES = [dict(SCb=SO, bS=b_SO, pe=pexp, bpe=b_pexp, pt=pT, bpt=b_pT, sm=sm3, bsm=b_sm3, tr=0),
                           dict(SCb=STP, bS=b_STP, pe=qkT[:].rearrange("p a b -> p (a b)"), bpe=b_qkT, pt=qk[0], bpt=b_qk[0],
                                sm=sm3b, bsm=b_sm3b, tr=1)]
                    for t0 in range(0, nt, 2):
                        pair = [(t0 + j, tiles[t0 + j][0], ctxs[t0 + j], RES[j]) for j in range(2) if t0 + j < nt]
                        for (t, T, cx, R) in pair:
                            P.op("tensor", lambda e, t=t, T=T, cx=cx, R=R: e.matmul(R["SCb"][:T, 0:256], lhsT=aqT[p][:, t * 128:t * 128 + T],
                                                                                    rhs=mkT[cx][:, hh, :], start=True, stop=True),
                                 reads=[b_aqT[p], b_ctx[cx]], writes=[R["bS"]])
                        for (t, T, cx, R) in pair:
                            P.op("vector", lambda e, T=T, R=R: e.reduce_max(out=R["sm"][:T, 0:1], in_=R["SCb"][:T, 0:256], axis=AX.X),
                                 reads=[R["bS"]], writes=[R["bsm"]])
                            P.op("vector", lambda e, T=T, R=R: e.tensor_scalar(out=R["sm"][:T, 1:2], in0=R["sm"][:T, 0:1], scalar1=-1.0, scalar2=None,
                                                                               op0=ALU.mult),
                                 reads=[R["bsm"]], writes=[R["bsm"]])
                        for (t, T, cx, R) in pair:
                            P.op("scalar", lambda e, T=T, R=R: e.activation(out=R["pe"][:T, :], in_=R["SCb"][:T, 0:256], func=AF.Exp, bias=R["sm"][:T, 1:2],
                                                                            scale=1.0, accum_out=R["sm"][:T, 2:3]),
                                 reads=[R["bS"], R["bsm"]], writes=[R["bpe"], R["bsm"]])
                        drip(2)
                        for (t, T, cx, R) in pair:
                            ti = R["tr"]
                            for c in range(2):
                                P.op("tensor", lambda e, ti=ti, c=c, T=T, R=R: e.transpose(out=TR[ti][:, c * 128:c * 128 + T],
                                                                                           in_=R["pe"][:T, c * 128:(c + 1) * 128], identity=ident[:T, :T]),
                                     reads=[R["bpe"]] + CR, writes=[b_TR[ti]], sig=(c == 1))
                        for (t, T, cx, R) in pair:
                            ti = R["tr"]
                            evac(R["pt"][:, :, :T], TR[ti][:, 0:256].rearrange("p (a b) -> p a b", b=128)[:, :, :T], [b_TR[ti]], [R["bpt"]])
                        drip(2)
                        for (t, T, cx, R) in pair:
                            for c in range(2):
                                P.op("tensor", lambda e, c=c, T=T, cx=cx, R=R: e.matmul(R["SCb"][:T, 256:384], lhsT=R["pt"][:, c, :T],
                                                                                        rhs=mvc[cx][:, c, hh * 128:(hh + 1) * 128],
                                                                                        start=(c == 0), stop=(c == 1)),
                                     reads=[R["bpt"], b_ctx[cx]], writes=[R["bS"]], sig=(c == 1))
                        for (t, T, cx, R) in pair:
                            P.op("vector", lambda e, T=T, R=R: e.reciprocal(out=R["sm"][:T, 3:4], in_=R["sm"][:T, 2:3]), reads=[R["bsm"]], writes=[R["bsm"]])
                            P.op("vector", lambda e, t=t, T=T, R=R: e.scalar_tensor_tensor(
                                out=mtok[:T, t, hh * 128:(hh + 1) * 128], in0=R["SCb"][:T, 256:384], scalar=R["sm"][:T, 3:4],
                                in1=sgg[:T, t, hh * 128:(hh + 1) * 128], op0=ALU.mult, op1=ALU.mult),
                                reads=[R["bS"], R["bsm"], b_sgg[t]], writes=[b_mtok[t]])
                    if hh == 3:
                        for t, (T, sid) in enumerate(tiles):
                            mix_transposes(t, T, 12)
                steps.append(dict(prep=prep, evac=ev, B=Bf, flush=False))
            return steps

        def out_steps(k0, tiles, dsts, nxt):
            steps = []
            nt = len(tiles)
            ef = []
            if nxt is not None:
                for t2, (src2, T2) in enumerate(nxt):
                    ef.append(lambda t2=t2, src2=src2, T2=T2: efront_load(t2, src2, T2))
                    ef.append(lambda t2=t2, T2=T2: efront_tr(t2, T2))
            for n in range(4):
                for t, (T, sid) in enumerate(tiles):
                    st_ = {}

                    def prep(n=n, t=t, T=T, st_=st_):
                        i = get_unit(k0 + n)
                        if t == 0:
                            get_unit(k0 + n + 1)
                        pj = st_["pj"] = next_pj()
                        return [(lambda kc=kc: P.op("tensor", lambda e: e.matmul(
                            PJ[pj][:T, :], lhsT=mixT[:, kc, t * 128:t * 128 + T], rhs=ub[i][:, kc, :],
                            start=(kc == 0), stop=(kc == 15)),
                            reads=[b_mixT[t], b_ub[i]], writes=[b_PJ[pj]], sig=(kc == 15))) for kc in range(16)]

                    def ev(n=n, t=t, T=T, st_=st_):
                        pj = st_["pj"]
                        P.op("vector", lambda e: e.tensor_tensor(out=x4[:T, t, n * 512:(n + 1) * 512], in0=PJ[pj][:T, :],
                                                                 in1=x4[:T, t, n * 512:(n + 1) * 512], op=ALU.add),
                             reads=[b_PJ[pj], b_x4[t]], writes=[b_x4[t]])
                        if ef:
                            ef.pop(0)()
                    steps.append(dict(prep=prep, evac=ev, B=None, flush=(n == 0 and t == 0)))

            def tail():
                while ef:
                    ef.pop(0)()
                if nxt is not None:
                    ust["preloaded"] = {0: load_unit(U_RG0), 1: load_unit(U_RG1)}
                    dma(ropeB[:], ust["next_rope"], [], [b_rope])
                for t, (T, sid) in enumerate(tiles):
                    P.op("scalar", lambda e, t=t, T=T: e.activation(out=junk[:T, :], in_=x4[:T, t, :], func=AF.Square, accum_out=ss4[:T, t:t + 1]),
                         reads=[b_x4[t]], writes=[b_junk, b_ss4])
                T0 = tiles[0][0]
                rstd_batch(ss4[:T0, :nt], rs4[:T0, :nt], nt, T0, [b_ss4], [b_rs4], scale=1.0 / D)
                for t, (T, sid) in enumerate(tiles):
                    P.op("vector", lambda e, t=t, T=T: e.scalar_tensor_tensor(out=x4[:T, t, :], in0=x4[:T, t, :], scalar=rs4[:T, t:t + 1],
                                                                              in1=gfin[:T, :], op0=ALU.mult, op1=ALU.mult),
                         reads=[b_x4[t], b_rs4] + CR, writes=[b_x4[t]])
                    dma(dsts[t], x4[:T, t, :], [b_x4[t]], [b_out])
            steps.append(dict(prep=lambda: [], evac=tail, B=None, flush=True))
            return steps

        def run_deferred(force):
            keep = []
            for item in ust["deferred"]:
                item[0] -= 1
                if force or item[0] <= 0:
                    item[1]()
                else:
                    keep.append(item)
            ust["deferred"] = keep

        def run_stream(steps):
            prevB = None
            n = len(steps)
            for idx, stp in enumerate(steps):
                if stp["flush"]:
                    ust["drip"] = None
                    if prevB is not None:
                        prevB()
                        prevB = None
                    run_deferred(True)
                if "mm" not in stp:
                    stp["mm"] = stp["prep"]()
                while stp["mm"]:
                    stp["mm"].pop(0)()
                if prevB is not None and "evl" in stp:
                    ust["evq"] = stp["evl"]()
                else:
                    ust["evq"] = None
                    stp["evac"]()
                if prevB is not None:
                    fut = []
                    for j in (idx + 1, idx + 2):
                        if j < n and not steps[j]["flush"]:
                            fut.append(steps[j])
                        else:
                            break
                    ust["drip"] = fut
                    prevB()
                    evdrip(100)
                    ust["evq"] = None
                    run_deferred(False)
                    ust["drip"] = None
                if ust["pre"]:
                    w2_step(ust["wrate"])
                prevB = stp["B"]
            if prevB is not None:
                prevB()
            run_deferred(True)

        def run_block(tiles, pre, ctxs, dsts, gv_dst, nxt=None):
            ust["loaded"] = dict(ust.pop("preloaded", {})) if not pre else {}
            ust["pre"] = pre
            steps = []
            if pre:
                ust["seq"] = list(range(8))
                for h in range(8):
                    steps += ret_steps(h, h, tiles, pre)
            else:
                ust["seq"] = [U_RG0, U_RG1] + list(range(8)) + [U_GG, U_GU, U_GV, U_AG, U_AQ] + [NU_IN + n for n in range(4)]
                steps += rg_steps(0, tiles)
                for h in range(8):
                    steps += ret_steps(2 + h, h, tiles, pre)
                steps += gmlp_steps(10, tiles, gv_dst)
                steps += xattn_steps(13, tiles, ctxs)
                steps += out_steps(15, tiles, dsts, nxt)
            run_stream(steps)

        def ctx_from_f32(cx, c, srck, srcv, rk, rv):
            evac(mkb[:, 0, :], srck, rk, [b_mkb], eng="vector")
            evac(mvc[cx][:, c, :], srcv, rv, [b_ctx[cx]], eng="gpsimd")
            ti = next_tr()
            for hh in range(4):
                P.op("tensor", lambda e, ti=ti, hh=hh: e.transpose(out=TR[ti][:, hh * 128:(hh + 1) * 128], in_=mkb[:, 0, hh * 128:(hh + 1) * 128],
                                                                  identity=ident[:, :]),
                     reads=[b_mkb] + CR, writes=[b_TR[ti]], sig=(hh == 3))
            evac(mkT[cx][:, :, c * 128:(c + 1) * 128], TR[ti][:, 0:512].rearrange("p (a b) -> p a b", b=128), [b_TR[ti]], [b_ctx[cx]])

        print('ops@states', P.nops)
        for h in range(8):
            P.op("vector", lambda e, h=h: e.memset(Sf[0][:, h, :], 0.0), writes=[b_Sf[0][h]])
            P.op("gpsimd", lambda e, h=h: e.memset(Sb[0][:, h, :], 0.0), writes=[b_Sb[0][h]])
        for s in range(2):
            dma(Sf[1 + s][:], st_s[s].rearrange("h d e -> d h e"), [], b_Sf[1 + s])
            for h in range(8):
                evac(Sb[1 + s][:, h, :], Sf[1 + s][:, h, :], [b_Sf[1 + s][h]], [b_Sb[1 + s][h]], eng="gpsimd")

        print('ops@pre', P.nops)
        blk = 0
        for pb in range(NB):
            dma(ropeB[:], rope_in[blk], [], [b_rope]); blk += 1
            w2_drain()
            front([(x_pre[pb * 512 + t * 128: pb * 512 + (t + 1) * 128, :], 128) for t in range(4)])
            if pb == 0:
                while wj["ns"] < 12:
                    w2_step(1)
            ust["wrate"] = 1
            run_block([(128, 0)] * 4, True, None, None, None)
        while wj["ns"] < NJ:
            w2_step(1)
            w2_drain()
        evs = []
        for q_ in range(4):
            evs += ([b_wq[q_].w] if b_wq[q_].w else []) + list(b_wq[q_].r)
        for t_ in range(4):
            b_mixT[t_].w = None
            b_mixT[t_].r = list(evs)
        print('ops@phaseM', P.nops)
        front([(mem[0:128, :], 128), (mem[128:256, :], 128)])
        print('ops@M-front-done', P.nops)
        ik = load_unit(NU_ALL)
        iv = load_unit(NU_ALL + 1)
        for c in range(2):
            pk = proj(c, 128, ik, 512)
            evac(mkf[:, 0, :], PJ[pk][:, :], [b_PJ[pk]], [b_mkf], eng="scalar")
            pv = proj(c, 128, iv, 512)
            evac(mkf[:, 1, :], PJ[pv][:, :], [b_PJ[pv]], [b_mkf], eng="vector")
            dma(mk_out[c * 128:(c + 1) * 128, :], mkf[:, 0, :], [b_mkf], [b_out])
            dma(mv_out[c * 128:(c + 1) * 128, :], mkf[:, 1, :], [b_mkf], [b_out])
            ctx_from_f32(0, c, mkf[:, 0, :], mkf[:, 1, :], [b_mkf], [b_mkf])
        print('ops@samplectx', P.nops)
        for s in range(2):
            for c in range(2):
                dma(mkf[:, 0, :], ck_s[s, c * 128:(c + 1) * 128, :], [], [b_mkf])
                dma(mkf[:, 1, :], cv_s[s, c * 128:(c + 1) * 128, :], [], [b_mkf])
                ctx_from_f32(1 + s, c, mkf[:, 0, :], mkf[:, 1, :], [b_mkf], [b_mkf])
        def main_tiles(mb):
            rows = [(mb * 512 + t * 128, mb * 512 + (t + 1) * 128) for t in range(4)]
            return rows, [(x_main[a_:b_, :], 128) for a_, b_ in rows]
        samp_tiles = [(x_s[0], 32), (x_s[1], 32)]
        for mb in range(NB):
            rows, xt_ = main_tiles(mb)
            if mb == 0:
                dma(ropeB[:], rope_in[blk], [], [b_rope])
            blk += 1
            front(xt_, only_x=(mb > 0))
            nxt = main_tiles(mb + 1)[1] if mb + 1 < NB else samp_tiles
            ust["next_rope"] = rope_in[blk]
            run_block([(128, 0)] * 4, False, [0] * 4, [y_main[a_:b_, :] for a_, b_ in rows], None, nxt=nxt)
        dma(sp_out.rearrange("h d e -> d h e"), Sf[0][:], b_Sf[0], [b_out])
        blk += 1
        front(samp_tiles, only_x=True)
        run_block([(32, 1), (32, 2)], False, [1, 2], [y_s[0], y_s[1]], [gv_out[0], gv_out[1]])
        for s_ in range(2):
            dma(ss_out[s_].rearrange("h d e -> d h e"), Sf[1 + s_][:], b_Sf[1 + s_], [b_out])
        P.wait_all("sync", [b_out])
        print("sbuf bytes remaining", nc.sbuf_bytes_remaining)
        print("planned ops", P.nops, {e: len(P.q[e]) for e in P.ENGS}, "sems", P.nsem)
        P.replay()
    return nc


def _consts(NB, half):
    h_idx = np.arange(8, dtype=np.float64)
    log_gamma = np.log(1.0 - 2.0 ** (-5.0 - h_idx))
    i = np.arange(128, dtype=np.float64)[:, None]
    dq = np.exp(log_gamma[None, :] * (i + 1.0))
    dk = np.exp(-log_gamma[None, :] * (i + 1.0)) * 128.0 ** -0.5
    gam128 = np.broadcast_to(np.exp(log_gamma * 128.0)[None, :], (128, 8))
    gam32 = np.broadcast_to(np.exp(log_gamma * 32.0)[None, :], (128, 8))
    ident = np.eye(128)
    jj = np.arange(128)[:, None]
    ii = np.arange(128)[None, :]
    mask = (ii >= jj).astype(np.float64)
    cst = np.concatenate([ident, mask, dq, dk, gam128, gam32, dk * gam128, dk * gam32, 1.0 / gam128, 1.0 / gam32], axis=1).astype(np.float32)
    dqT = np.ascontiguousarray(np.broadcast_to(dq.T[None, :, :], (128, 8, 128))).astype(np.float32)
    inv_freq = 10000.0 ** (-np.arange(0, 128, 2, dtype=np.float32) / 128.0)
    NTOK = NB * 512

    def tab(pos):
        ang = pos.astype(np.float32)[:, None] * inv_freq[None, :]
        c = np.cos(ang).astype(np.float32)
        s = np.sin(ang).astype(np.float32)
        return np.concatenate([c, c, -s, s], axis=1)

    blocks = []
    for pb in range(NB):
        pos = pb * 512 + np.arange(512)
        blocks.append(tab(pos).reshape(4, 128, 256).transpose(1, 0, 2))
    for mb in range(NB):
        pos = half * NTOK + mb * 512 + np.arange(512)
        blocks.append(tab(pos).reshape(4, 128, 256).transpose(1, 0, 2))
    ps = tab(1024 + np.arange(32))
    sblk = np.zeros((128, 4, 256), np.float32)
    sblk[:32, 0] = ps
    sblk[:32, 1] = ps
    blocks.append(sblk)
    rope = np.ascontiguousarray(np.stack(blocks, 0)).astype(np.float32)
    return cst, rope, dqT


def _unit_layout(w):
    return w.reshape(16, 128, w.shape[1]).transpose(1, 0, 2)


_CACHE = {}


def kernel(x_prompt, x_sample, mem_prompt, state_ret, cache_mem_k, cache_mem_v,
           g_norm, w_in, g_ret, g_gmlp, w_s, b_s, g_mem, w_mem_kv, w_out, g_final):
    f = np.float32
    x_prompt = np.asarray(x_prompt, f); x_sample = np.asarray(x_sample, f)
    Bp, L, _ = x_prompt.shape
    NTOK = L // 2
    NB = NTOK // 512
    if NB not in _CACHE:
        _CACHE[NB] = build_program(NB)
    nc = _CACHE[NB]
    w_in0 = np.asarray(w_in, f)[0]
    w_out0 = np.asarray(w_out, f)[0]
    wm0 = np.asarray(w_mem_kv, f)[0]
    units = []
    for h in range(8):
        cols = np.concatenate([np.arange(h * 128, (h + 1) * 128), np.arange(1024 + h * 128, 1024 + (h + 1) * 128),
                               np.arange(2048 + h * 128, 2048 + (h + 1) * 128), np.arange(h * 128, (h + 1) * 128)])
        units.append(_unit_layout(w_in0[:, cols]))
    for c0 in (3072, 3584, 5120, 4096, 4608, 6144, 5632):
        units.append(_unit_layout(w_in0[:, c0:c0 + 512]))
    for n in range(4):
        units.append(_unit_layout(w_out0[:, n * 512:(n + 1) * 512]))
    w_u = np.ascontiguousarray(np.stack(units, 0))
    wm_u = np.ascontiguousarray(np.stack([_unit_layout(wm0[:, 0:512]), _unit_layout(wm0[:, 512:1024])], 0))
    common = {
        "w_u": w_u, "wm_u": wm_u,
        "gnorm_pk": np.ascontiguousarray(np.asarray(g_norm, f)[0].reshape(16, 128).T),
        "gmem_pk": np.ascontiguousarray(np.asarray(g_mem, f)[0].reshape(16, 128).T),
        "gret_b": np.ascontiguousarray(np.broadcast_to(np.asarray(g_ret, f)[0][None, :], (128, 1024))),
        "ggm_b": np.ascontiguousarray(np.broadcast_to(np.asarray(g_gmlp, f)[0][None, :], (128, 512))),
        "gfin_b": np.ascontiguousarray(np.broadcast_to(np.asarray(g_final, f)[None, :], (128, D))),
        "wsT": np.ascontiguousarray(np.asarray(w_s, f)[0].transpose(2, 0, 1)),
        "bs_t": np.ascontiguousarray(np.asarray(b_s, f)[0].T),
    }
    sr = np.asarray(state_ret, f)[0]
    ck = np.asarray(cache_mem_k, f)[0].reshape(16, 256, 512)
    cv = np.asarray(cache_mem_v, f)[0].reshape(16, 256, 512)
    mp = np.asarray(mem_prompt, f)
    in_maps = []
    for c in range(8):
        b, half = c // 2, c % 2
        cst, rope, dqT_c = _consts(NB, half)
        m = dict(common)
        m["x_main"] = np.ascontiguousarray(x_prompt[b, half * NTOK:(half + 1) * NTOK])
        m["x_pre"] = np.ascontiguousarray(x_prompt[b, 0:NTOK]) if half == 1 else np.zeros((NTOK, D), f)
        m["x_s"] = np.ascontiguousarray(x_sample[2 * c:2 * c + 2])
        m["mem"] = np.ascontiguousarray(mp[b])
        m["st_s"] = np.ascontiguousarray(sr[2 * c:2 * c + 2])
        m["ck_s"] = np.ascontiguousarray(ck[2 * c:2 * c + 2])
        m["cv_s"] = np.ascontiguousarray(cv[2 * c:2 * c + 2])
        m["cst"] = cst
        m["dqT"] = dqT_c
        m["rope"] = rope
        in_maps.append(m)
    if KCORES < 8:
        res = run_bass_kernel_spmd(nc, in_maps[:KCORES], core_ids=list(range(KCORES)))
        R = list(res.results) * 8
    else:
        res = run_bass_kernel_spmd(nc, in_maps, core_ids=list(range(8)))
        R = res.results
    y_prompt = np.stack([np.concatenate([R[2 * b]["y_main"], R[2 * b + 1]["y_main"]], 0) for b in range(Bp)], 0)
    y_sample = np.concatenate([R[c]["y_s"] for c in range(8)], 0)
    sp = np.stack([R[2 * b + 1]["sp_out"] for b in range(Bp)], 0)[None]
    mk = np.stack([R[2 * b]["mk_out"].reshape(256, 4, 128) for b in range(Bp)], 0)[None]
    mv = np.stack([R[2 * b]["mv_out"].reshape(256, 4, 128) for b in range(Bp)], 0)[None]
    ss = np.concatenate([R[c]["ss_out"] for c in range(8)], 0)[None]
    gv = np.concatenate([R[c]["gv_out"] for c in range(8)], 0)[None]
    return (y_prompt.astype(f), y_sample.astype(f), sp.astype(f), mk.astype(f), mv.astype(f), ss.astype(f), gv.astype(f))
```

```python
import numpy as np
import concourse.bass as bass
import concourse.mybir as mybir
from concourse.bass_utils import run_bass_kernel_spmd
from contextlib import ExitStack

F32 = mybir.dt.float32
BF16 = mybir.dt.bfloat16
ALU = mybir.AluOpType
AF = mybir.ActivationFunctionType
AX = mybir.AxisListType

import os
KSTOP = int(os.environ.get("KSTOP", "100000000"))
KCORES = int(os.environ.get("KCORES", "8"))
SEM_LIMIT = 30000
D = 2048
EPS = 1e-6
NU_IN = 15
NU_ALL = 19
U_RG0, U_RG1, U_GG, U_GU, U_GV, U_AG, U_AQ = 8, 9, 10, 11, 12, 13, 14


class Sem:
    def __init__(self, h):
        self.h = h
        self.n = 0


class Buf:
    __slots__ = ("name", "w", "r", "dsem", "excl")

    def __init__(self, name, excl=False):
        self.excl = excl
        self.name = name
        self.w = None
        self.r = []
        self.dsem = None


class Prog:
    ENGS = ("sync", "gpsimd", "scalar", "vector", "tensor")

    def __init__(self, nc, stack):
        self.nc = nc
        self.stack = stack
        self.q = {e: [] for e in self.ENGS}
        self.esem = {}
        self.known = {e: {} for e in self.ENGS}
        self.nsem = 0
        for e in self.ENGS:
            self.esem[e] = self.new_sem("e_" + e)

    def new_sem(self, name):
        self.nsem += 1
        h = self.stack.enter_context(self.nc.semaphore(f"{name}_{self.nsem}"))
        return Sem(h)

    def op(self, eng, fn, reads=(), writes=(), dma=False, sig=True):
        self.nops = getattr(self, "nops", 0) + 1
        if self.nops > KSTOP:
            return
        waits = {}
        kn = self.known[eng]
        pe_sem = self.esem["tensor"]

        def need(ev):
            if ev is None:
                return
            s, v = ev
            if eng == "tensor" and s is pe_sem:
                return
            if kn.get(s, 0) >= v:
                return
            if waits.get(s, 0) < v:
                waits[s] = v

        own = self.esem[eng]
        for b in reads:
            need(b.w)
            if b.excl:
                for ev in b.r:
                    if ev[0] is not own:
                        need(ev)
        for b in writes:
            need(b.w)
            for ev in b.r:
                need(ev)
        for s, v in waits.items():
            kn[s] = v
        if dma:
            tgt = writes[0] if writes else reads[0]
            if tgt.dsem is None or tgt.dsem.n + 16 > SEM_LIMIT:
                tgt.dsem = self.new_sem("d")
            dsem = tgt.dsem
            dsem.n += 16
            ev = (dsem, dsem.n)
            inc = (dsem, 16)
        else:
            s = self.esem[eng]
            if sig:
                s.n += 1
                ev = (s, s.n)
                inc = (s, 1)
                if s.n >= SEM_LIMIT:
                    self.esem[eng] = self.new_sem("e_" + eng)
            else:
                ev = (s, s.n + 1)
                inc = None
        for b in reads:
            b.r.append(ev)
        for b in writes:
            b.w = ev
            b.r = []
        self.q[eng].append((list(waits.items()), fn, inc))

    def wait_all(self, eng, bufs):
        waits = {}
        for b in bufs:
            for ev in ([b.w] if b.w else []) + list(b.r):
                s, v = ev
                if waits.get(s, 0) < v:
                    waits[s] = v
        self.q[eng].append((list(waits.items()), None, None))

    def replay(self):
        nc = self.nc
        with nc.Block() as block:
            def mk(ename):
                def body(e):
                    for waits, fn, inc in self.q[ename]:
                        for s, v in waits:
                            e.wait_ge(s.h, v)
                        if fn is None:
                            continue
                        ins = fn(e)
                        if inc is not None:
                            ins.then_inc(inc[0].h, inc[1])
                return body
            block.sync(mk("sync"))
            block.gpsimd(mk("gpsimd"))
            block.scalar(mk("scalar"))
            block.vector(mk("vector"))
            block.tensor(mk("tensor"))


def build_program(NB):
    nc = bass.Bass("TRN2", target_bir_lowering=False)
    NTOK = NB * 512
    NBLK = 2 * NB + 1

    def din(name, shape, dt=F32):
        return nc.dram_tensor(name, shape, dt, kind="ExternalInput").ap()

    def dout(name, shape):
        return nc.dram_tensor(name, shape, F32, kind="ExternalOutput").ap()

    x_main = din("x_main", [NTOK, D])
    x_pre = din("x_pre", [NTOK, D])
    x_s = din("x_s", [2, 32, D])
    mem = din("mem", [256, D])
    st_s = din("st_s", [2, 8, 128, 128])
    ck_s = din("ck_s", [2, 256, 512])
    cv_s = din("cv_s", [2, 256, 512])
    w_u = din("w_u", [NU_ALL, 128, 16, 512])
    wm_u = din("wm_u", [2, 128, 16, 512])
    gnorm_pk = din("gnorm_pk", [128, 16])
    gmem_pk = din("gmem_pk", [128, 16])
    gret_b = din("gret_b", [128, 1024])
    ggm_b = din("ggm_b", [128, 512])
    gfin_b = din("gfin_b", [128, D])
    wsT_in = din("wsT", [128, 4, 128])
    bs_in = din("bs_t", [128, 4])
    rope_in = din("rope", [NBLK, 128, 4, 256])
    cst_in = din("cst", [128, 320])
    dqT_in = din("dqT", [128, 8, 128])

    y_main = dout("y_main", [NTOK, D])
    y_s = dout("y_s", [2, 32, D])
    sp_out = dout("sp_out", [8, 128, 128])
    mk_out = dout("mk_out", [256, 512])
    mv_out = dout("mv_out", [256, 512])
    ss_out = dout("ss_out", [2, 8, 128, 128])
    gv_out = dout("gv_out", [2, 32, 512])

    scr = nc.dram_tensor("scr_w", [NU_ALL + 2, 128, 16, 512], BF16, kind="Internal").ap()

    with ExitStack() as st:
        P = Prog(nc, st)

        def sb(name, shape, dt):
            return st.enter_context(nc.sbuf_tensor(name, shape, dt))

        def ps(name, shape, dt):
            return st.enter_context(nc.psum_tensor(name, shape, dt))

        def B(name):
            return Buf(name)

        hT = sb("hT", [128, 16, 512], BF16); b_hT = [B(f"hT{t}") for t in range(4)]
        mixT = sb("mixT", [128, 16, 512], BF16); b_mixT = [B(f"mixT{t}") for t in range(4)]
        ub = [sb(f"ub{i}", [128, 16, 512], BF16) for i in range(2)]; b_ub = [B(f"ub{i}") for i in range(2)]
        x4 = sb("x4", [128, 4, D], F32); b_x4 = [B(f"x4_{t}") for t in range(4)]
        xb = sb("xb", [128, D], BF16); b_xb = B("xb")
        junk = sb("junk", [128, D], BF16); b_junk = B("junk")
        sgg = sb("sgg", [128, 4, 512], F32); b_sgg = [B(f"sgg{t}") for t in range(4)]
        gvs = [sb(f"gvs{i}", [128, 512], F32) for i in range(2)]; b_gvs = [B(f"gvs{i}") for i in range(2)]
        gvb = sb("gvb", [128, 512], BF16); b_gvb = B("gvb")
        sgr = sb("sgr", [128, 4, 1024], BF16); b_sgr = [B(f"sgr{t}") for t in range(4)]
        mtok = sb("mtok", [128, 4, 512], BF16); b_mtok = [B(f"mtok{t}") for t in range(4)]
        ropeB = sb("ropeB", [128, 4, 256], F32); b_rope = B("rope")
        gret = sb("gret", [128, 1024], F32)
        ggm = sb("ggm", [128, 512], F32)
        gfin = sb("gfin", [128, D], F32)
        gnp = sb("gnp", [128, 16], F32)
        gmp = sb("gmp", [128, 16], F32)
        b_cst = B("cst")
        cst = sb("cstt", [128, 320], F32)
        identb = sb("identb", [128, 128], BF16)
        wsT_f = sb("wsT_f", [128, 4, 128], F32)
        wsT = sb("wsTb", [128, 4, 128], BF16)
        bst = sb("bs_sb", [128, 4], F32)
        mkT = [sb(f"mkT{i}", [128, 4, 256], BF16) for i in range(3)]
        mvc = [sb(f"mvc{i}", [128, 2, 512], BF16) for i in range(3)]
        b_ctx = [B(f"ctx{i}") for i in range(3)]
        Sf = [sb(f"Sf{i}", [128, 8, 128], F32) for i in range(3)]
        Sb = [sb(f"Sb{i}", [128, 8, 128], BF16) for i in range(3)]
        b_Sf = [[B(f"Sf{i}_{h}") for h in range(8)] for i in range(3)]
        b_Sb = [[B(f"Sb{i}_{h}") for h in range(8)] for i in range(3)]
        r1 = sb("r1", [128, 2, 128], F32); b_r1 = B("r1")
        r2 = sb("r2", [128, 2, 128], F32); b_r2 = B("r2")
        qk = [sb(f"qk{i}", [128, 2, 128], BF16) for i in range(2)]; b_qk = [B(f"qk{i}") for i in range(2)]
        vd = [sb(f"vd{i}", [128, 128], BF16) for i in range(2)]; b_vd = [B(f"vd{i}") for i in range(2)]
        dqT = sb("dqT_sb", [128, 8, 128], F32)
        qkT = sb("qkT", [128, 2, 128], BF16); b_qkT = B("qkT")
        scm = sb("scm", [128, 128], BF16); b_scm = B("scm")
        sg4 = [sb(f"sg4_{i}", [128, 4, 128], F32) for i in range(2)]; b_sg4 = [[B(f"sg4_{i}_{t}") for t in range(4)] for i in range(2)]
        os4 = [sb(f"os4_{i}", [128, 4, 128], F32) for i in range(2)]; b_os4 = [[B(f"os4_{i}_{t}") for t in range(4)] for i in range(2)]
        bn6 = sb("bn6", [128, 6], F32); b_bn6 = B("bn6")
        mv4 = [sb(f"mv4_{i}", [128, 6, 4], F32) for i in range(2)]; b_mv4 = [B(f"mv4_{i}") for i in range(2)]
        mvg = sb("mvg", [128, 2], F32); b_mvg = B("mvg")
        rs4 = sb("rs4", [128, 4], F32); b_rs4 = B("rs4")
        sq4 = sb("sq4", [128, 4], F32); b_sq4 = B("sq4")
        ss4 = sb("ss4", [128, 4], F32); b_ss4 = B("ss4")
        eps_t = sb("eps_t", [128, 1], F32)
        aqT = [sb(f"aqT{i}", [128, 512], BF16) for i in range(2)]; b_aqT = [B(f"aqT{i}") for i in range(2)]
        pexp = sb("pexp", [128, 256], BF16); b_pexp = B("pexp")
        pT = sb("pT", [128, 2, 128], BF16); b_pT = B("pT")
        sm3 = sb("sm3", [128, 4], F32); b_sm3 = B("sm3")
        sm3b = sb("sm3b", [128, 4], F32); b_sm3b = B("sm3b")
        mkf = sb("mkf", [128, 2, 512], F32); b_mkf = B("mkf")
        mkb = sb("mkb", [128, 2, 512], BF16); b_mkb = B("mkb")
        PJ = [ps(f"PJ{i}", [128, 512], F32) for i in range(4)]; b_PJ = [Buf(f"PJ{i}", True) for i in range(4)]
        TR = [ps(f"TR{i}", [128, 1024], BF16) for i in range(2)]; b_TR = [Buf(f"TR{i}", True) for i in range(2)]
        SO = ps("SO", [128, 512], F32); b_SO = Buf("SO", True)
        SC = SO; b_SC = b_SO
        OP = SO; b_OP = b_SO
        STP = ps("STP", [128, 512], F32); b_STP = Buf("STP", True)

        ctr = {"pj": 0, "tr": 0, "ub": 0, "ev": 0}
        b_scr = [B(f"scr{u}") for u in range(NU_ALL + 2)]
        b_out = B("out")

        ident = identb
        mask = cst[:, 128:256]
        dq = cst[:, 256:264]
        dk = cst[:, 264:272]
        gam = {128: cst[:, 272:280], 32: cst[:, 280:288]}
        dkg = {128: cst[:, 288:296], 32: cst[:, 296:304]}
        ginv = {128: cst[:, 304:312], 32: cst[:, 312:320]}

        def dma(out, in_, reads, writes):
            P.op("sync", lambda e: e.dma_start(out=out, in_=in_), reads=reads, writes=writes, dma=True)

        def evac(out, in_, reads, writes, eng=None):
            if eng is None:
                ctr["ev"] += 1
                eng = "scalar" if ctr["ev"] % 2 else "vector"
            if eng == "scalar":
                P.op("scalar", lambda e: e.activation(out=out, in_=in_, func=AF.Copy), reads=reads, writes=writes)
            else:
                P.op(eng, lambda e: e.tensor_copy(out=out, in_=in_), reads=reads, writes=writes)

        dma(cst[:], cst_in, [], [b_cst])
        dma(dqT[:], dqT_in, [], [b_cst])
        dma(gret[:], gret_b, [], [b_cst])
        dma(ggm[:], ggm_b, [], [b_cst])
        dma(gfin[:], gfin_b, [], [b_cst])
        dma(gnp[:], gnorm_pk, [], [b_cst])
        dma(gmp[:], gmem_pk, [], [b_cst])
        dma(wsT_f[:], wsT_in, [], [b_cst])
        dma(bst[:], bs_in, [], [b_cst])
        b_c2 = B("c2")
        P.op("vector", lambda e: e.tensor_copy(out=identb[:], in_=cst[:, 0:128]), reads=[b_cst], writes=[b_c2])
        P.op("vector", lambda e: e.memset(eps_t[:], EPS), writes=[b_c2])
        for g in range(4):
            P.op("vector", lambda e, g=g: e.tensor_tensor(out=wsT[:, g, :], in0=wsT_f[:, g, :], in1=mask, op=ALU.mult),
                 reads=[b_cst], writes=[b_c2])
        CR = [b_cst, b_c2]

        cv_eng = ["vector", "scalar", "vector"]
        cvs = {"i": 0}

        def conv_quarter(u, qd, stage, b_stage, dst, b_dst, engs):
            src = w_u[u] if u < NU_ALL else wm_u[u - NU_ALL]
            gsc = gnp if u < NU_IN else (gmp if u >= NU_ALL else None)
            dma(stage, src[:, qd * 4:(qd + 1) * 4, :], [], [b_stage])
            for a in range(4):
                kc = qd * 4 + a
                eng = engs[cvs["i"] % len(engs)]; cvs["i"] += 1
                if gsc is None:
                    evac(dst[:, kc, :], stage[:, a, :], [b_stage], [b_dst], eng=eng)
                elif eng == "scalar":
                    P.op("scalar", lambda e, kc=kc, a=a: e.activation(
                        out=dst[:, kc, :], in_=stage[:, a, :], func=AF.Copy, scale=gsc[:, kc:kc + 1]),
                        reads=[b_stage] + CR, writes=[b_dst])
                else:
                    P.op(eng, lambda e, kc=kc, a=a: e.tensor_scalar(
                        out=dst[:, kc, :], in0=stage[:, a, :], scalar1=gsc[:, kc:kc + 1], scalar2=None, op0=ALU.mult),
                        reads=[b_stage] + CR, writes=[b_dst])

        b_wq = [B(f"wq{i}") for i in range(4)]
        w2_jobs = []
        for u in list(range(8)) + [NU_ALL, NU_ALL + 1] + list(range(8, NU_ALL)):
            for qd in range(4):
                w2_jobs.append((u, qd))

        jobs = list(w2_jobs)
        NJ = len(jobs)
        b_scrq = {}
        slot_sem = [P.new_sem(f"wst{i}") for i in range(4)]
        for j_, (u_, q_) in enumerate(jobs):
            b_scrq[(u_, q_)] = B(f"scr{u_}_{q_}")
            b_scrq[(u_, q_)].dsem = slot_sem[j_ % 4]

        def j_load(j):
            if not (0 <= j < NJ):
                return
            u, qd = jobs[j]
            slot = j % 4
            stage = x4[:, slot, :].rearrange("p (a b) -> p a b", b=512)
            src = w_u[u] if u < NU_ALL else wm_u[u - NU_ALL]
            nc_ = 384 if u < 8 else 512
            dma(stage[:, :, 0:nc_], src[:, qd * 4:(qd + 1) * 4, 0:nc_], [], [b_x4[slot]])

        def j_conv(j):
            if not (0 <= j < NJ):
                return
            u, qd = jobs[j]
            slot = j % 4
            stage = x4[:, slot, :].rearrange("p (a b) -> p a b", b=512)
            dstq = mixT[:, slot * 4:(slot + 1) * 4, :]
            gsc = gnp if u < NU_IN else (gmp if u >= NU_ALL else None)
            nc_ = 384 if u < 8 else 512
            for a_ in range(4):
                eng = cv_eng[cvs["i"] % 3]; cvs["i"] += 1
                if gsc is None:
                    evac(dstq[:, a_, 0:nc_], stage[:, a_, 0:nc_], [b_x4[slot]], [b_wq[slot]], eng=eng)
                elif eng == "scalar":
                    P.op("scalar", lambda e, a_=a_: e.activation(out=dstq[:, a_, 0:nc_], in_=stage[:, a_, 0:nc_], func=AF.Copy,
                                                                 scale=gsc[:, qd * 4 + a_:qd * 4 + a_ + 1]),
                         reads=[b_x4[slot]] + CR, writes=[b_wq[slot]])
                else:
                    P.op(eng, lambda e, a_=a_: e.tensor_scalar(out=dstq[:, a_, 0:nc_], in0=stage[:, a_, 0:nc_],
                                                                scalar1=gsc[:, qd * 4 + a_:qd * 4 + a_ + 1], scalar2=None, op0=ALU.mult),
                         reads=[b_x4[slot]] + CR, writes=[b_wq[slot]])

        def j_store(j):
            if not (0 <= j < NJ):
                return
            u, qd = jobs[j]
            slot = j % 4
            dstq = mixT[:, slot * 4:(slot + 1) * 4, :]
            nc_ = 384 if u < 8 else 512
            dma(scr[u][:, qd * 4:(qd + 1) * 4, 0:nc_], dstq[:, :, 0:nc_], [b_wq[slot]], [b_scrq[(u, qd)]])

        wj = {"nl": 0, "nc": 0, "ns": 0}

        def w2_step(n=1):
            for _ in range(n):
                while wj["nl"] < NJ and wj["nl"] < wj["nc"] + 3:
                    j_load(wj["nl"]); wj["nl"] += 1
                if wj["ns"] < wj["nc"]:
                    j_store(wj["ns"]); wj["ns"] += 1
                if wj["nc"] < wj["nl"]:
                    j_conv(wj["nc"]); wj["nc"] += 1

        def w2_drain():
            while wj["nc"] < wj["nl"]:
                j_conv(wj["nc"]); wj["nc"] += 1
            while wj["ns"] < wj["nc"]:
                j_store(wj["ns"]); wj["ns"] += 1

        def load_unit(u, ncols=512):
            i = ctr["ub"] % 2; ctr["ub"] += 1
            rd = [b_scrq[(u, q_)] for q_ in range(4)]
            assert all(b_.w is not None for b_ in rd), ("unit loaded before converted", u)
            if ncols == 512:
                dma(ub[i][:], scr[u], rd, [b_ub[i]])
            elif ncols == 256:
                dma(ub[i][:, :, 128:384], scr[u][:, :, 128:384], rd, [b_ub[i]])
            else:
                dma(ub[i][:, :, 0:ncols], scr[u][:, :, 0:ncols], rd, [b_ub[i]])
            return i

        def next_pj():
            i = ctr["pj"] % 4; ctr["pj"] += 1
            return i

        def next_tr():
            i = ctr["tr"] % 2; ctr["tr"] += 1
            return i

        def rstd_batch(src, dst, n, T, reads_b, writes_b, scale=None):
            if scale is None:
                P.op("scalar", lambda e: e.activation(out=sq4[:T, :n], in_=src, func=AF.Sqrt, bias=eps_t[:T, 0:1], scale=1.0),
                     reads=reads_b + CR, writes=[b_sq4])
            else:
                P.op("scalar", lambda e: e.activation(out=sq4[:T, :n], in_=src, func=AF.Sqrt, bias=eps_t[:T, 0:1], scale=scale),
                     reads=reads_b + CR, writes=[b_sq4])
            P.op("vector", lambda e: e.reciprocal(out=dst, in_=sq4[:T, :n]), reads=[b_sq4], writes=writes_b)

        def front(tiles, only_x=False):
            nt = len(tiles)
            if only_x:
                for t, (src, T) in enumerate(tiles):
                    dma(x4[:T, t, :], src, [], [b_x4[t]])
                return
            for t, (src, T) in enumerate(tiles):
                dma(x4[:T, t, :], src, [], [b_x4[t]])
                P.op("scalar", lambda e, t=t, T=T: e.activation(out=junk[:T, :], in_=x4[:T, t, :], func=AF.Square,
                                                               accum_out=ss4[:T, t:t + 1]),
                     reads=[b_x4[t]], writes=[b_junk, b_ss4])
            T0 = tiles[0][1]
            rstd_batch(ss4[:T0, :nt], rs4[:T0, :nt], nt, T0, [b_ss4], [b_rs4], scale=1.0 / D)
            for t, (src, T) in enumerate(tiles):
                P.op("vector", lambda e, t=t, T=T: e.tensor_scalar(out=xb[:T, :], in0=x4[:T, t, :], scalar1=rs4[:T, t:t + 1],
                                                                   scalar2=None, op0=ALU.mult),
                     reads=[b_x4[t], b_rs4], writes=[b_xb])
                for grp in range(2):
                    ti = next_tr()
                    for j in range(8):
                        kc = grp * 8 + j
                        P.op("tensor", lambda e, ti=ti, j=j, kc=kc, T=T: e.transpose(
                            out=TR[ti][:, j * 128:j * 128 + T], in_=xb[:T, kc * 128:(kc + 1) * 128], identity=ident[:T, :T]),
                            reads=[b_xb] + CR, writes=[b_TR[ti]], sig=(j == 7))
                    evac(hT[:, grp * 8:(grp + 1) * 8, t * 128:t * 128 + T],
                         TR[ti][:].rearrange("p (a b) -> p a b", b=128)[:, :, :T], [b_TR[ti]], [b_hT[t]])

        ssE = sb("ssE", [128, 4], F32); b_ssE = B("ssE")
        rsE = sb("rsE", [128, 4], F32); b_rsE = B("rsE")
        sgg_flat = sgg[:].rearrange("p a b -> p (a b)")

        def efront_load(t, src, T):
            dma(sgg_flat[:T, :], src, [], b_sgg)
            P.op("scalar", lambda e: e.activation(out=junk[:T, :], in_=sgg_flat[:T, :], func=AF.Square, accum_out=ssE[:T, t:t + 1]),
                 reads=b_sgg, writes=[b_junk, b_ssE])
            P.op("scalar", lambda e: e.activation(out=sq4[:T, 0:1], in_=ssE[:T, t:t + 1], func=AF.Sqrt, bias=eps_t[:T, 0:1], scale=1.0 / D),
                 reads=[b_ssE] + CR, writes=[b_sq4])
            P.op("vector", lambda e: e.reciprocal(out=rsE[:T, t:t + 1], in_=sq4[:T, 0:1]), reads=[b_sq4], writes=[b_rsE])
            P.op("vector", lambda e: e.tensor_scalar(out=xb[:T, :], in0=sgg_flat[:T, :], scalar1=rsE[:T, t:t + 1], scalar2=None, op0=ALU.mult),
                 reads=b_sgg + [b_rsE], writes=[b_xb])

        def efront_tr(t, T):
            for grp in range(2):
                ti = next_tr()
                for j in range(8):
                    kc = grp * 8 + j
                    P.op("tensor", lambda e, ti=ti, j=j, kc=kc: e.transpose(
                        out=TR[ti][:, j * 128:j * 128 + T], in_=xb[:T, kc * 128:(kc + 1) * 128], identity=ident[:T, :T]),
                        reads=[b_xb] + CR, writes=[b_TR[ti]], sig=(j == 7))
                evac(hT[:, grp * 8:(grp + 1) * 8, t * 128:t * 128 + T],
                     TR[ti][:].rearrange("p (a b) -> p a b", b=128)[:, :, :T], [b_TR[ti]], [b_hT[t]])

        def proj(t, T, i, ncols, c0=0):
            pj = next_pj()
            for kc in range(16):
                P.op("tensor", lambda e, kc=kc, pj=pj, t=t, T=T, i=i: e.matmul(
                    PJ[pj][:T, :ncols], lhsT=hT[:, kc, t * 128:t * 128 + T], rhs=ub[i][:, kc, c0:c0 + ncols],
                    start=(kc == 0), stop=(kc == 15)),
                    reads=[b_hT[t], b_ub[i]], writes=[b_PJ[pj]], sig=(kc == 15))
            return pj

        def rope2(pj, ns, T, t, p):
            X = PJ[pj][:T, 0:ns * 128].rearrange("p (s d) -> p s d", d=128)
            A = ropeB[:T, t, 0:128].unsqueeze(1).broadcast_to([T, ns, 128])
            B1 = ropeB[:T, t, 128:192].unsqueeze(1).broadcast_to([T, ns, 64])
            B2 = ropeB[:T, t, 192:256].unsqueeze(1).broadcast_to([T, ns, 64])
            return [
                lambda: P.op("vector", lambda e: e.tensor_tensor(out=r1[:T, 0:ns, :], in0=X, in1=A, op=ALU.mult),
                             reads=[b_PJ[pj], b_rope], writes=[b_r1]),
                lambda: P.op("vector", lambda e: e.tensor_tensor(out=r2[:T, 0:ns, 0:64], in0=X[:, :, 64:128], in1=B1, op=ALU.mult),
                             reads=[b_PJ[pj], b_rope], writes=[b_r2]),
                lambda: P.op("vector", lambda e: e.tensor_tensor(out=r2[:T, 0:ns, 64:128], in0=X[:, :, 0:64], in1=B2, op=ALU.mult),
                             reads=[b_PJ[pj], b_rope], writes=[b_r2]),
                lambda: P.op("vector", lambda e: e.tensor_tensor(out=qk[p][:T, 2 - ns:2, :], in0=r1[:T, 0:ns, :], in1=r2[:T, 0:ns, :], op=ALU.add),
                             reads=[b_r1, b_r2], writes=[b_qk[p]]),
            ]

        ust = {"loaded": {}, "seq": [], "stepno": 0, "drip": None, "pre": False, "deferred": [], "wrate": 1}

        def get_unit(k):
            if k >= len(ust["seq"]):
                return None
            if k not in ust["loaded"]:
                u = ust["seq"][k]
                ust["loaded"][k] = load_unit(u, (256 if ust["pre"] else 384) if u < 8 else 512)
            return ust["loaded"][k]

        def drip(n):
            for d in (ust["drip"] or []):
                if "mm" not in d:
                    d["mm"] = d["prep"]()
                while n > 0 and d["mm"]:
                    d["mm"].pop(0)()
                    n -= 1
                if n == 0:
                    return

        def evdrip(n):
            q_ = ust.get("evq")
            while q_ and n > 0:
                q_.pop(0)()
                n -= 1

        def proj_mm(t, T, i, ncols, pj, c0=0):
            return [(lambda kc=kc: P.op("tensor", lambda e: e.matmul(
                PJ[pj][:T, :ncols], lhsT=hT[:, kc, t * 128:t * 128 + T], rhs=ub[i][:, kc, c0:c0 + ncols],
                start=(kc == 0), stop=(kc == 15)),
                reads=[b_hT[t], b_ub[i]], writes=[b_PJ[pj]], sig=(kc == 15))) for kc in range(16)]

        def ret_steps(k, h, tiles, pre):
            steps = []
            nt = len(tiles)
            up = h % 2
            for t, (T, sid) in enumerate(tiles):
                st_ = {}

                def prep(t=t, T=T, st_=st_):
                    i = get_unit(k)
                    if t == 0:
                        get_unit(k + 1)
                    st_["p"] = ust["stepno"] % 2; ust["stepno"] += 1
                    st_["pj"] = next_pj()
                    return proj_mm(t, T, i, 256 if pre else 384, st_["pj"], c0=(128 if pre else 0))

                def evl(t=t, T=T, st_=st_):
                    p, pj = st_["p"], st_["pj"]
                    vc = 128 if pre else 256
                    return [lambda: P.op("scalar", lambda e: e.activation(out=vd[p][:T, :], in_=PJ[pj][:T, vc:vc + 128], func=AF.Copy,
                                                                          scale=dkg[T][:T, h:h + 1]),
                                         reads=[b_PJ[pj]] + CR, writes=[b_vd[p]])] + rope2(pj, 1 if pre else 2, T, t, p)

                def ev(evl=evl):
                    for f_ in evl():
                        f_()

                def tail():
                    T0 = tiles[0][0]
                    M = mv4[up]
                    P.op("vector", lambda e: e.tensor_scalar(out=M[:T0, 2, :nt], in0=M[:T0, 0, :nt], scalar1=1.0 / 128, scalar2=None, op0=ALU.mult),
                         reads=[b_mv4[up]], writes=[b_mv4[up]])
                    P.op("vector", lambda e: e.tensor_tensor(out=M[:T0, 3, :nt], in0=M[:T0, 2, :nt], in1=M[:T0, 2, :nt], op=ALU.mult),
                         reads=[b_mv4[up]], writes=[b_mv4[up]])
                    P.op("vector", lambda e: e.scalar_tensor_tensor(out=M[:T0, 3, :nt], in0=M[:T0, 1, :nt], scalar=1.0 / 128, in1=M[:T0, 3, :nt],
                                                                    op0=ALU.mult, op1=ALU.subtract),
                         reads=[b_mv4[up]], writes=[b_mv4[up]])
                    rstd_batch(M[:T0, 3, :nt], M[:T0, 4, :nt], nt, T0, [b_mv4[up]], [b_mv4[up]])
                    P.op("vector", lambda e: e.scalar_tensor_tensor(out=M[:T0, 5, :nt], in0=M[:T0, 2, :nt], scalar=-1.0, in1=M[:T0, 4, :nt],
                                                                    op0=ALU.mult, op1=ALU.mult),
                         reads=[b_mv4[up]], writes=[b_mv4[up]])
                    for t2, (T2, sid2) in enumerate(tiles):
                        P.op("scalar", lambda e, t2=t2, T2=T2: e.activation(out=os4[up][:T2, t2, :], in_=os4[up][:T2, t2, :], func=AF.Identity,
                                                                            bias=M[:T2, 5, t2:t2 + 1], scale=M[:T2, 4, t2:t2 + 1]),
                             reads=[b_os4[up][t2], b_mv4[up]], writes=[b_os4[up][t2]])
                        P.op("gpsimd", lambda e, t2=t2, T2=T2: e.tensor_tensor(out=sg4[up][:T2, t2, :], in0=sgr[:T2, t2, h * 128:(h + 1) * 128],
                                                                               in1=gret[:T2, h * 128:(h + 1) * 128], op=ALU.mult),
                             reads=[b_sgr[t2]] + CR, writes=[b_sg4[up][t2]])
                        P.op("vector", lambda e, t2=t2, T2=T2: e.tensor_tensor(out=mtok[:T2, t2, 0:128], in0=os4[up][:T2, t2, :], in1=sg4[up][:T2, t2, :],
                                                                               op=ALU.mult),
                             reads=[b_os4[up][t2], b_sg4[up][t2]], writes=[b_mtok[t2]])
                    for t2, (T2, sid2) in enumerate(tiles):
                        ti = 1
                        P.op("tensor", lambda e, ti=ti, t2=t2, T2=T2: e.transpose(out=TR[ti][:, t2 * 128:t2 * 128 + T2], in_=mtok[:T2, t2, 0:128], identity=ident[:T2, :T2]),
                             reads=[b_mtok[t2]] + CR, writes=[b_TR[ti]])
                        evac(mixT[:, h, t2 * 128:t2 * 128 + T2], TR[ti][:, t2 * 128:t2 * 128 + T2], [b_TR[ti]], [b_mixT[t2]])

                def Bf(t=t, T=T, sid=sid, st_=st_):
                    p = st_["p"]
                    if not pre:
                        ti = 0
                        for s_ in range(2):
                            P.op("tensor", lambda e, s_=s_: e.transpose(out=TR[ti][:, s_ * 128:s_ * 128 + T], in_=qk[p][:T, s_, :],
                                                                         identity=ident[:T, :T]),
                                 reads=[b_qk[p]] + CR, writes=[b_TR[ti]], sig=(s_ == 1))
                        evac(qkT[:, 1, :T], TR[ti][:, 128:128 + T], [b_TR[ti]], [b_qkT], eng="scalar")
                        P.op("vector", lambda e: e.tensor_tensor(out=qkT[:, 0, :T], in0=TR[ti][:, 0:T], in1=dqT[:, h, :T], op=ALU.mult),
                             reads=[b_TR[ti]] + CR, writes=[b_qkT])
                        evdrip(2)
                    P.op("tensor", lambda e: e.matmul(STP[:, 0:128], lhsT=qk[p][:T, 1, :], rhs=vd[p][:T, :], start=True, stop=True),
                         reads=[b_qk[p], b_vd[p]], writes=[b_STP])
                    if not pre:
                        drip(8)
                        P.op("tensor", lambda e: e.matmul(SC[:T, :T], lhsT=qkT[:, 1, :T], rhs=qkT[:, 0, :T], start=True, stop=True),
                             reads=[b_qkT], writes=[b_SC])
                        P.op("vector", lambda e: e.scalar_tensor_tensor(out=scm[:T, :T], in0=SC[:T, :T], scalar=ginv[T][:T, h:h + 1], in1=mask[:T, :T],
                                                                        op0=ALU.mult, op1=ALU.mult),
                             reads=[b_SC] + CR, writes=[b_scm])
                        evdrip(2)
                        drip(8)
                        P.op("tensor", lambda e: e.matmul(OP[:T, 256:384], lhsT=scm[:T, :T], rhs=vd[p][:T, :], start=True, stop=False),
                             reads=[b_scm, b_vd[p]], writes=[b_OP], sig=False)
                        P.op("tensor", lambda e: e.matmul(OP[:T, 256:384], lhsT=qkT[:, 0, :T], rhs=Sb[sid][:, h, :], start=False, stop=True),
                             reads=[b_qkT, b_Sb[sid][h]], writes=[b_OP])
                        P.op("scalar", lambda e: e.activation(out=os4[up][:T, t, :], in_=OP[:T, 256:384], func=AF.Copy,
                                                              accum_out=mv4[up][:T, 0, t:t + 1]),
                             reads=[b_OP], writes=[b_os4[up][t], b_mv4[up]])
                        P.op("scalar", lambda e: e.activation(out=junk[:T, 0:128], in_=OP[:T, 256:384], func=AF.Square,
                                                              accum_out=mv4[up][:T, 1, t:t + 1]),
                             reads=[b_OP], writes=[b_junk, b_mv4[up]])
                    g_ = gam[T][:, h:h + 1]
                    P.op("vector", lambda e: e.scalar_tensor_tensor(out=Sf[sid][:, h, :], in0=Sf[sid][:, h, :], scalar=g_, in1=STP[:, 0:128],
                                                                    op0=ALU.mult, op1=ALU.add),
                         reads=[b_STP, b_Sf[sid][h]] + CR, writes=[b_Sf[sid][h]])
                    if not pre or t == nt - 1:
                        P.op("gpsimd", lambda e: e.tensor_copy(out=Sb[sid][:, h, :], in_=Sf[sid][:, h, :]),
                             reads=[b_Sf[sid][h]], writes=[b_Sb[sid][h]])
                    if pre or t != nt - 1:
                        return
                    ust["deferred"].append([2, tail])

                steps.append(dict(prep=prep, evac=ev, evl=evl, B=Bf, flush=False))
            return steps

        def rg_steps(k0, tiles):
            steps = []
            for half in range(2):
                for t, (T, sid) in enumerate(tiles):
                    st_ = {}

                    def prep(half=half, t=t, T=T, st_=st_):
                        i = get_unit(k0 + half)
                        if t == 0:
                            get_unit(k0 + half + 1)
                        st_["pj"] = next_pj()
                        return proj_mm(t, T, i, 512, st_["pj"])

                    def ev(half=half, t=t, T=T, st_=st_):
                        pj = st_["pj"]
                        P.op("scalar", lambda e: e.activation(out=sgr[:T, t, half * 512:(half + 1) * 512], in_=PJ[pj][:T, :], func=AF.Silu),
                             reads=[b_PJ[pj]], writes=[b_sgr[t]])
                    steps.append(dict(prep=prep, evac=ev, B=None, flush=False))
            return steps

        def mix_transposes(t, T, kc0):
            ti = next_tr()
            for g in range(4):
                P.op("tensor", lambda e, ti=ti, g=g, t=t, T=T: e.transpose(out=TR[ti][:, g * 128:g * 128 + T],
                                                                           in_=mtok[:T, t, g * 128:(g + 1) * 128], identity=ident[:T, :T]),
                     reads=[b_mtok[t]] + CR, writes=[b_TR[ti]], sig=(g == 3))
            evac(mixT[:, kc0:kc0 + 4, t * 128:t * 128 + T],
                 TR[ti][:, 0:512].rearrange("p (a b) -> p a b", b=128)[:, :, :T], [b_TR[ti]], [b_mixT[t]])

        def gmlp_steps(k0, tiles, gv_dst):
            steps = []
            for t, (T, sid) in enumerate(tiles):
                st_ = {}

                def prep(t=t, T=T, st_=st_):
                    i = get_unit(k0)
                    if t == 0:
                        get_unit(k0 + 1)
                    st_["pj"] = next_pj()
                    return proj_mm(t, T, i, 512, st_["pj"])

                def ev(t=t, T=T, st_=st_):
                    pj = st_["pj"]
                    P.op("scalar", lambda e: e.activation(out=sgg[:T, t, :], in_=PJ[pj][:T, :], func=AF.Silu),
                         reads=[b_PJ[pj]], writes=[b_sgg[t]])
                steps.append(dict(prep=prep, evac=ev, B=None, flush=False))
            for t, (T, sid) in enumerate(tiles):
                st_ = {}

                def prep(t=t, T=T, st_=st_):
                    i = get_unit(k0 + 1)
                    if t == 0:
                        get_unit(k0 + 2)
                    st_["pj"] = next_pj()
                    return proj_mm(t, T, i, 512, st_["pj"])

                def ev(t=t, T=T, st_=st_):
                    pj = st_["pj"]
                    P.op("vector", lambda e: e.tensor_tensor(out=sgg[:T, t, :], in0=PJ[pj][:T, :], in1=sgg[:T, t, :], op=ALU.mult),
                         reads=[b_PJ[pj], b_sgg[t]], writes=[b_sgg[t]])
                steps.append(dict(prep=prep, evac=ev, B=None, flush=False))
            for t, (T, sid) in enumerate(tiles):
                st_ = {}

                def prep(t=t, T=T, st_=st_):
                    i = get_unit(k0 + 2)
                    if t == 0:
                        get_unit(k0 + 3)
                    st_["p"] = ust["stepno"] % 2; ust["stepno"] += 1
                    st_["pj"] = next_pj()
                    return proj_mm(t, T, i, 512, st_["pj"])

                def ev(t=t, T=T, st_=st_):
                    p, pj = st_["p"], st_["pj"]
                    evac(gvs[p][:T, :], PJ[pj][:T, :], [b_PJ[pj]], [b_gvs[p]], eng="scalar")

                def Bf(t=t, T=T, st_=st_):
                    p = st_["p"]
                    G = gvs[p]
                    P.op("vector", lambda e: e.bn_stats(out=bn6[:T, :], in_=G[:T, :]), reads=[b_gvs[p]], writes=[b_bn6])
                    P.op("vector", lambda e: e.bn_aggr(out=mvg[:T, :], in_=bn6[:T, :]), reads=[b_bn6], writes=[b_mvg])
                    rstd_batch(mvg[:T, 1:2], rs4[:T, 0:1], 1, T, [b_mvg], [b_rs4])
                    P.op("vector", lambda e: e.tensor_scalar(out=G[:T, :], in0=G[:T, :], scalar1=mvg[:T, 0:1], scalar2=rs4[:T, 0:1],
                                                             op0=ALU.subtract, op1=ALU.mult),
                         reads=[b_gvs[p], b_mvg, b_rs4], writes=[b_gvs[p]])
                    P.op("vector", lambda e: e.tensor_tensor(out=G[:T, :], in0=G[:T, :], in1=ggm[:T, :], op=ALU.mult),
                         reads=[b_gvs[p]] + CR, writes=[b_gvs[p]])
                    if gv_dst is not None:
                        dma(gv_dst[t], G[:T, :], [b_gvs[p]], [b_out])
                    evac(gvb[:T, :], G[:T, :], [b_gvs[p]], [b_gvb], eng="scalar")
                    drip(8)
                    for g in range(4):
                        P.op("tensor", lambda e, g=g: e.matmul(OP[:T, g * 128:(g + 1) * 128], lhsT=wsT[:T, g, :T],
                                                               rhs=gvb[:T, g * 128:(g + 1) * 128], start=True, stop=True),
                             reads=[b_gvb] + CR, writes=[b_OP], sig=(g == 3))
                    for g in range(4):
                        P.op("vector", lambda e, g=g: e.scalar_tensor_tensor(
                            out=mtok[:T, t, g * 128:(g + 1) * 128], in0=OP[:T, g * 128:(g + 1) * 128], scalar=bst[:T, g:g + 1],
                            in1=sgg[:T, t, g * 128:(g + 1) * 128], op0=ALU.add, op1=ALU.mult),
                            reads=[b_OP, b_sgg[t]] + CR, writes=[b_mtok[t]])
                    drip(8)
                    mix_transposes(t, T, 8)
                steps.append(dict(prep=prep, evac=ev, B=Bf, flush=False))
            return steps

        def xattn_steps(k0, tiles, ctxs):
            steps = []
            nt = len(tiles)
            ncol = nt * 128
            for t, (T, sid) in enumerate(tiles):
                st_ = {}

                def prep(t=t, T=T, st_=st_):
                    i = get_unit(k0)
                    if t == 0:
                        get_unit(k0 + 1)
                    st_["pj"] = next_pj()
                    return proj_mm(t, T, i, 512, st_["pj"])

                def ev(t=t, T=T, st_=st_):
                    pj = st_["pj"]
                    P.op("scalar", lambda e: e.activation(out=sgg[:T, t, :], in_=PJ[pj][:T, :], func=AF.Silu),
                         reads=[b_PJ[pj]], writes=[b_sgg[t]])
                steps.append(dict(prep=prep, evac=ev, B=None, flush=False))
            for hh in range(4):
                st_ = {}

                def prep(hh=hh, st_=st_):
                    i = get_unit(k0 + 1)
                    if hh == 0:
                        get_unit(k0 + 2)
                    st_["p"] = ust["stepno"] % 2; ust["stepno"] += 1
                    pj = st_["pj"] = next_pj()
                    return [(lambda kc=kc: P.op("tensor", lambda e: e.matmul(
                        PJ[pj][:, :ncol], lhsT=ub[i][:, kc, hh * 128:(hh + 1) * 128], rhs=hT[:, kc, 0:ncol],
                        start=(kc == 0), stop=(kc == 15)),
                        reads=b_hT[:nt] + [b_ub[i]], writes=[b_PJ[pj]], sig=(kc == 15))) for kc in range(16)]

                def ev(hh=hh, st_=st_):
                    p, pj = st_["p"], st_["pj"]
                    P.op("scalar", lambda e: e.activation(out=aqT[p][:, :ncol], in_=PJ[pj][:, :ncol], func=AF.Copy, scale=128.0 ** -0.5),
                         reads=[b_PJ[pj]], writes=[b_aqT[p]])

                def Bf(hh=hh, st_=st_):
                    p = st_["p"]
                    RES = [dict(SCb=SO, bS=b_SO, pe=pexp, bpe=b_pexp, pt=pT, bpt=b_pT, sm=sm3, bsm=b_sm3, tr=0),
                           dict(SCb=STP, bS=b_STP, pe=qkT[:].rearrange("p a b -> p (a b)"), bpe=b_qkT, pt=qk[0], bpt=b_qk[0],
                                sm=sm3b, bsm=b_sm3b, tr=1)]
                    for t0 in range(0, nt, 2):
                        pair = [(t0 + j, tiles[t0 + j][0], ctxs[t0 + j], RES[j]) for j in range(2) if t0 + j < nt]
                        for (t, T, cx, R) in pair:
                            P.op("tensor", lambda e, t=t, T=T, cx=cx, R=R: e.matmul(R["SCb"][:T, 0:256], lhsT=aqT[p][:, t * 128:t * 128 + T],
                                                                                    rhs=mkT[cx][:, hh, :], start=True, stop=True),
                                 reads=[b_aqT[p], b_ctx[cx]], writes=[R["bS"]])
                        for (t, T, cx, R) in pair:
                            P.op("vector", lambda e, T=T, R=R: e.reduce_max(out=R["sm"][:T, 0:1], in_=R["SCb"][:T, 0:256], axis=AX.X),
                                 reads=[R["bS"]], writes=[R["bsm"]])
                            P.op("vector", lambda e, T=T, R=R: e.tensor_scalar(out=R["sm"][:T, 1:2], in0=R["sm"][:T, 0:1], scalar1=-1.0, scalar2=None,
                                                                               op0=ALU.mult),
                                 reads=[R["bsm"]], writes=[R["bsm"]])
                        for (t, T, cx, R) in pair:
                            P.op("scalar", lambda e, T=T, R=R: e.activation(out=R["pe"][:T, :], in_=R["SCb"][:T, 0:256], func=AF.Exp, bias=R["sm"][:T, 1:2],
                                                                            scale=1.0, accum_out=R["sm"][:T, 2:3]),
                                 reads=[R["bS"], R["bsm"]], writes=[R["bpe"], R["bsm"]])
                        drip(2)
                        for (t, T, cx, R) in pair:
                            ti = R["tr"]
                            for c in range(2):
                                P.op("tensor", lambda e, ti=ti, c=c, T=T, R=R: e.transpose(out=TR[ti][:, c * 128:c * 128 + T],
                                                                                           in_=R["pe"][:T, c * 128:(c + 1) * 128], identity=ident[:T, :T]),
                                     reads=[R["bpe"]] + CR, writes=[b_TR[ti]], sig=(c == 1))
                        for (t, T, cx, R) in pair:
                            ti = R["tr"]
                            evac(R["pt"][:, :, :T], TR[ti][:, 0:256].rearrange("p (a b) -> p a b", b=128)[:, :, :T], [b_TR[ti]], [R["bpt"]])
                        drip(2)
                        for (t, T, cx, R) in pair:
                            for c in range(2):
                                P.op("tensor", lambda e, c=c, T=T, cx=cx, R=R: e.matmul(R["SCb"][:T, 256:384], lhsT=R["pt"][:, c, :T],
                                                                                        rhs=mvc[cx][:, c, hh * 128:(hh + 1) * 128],
                                                                                        start=(c == 0), stop=(c == 1)),
                                     reads=[R["bpt"], b_ctx[cx]], writes=[R["bS"]], sig=(c == 1))
                        for (t, T, cx, R) in pair:
                            P.op("vector", lambda e, T=T, R=R: e.reciprocal(out=R["sm"][:T, 3:4], in_=R["sm"][:T, 2:3]), reads=[R["bsm"]], writes=[R["bsm"]])
                            P.op("vector", lambda e, t=t, T=T, R=R: e.scalar_tensor_tensor(
                                out=mtok[:T, t, hh * 128:(hh + 1) * 128], in0=R["SCb"][:T, 256:384], scalar=R["sm"][:T, 3:4],
                                in1=sgg[:T, t, hh * 128:(hh + 1) * 128], op0=ALU.mult, op1=ALU.mult),
                                reads=[R["bS"], R["bsm"], b_sgg[t]], writes=[b_mtok[t]])
                    if hh == 3:
                        for t, (T, sid) in enumerate(tiles):
                            mix_transposes(t, T, 12)
                steps.append(dict(prep=prep, evac=ev, B=Bf, flush=False))
            return steps

        def out_steps(k0, tiles, dsts, nxt):
            steps = []
            nt = len(tiles)
            ef = []
            if nxt is not None:
                for t2, (src2, T2) in enumerate(nxt):
                    ef.append(lambda t2=t2, src2=src2, T2=T2: efront_load(t2, src2, T2))
                    ef.append(lambda t2=t2, T2=T2: efront_tr(t2, T2))
            for n in range(4):
                for t, (T, sid) in enumerate(tiles):
                    st_ = {}

                    def prep(n=n, t=t, T=T, st_=st_):
                        i = get_unit(k0 + n)
                        if t == 0:
                            get_unit(k0 + n + 1)
                        pj = st_["pj"] = next_pj()
                        return [(lambda kc=kc: P.op("tensor", lambda e: e.matmul(
                            PJ[pj][:T, :], lhsT=mixT[:, kc, t * 128:t * 128 + T], rhs=ub[i][:, kc, :],
                            start=(kc == 0), stop=(kc == 15)),
                            reads=[b_mixT[t], b_ub[i]], writes=[b_PJ[pj]], sig=(kc == 15))) for kc in range(16)]

                    def ev(n=n, t=t, T=T, st_=st_):
                        pj = st_["pj"]
                        P.op("vector", lambda e: e.tensor_tensor(out=x4[:T, t, n * 512:(n + 1) * 512], in0=PJ[pj][:T, :],
                                                                 in1=x4[:T, t, n * 512:(n + 1) * 512], op=ALU.add),
                             reads=[b_PJ[pj], b_x4[t]], writes=[b_x4[t]])
                        if ef:
                            ef.pop(0)()
                    steps.append(dict(prep=prep, evac=ev, B=None, flush=(n == 0 and t == 0)))

            def tail():
                while ef:
                    ef.pop(0)()
                if nxt is not None:
                    ust["preloaded"] = {0: load_unit(U_RG0), 1: load_unit(U_RG1)}
                    dma(ropeB[:], ust["next_rope"], [], [b_rope])
                for t, (T, sid) in enumerate(tiles):
                    P.op("scalar", lambda e, t=t, T=T: e.activation(out=junk[:T, :], in_=x4[:T, t, :], func=AF.Square, accum_out=ss4[:T, t:t + 1]),
                         reads=[b_x4[t]], writes=[b_junk, b_ss4])
                T0 = tiles[0][0]
                rstd_batch(ss4[:T0, :nt], rs4[:T0, :nt], nt, T0, [b_ss4], [b_rs4], scale=1.0 / D)
                for t, (T, sid) in enumerate(tiles):
                    P.op("vector", lambda e, t=t, T=T: e.scalar_tensor_tensor(out=x4[:T, t, :], in0=x4[:T, t, :], scalar=rs4[:T, t:t + 1],
                                                                              in1=gfin[:T, :], op0=ALU.mult, op1=ALU.mult),
                         reads=[b_x4[t], b_rs4] + CR, writes=[b_x4[t]])
                    dma(dsts[t], x4[:T, t, :], [b_x4[t]], [b_out])
            steps.append(dict(prep=lambda: [], evac=tail, B=None, flush=True))
            return steps

        def run_deferred(force):
            keep = []
            for item in ust["deferred"]:
                item[0] -= 1
                if force or item[0] <= 0:
                    item[1]()
                else:
                    keep.append(item)
            ust["deferred"] = keep

        def run_stream(steps):
            prevB = None
            n = len(steps)
            for idx, stp in enumerate(steps):
                if stp["flush"]:
                    ust["drip"] = None
                    if prevB is not None:
                        prevB()
                        prevB = None
                    run_deferred(True)
                if "mm" not in stp:
                    stp["mm"] = stp["prep"]()
                while stp["mm"]:
                    stp["mm"].pop(0)()
                if prevB is not None and "evl" in stp:
                    ust["evq"] = stp["evl"]()
                else:
                    ust["evq"] = None
                    stp["evac"]()
                if prevB is not None:
                    fut = []
                    for j in (idx + 1, idx + 2):
                        if j < n and not steps[j]["flush"]:
                            fut.append(steps[j])
                        else:
                            break
                    ust["drip"] = fut
                    prevB()
                    evdrip(100)
                    ust["evq"] = None
                    run_deferred(False)
                    ust["drip"] = None
                if ust["pre"]:
                    w2_step(ust["wrate"])
                prevB = stp["B"]
            if prevB is not None:
                prevB()
            run_deferred(True)

        def run_block(tiles, pre, ctxs, dsts, gv_dst, nxt=None):
            ust["loaded"] = dict(ust.pop("preloaded", {})) if not pre else {}
            ust["pre"] = pre
            steps = []
            if pre:
                ust["seq"] = list(range(8))
                for h in range(8):
                    steps += ret_steps(h, h, tiles, pre)
            else:
                ust["seq"] = [U_RG0, U_RG1] + list(range(8)) + [U_GG, U_GU, U_GV, U_AG, U_AQ] + [NU_IN + n for n in range(4)]
                steps += rg_steps(0, tiles)
                for h in range(8):
                    steps += ret_steps(2 + h, h, tiles, pre)
                steps += gmlp_steps(10, tiles, gv_dst)
                steps += xattn_steps(13, tiles, ctxs)
                steps += out_steps(15, tiles, dsts, nxt)
            run_stream(steps)

        def ctx_from_f32(cx, c, srck, srcv, rk, rv):
            evac(mkb[:, 0, :], srck, rk, [b_mkb], eng="vector")
            evac(mvc[cx][:, c, :], srcv, rv, [b_ctx[cx]], eng="gpsimd")
            ti = next_tr()
            for hh in range(4):
                P.op("tensor", lambda e, ti=ti, hh=hh: e.transpose(out=TR[ti][:, hh * 128:(hh + 1) * 128], in_=mkb[:, 0, hh * 128:(hh + 1) * 128],
                                                                  identity=ident[:, :]),
                     reads=[b_mkb] + CR, writes=[b_TR[ti]], sig=(hh == 3))
            evac(mkT[cx][:, :, c * 128:(c + 1) * 128], TR[ti][:, 0:512].rearrange("p (a b) -> p a b", b=128), [b_TR[ti]], [b_ctx[cx]])

        print('ops@states', P.nops)
        for h in range(8):
            P.op("vector", lambda e, h=h: e.memset(Sf[0][:, h, :], 0.0), writes=[b_Sf[0][h]])
            P.op("gpsimd", lambda e, h=h: e.memset(Sb[0][:, h, :], 0.0), writes=[b_Sb[0][h]])
        for s in range(2):
            dma(Sf[1 + s][:], st_s[s].rearrange("h d e -> d h e"), [], b_Sf[1 + s])
            for h in range(8):
                evac(Sb[1 + s][:, h, :], Sf[1 + s][:, h, :], [b_Sf[1 + s][h]], [b_Sb[1 + s][h]], eng="gpsimd")

        print('ops@pre', P.nops)
        blk = 0
        for pb in range(NB):
            dma(ropeB[:], rope_in[blk], [], [b_rope]); blk += 1
            w2_drain()
            front([(x_pre[pb * 512 + t * 128: pb * 512 + (t + 1) * 128, :], 128) for t in range(4)])
            if pb == 0:
                while wj["ns"] < 12:
                    w2_step(1)
            ust["wrate"] = 1
            run_block([(128, 0)] * 4, True, None, None, None)
        while wj["ns"] < NJ:
            w2_step(1)
            w2_drain()
        evs = []
        for q_ in range(4):
            evs += ([b_wq[q_].w] if b_wq[q_].w else []) + list(b_wq[q_].r)
        for t_ in range(4):
            b_mixT[t_].w = None
            b_mixT[t_].r = list(evs)
        print('ops@phaseM', P.nops)
        front([(mem[0:128, :], 128), (mem[128:256, :], 128)])
        print('ops@M-front-done', P.nops)
        ik = load_unit(NU_ALL)
        iv = load_unit(NU_ALL + 1)
        for c in range(2):
            pk = proj(c, 128, ik, 512)
            evac(mkf[:, 0, :], PJ[pk][:, :], [b_PJ[pk]], [b_mkf], eng="scalar")
            pv = proj(c, 128, iv, 512)
            evac(mkf[:, 1, :], PJ[pv][:, :], [b_PJ[pv]], [b_mkf], eng="vector")
            dma(mk_out[c * 128:(c + 1) * 128, :], mkf[:, 0, :], [b_mkf], [b_out])
            dma(mv_out[c * 128:(c + 1) * 128, :], mkf[:, 1, :], [b_mkf], [b_out])
            ctx_from_f32(0, c, mkf[:, 0, :], mkf[:, 1, :], [b_mkf], [b_mkf])
        print('ops@samplectx', P.nops)
        for s in range(2):
            for c in range(2):
                dma(mkf[:, 0, :], ck_s[s, c * 128:(c + 1) * 128, :], [], [b_mkf])
                dma(mkf[:, 1, :], cv_s[s, c * 128:(c + 1) * 128, :], [], [b_mkf])
                ctx_from_f32(1 + s, c, mkf[:, 0, :], mkf[:, 1, :], [b_mkf], [b_mkf])
        def main_tiles(mb):
            rows = [(mb * 512 + t * 128, mb * 512 + (t + 1) * 128) for t in range(4)]
            return rows, [(x_main[a_:b_, :], 128) for a_, b_ in rows]
        samp_tiles = [(x_s[0], 32), (x_s[1], 32)]
        for mb in range(NB):
            rows, xt_ = main_tiles(mb)
            if mb == 0:
                dma(ropeB[:], rope_in[blk], [], [b_rope])
            blk += 1
            front(xt_, only_x=(mb > 0))
            nxt = main_tiles(mb + 1)[1] if mb + 1 < NB else samp_tiles
            ust["next_rope"] = rope_in[blk]
            run_block([(128, 0)] * 4, False, [0] * 4, [y_main[a_:b_, :] for a_, b_ in rows], None, nxt=nxt)
        dma(sp_out.rearrange("h d e -> d h e"), Sf[0][:], b_Sf[0], [b_out])
        blk += 1
        front(samp_tiles, only_x=True)
        run_block([(32, 1), (32, 2)], False, [1, 2], [y_s[0], y_s[1]], [gv_out[0], gv_out[1]])
        for s_ in range(2):
            dma(ss_out[s_].rearrange("h d e -> d h e"), Sf[1 + s_][:], b_Sf[1 + s_], [b_out])
        P.wait_all("sync", [b_out])
        print("sbuf bytes remaining", nc.sbuf_bytes_remaining)
        print("planned ops", P.nops, {e: len(P.q[e]) for e in P.ENGS}, "sems", P.nsem)
        P.replay()
    return nc


def _consts(NB, half):
    h_idx = np.arange(8, dtype=np.float64)
    log_gamma = np.log(1.0 - 2.0 ** (-5.0 - h_idx))
    i = np.arange(128, dtype=np.float64)[:, None]
    dq = np.exp(log_gamma[None, :] * (i + 1.0))
    dk = np.exp(-log_gamma[None, :] * (i + 1.0)) * 128.0 ** -0.5
    gam128 = np.broadcast_to(np.exp(log_gamma * 128.0)[None, :], (128, 8))
    gam32 = np.broadcast_to(np.exp(log_gamma * 32.0)[None, :], (128, 8))
    ident = np.eye(128)
    jj = np.arange(128)[:, None]
    ii = np.arange(128)[None, :]
    mask = (ii >= jj).astype(np.float64)
    cst = np.concatenate([ident, mask, dq, dk, gam128, gam32, dk * gam128, dk * gam32, 1.0 / gam128, 1.0 / gam32], axis=1).astype(np.float32)
    dqT = np.ascontiguousarray(np.broadcast_to(dq.T[None, :, :], (128, 8, 128))).astype(np.float32)
    inv_freq = 10000.0 ** (-np.arange(0, 128, 2, dtype=np.float32) / 128.0)
    NTOK = NB * 512

    def tab(pos):
        ang = pos.astype(np.float32)[:, None] * inv_freq[None, :]
        c = np.cos(ang).astype(np.float32)
        s = np.sin(ang).astype(np.float32)
        return np.concatenate([c, c, -s, s], axis=1)

    blocks = []
    for pb in range(NB):
        pos = pb * 512 + np.arange(512)
        blocks.append(tab(pos).reshape(4, 128, 256).transpose(1, 0, 2))
    for mb in range(NB):
        pos = half * NTOK + mb * 512 + np.arange(512)
        blocks.append(tab(pos).reshape(4, 128, 256).transpose(1, 0, 2))
    ps = tab(1024 + np.arange(32))
    sblk = np.zeros((128, 4, 256), np.float32)
    sblk[:32, 0] = ps
    sblk[:32, 1] = ps
    blocks.append(sblk)
    rope = np.ascontiguousarray(np.stack(blocks, 0)).astype(np.float32)
    return cst, rope, dqT


def _unit_layout(w):
    return w.reshape(16, 128, w.shape[1]).transpose(1, 0, 2)


_CACHE = {}


def kernel(x_prompt, x_sample, mem_prompt, state_ret, cache_mem_k, cache_mem_v,
           g_norm, w_in, g_ret, g_gmlp, w_s, b_s, g_mem, w_mem_kv, w_out, g_final):
    f = np.float32
    x_prompt = np.asarray(x_prompt, f); x_sample = np.asarray(x_sample, f)
    Bp, L, _ = x_prompt.shape
    NTOK = L // 2
    NB = NTOK // 512
    if NB not in _CACHE:
        _CACHE[NB] = build_program(NB)
    nc = _CACHE[NB]
    w_in0 = np.asarray(w_in, f)[0]
    w_out0 = np.asarray(w_out, f)[0]
    wm0 = np.asarray(w_mem_kv, f)[0]
    units = []
    for h in range(8):
        cols = np.concatenate([np.arange(h * 128, (h + 1) * 128), np.arange(1024 + h * 128, 1024 + (h + 1) * 128),
                               np.arange(2048 + h * 128, 2048 + (h + 1) * 128), np.arange(h * 128, (h + 1) * 128)])
        units.append(_unit_layout(w_in0[:, cols]))
    for c0 in (3072, 3584, 5120, 4096, 4608, 6144, 5632):
        units.append(_unit_layout(w_in0[:, c0:c0 + 512]))
    for n in range(4):
        units.append(_unit_layout(w_out0[:, n * 512:(n + 1) * 512]))
    w_u = np.ascontiguousarray(np.stack(units, 0))
    wm_u = np.ascontiguousarray(np.stack([_unit_layout(wm0[:, 0:512]), _unit_layout(wm0[:, 512:1024])], 0))
    common = {
        "w_u": w_u, "wm_u": wm_u,
        "gnorm_pk": np.ascontiguousarray(np.asarray(g_norm, f)[0].reshape(16, 128).T),
        "gmem_pk": np.ascontiguousarray(np.asarray(g_mem, f)[0].reshape(16, 128).T),
        "gret_b": np.ascontiguousarray(np.broadcast_to(np.asarray(g_ret, f)[0][None, :], (128, 1024))),
        "ggm_b": np.ascontiguousarray(np.broadcast_to(np.asarray(g_gmlp, f)[0][None, :], (128, 512))),
        "gfin_b": np.ascontiguousarray(np.broadcast_to(np.asarray(g_final, f)[None, :], (128, D))),
        "wsT": np.ascontiguousarray(np.asarray(w_s, f)[0].transpose(2, 0, 1)),
        "bs_t": np.ascontiguousarray(np.asarray(b_s, f)[0].T),
    }
    sr = np.asarray(state_ret, f)[0]
    ck = np.asarray(cache_mem_k, f)[0].reshape(16, 256, 512)
    cv = np.asarray(cache_mem_v, f)[0].reshape(16, 256, 512)
    mp = np.asarray(mem_prompt, f)
    in_maps = []
    for c in range(8):
        b, half = c // 2, c % 2
        cst, rope, dqT_c = _consts(NB, half)
        m = dict(common)
        m["x_main"] = np.ascontiguousarray(x_prompt[b, half * NTOK:(half + 1) * NTOK])
        m["x_pre"] = np.ascontiguousarray(x_prompt[b, 0:NTOK]) if half == 1 else np.zeros((NTOK, D), f)
        m["x_s"] = np.ascontiguousarray(x_sample[2 * c:2 * c + 2])
        m["mem"] = np.ascontiguousarray(mp[b])
        m["st_s"] = np.ascontiguousarray(sr[2 * c:2 * c + 2])
        m["ck_s"] = np.ascontiguousarray(ck[2 * c:2 * c + 2])
        m["cv_s"] = np.ascontiguousarray(cv[2 * c:2 * c + 2])
        m["cst"] = cst
        m["dqT"] = dqT_c
        m["rope"] = rope
        in_maps.append(m)
    if KCORES < 8:
        res = run_bass_kernel_spmd(nc, in_maps[:KCORES], core_ids=list(range(KCORES)))
        R = list(res.results) * 8
    else:
        res = run_bass_kernel_spmd(nc, in_maps, core_ids=list(range(8)))
        R = res.results
    y_prompt = np.stack([np.concatenate([R[2 * b]["y_main"], R[2 * b + 1]["y_main"]], 0) for b in range(Bp)], 0)
    y_sample = np.concatenate([R[c]["y_s"] for c in range(8)], 0)
    sp = np.stack([R[2 * b + 1]["sp_out"] for b in range(Bp)], 0)[None]
    mk = np.stack([R[2 * b]["mk_out"].reshape(256, 4, 128) for b in range(Bp)], 0)[None]
    mv = np.stack([R[2 * b]["mv_out"].reshape(256, 4, 128) for b in range(Bp)], 0)[None]
    ss = np.concatenate([R[c]["ss_out"] for c in range(8)], 0)[None]
    gv = np.concatenate([R[c]["gv_out"] for c in range(8)], 0)[None]
    return (y_prompt.astype(f), y_sample.astype(f), sp.astype(f), mk.astype(f), mv.astype(f), ss.astype(f), gv.astype(f))
```
